# Optimizing a Trainium2 kernel written in Bass

```python
import jax, jax.numpy as jnp
from jax import lax
import numpy as np

D_MODEL = 4096
BATCH = 2
SEQ = 4096
DEPTH = 1
DEC_BATCH = 32
DEC_SEQ = 1
PAST_LEN = 8192
PAGE_SIZE = 128

HEAD_DIM = 128
N_HEADS = D_MODEL // HEAD_DIM
H_NSA = N_HEADS // 2
H_DSA = N_HEADS - H_NSA
KV_NSA = 2
KV_DSA = 4
GRP_NSA = H_NSA // KV_NSA
GRP_DSA = H_DSA // KV_DSA
CMP_LEN = 32
CMP_STRIDE = 16
CMP_HID = 256
SEL_LEN = 64
SEL_TOP = 16
WINDOW = 512
IDX_HEADS = 16
IDX_DIM = 64
DSA_TOPK_MAX = 256
D_FF = 4 * D_MODEL
PLE_DIM = 256
ROPE_THETA = 10000.0
NORM_EPS = 1e-6
Q_BLOCK = 128
NEG_INF = -1e30
FORCE_BONUS = 1e4
SPLITS = (H_NSA * HEAD_DIM, KV_NSA * 2 * HEAD_DIM, KV_NSA * 2 * HEAD_DIM, KV_NSA * 2 * HEAD_DIM, 3 * H_NSA,
          H_DSA * HEAD_DIM, KV_DSA * 2 * HEAD_DIM, IDX_HEADS * IDX_DIM, IDX_DIM, IDX_HEADS, 2 * D_MODEL)
PROJ_W = sum(SPLITS)

kernel_name = 'nsa_dsa_gated_hybrid_step'


def rmsnorm(x, g):
    xf = x.astype(jnp.float32)
    y = xf * lax.rsqrt(jnp.mean(xf * xf, axis=-1, keepdims=True) + NORM_EPS)
    return (y * g.astype(jnp.float32)).astype(x.dtype)


def rope(x, pos):
    half = x.shape[-1] // 2
    inv = ROPE_THETA ** (-jnp.arange(half, dtype=jnp.float32) / half)
    ang = pos.astype(jnp.float32)[:, None] * inv[None, :]
    ang = ang.reshape(ang.shape[:1] + (1,) * (x.ndim - 3) + (half,))
    cos, sin = jnp.cos(ang), jnp.sin(ang)
    xf = x.astype(jnp.float32)
    x1, x2 = xf[..., :half], xf[..., half:]
    return jnp.concatenate([x1 * cos - x2 * sin, x1 * sin + x2 * cos], axis=-1).astype(x.dtype)


def rope_kv(kv, pos):
    return jnp.stack([rope(kv[..., 0, :], pos), kv[..., 1, :]], axis=-2)


def masked_softmax(s, mask):
    p = jax.nn.softmax(jnp.where(mask, s, NEG_INF), axis=-1)
    return jnp.where(mask, p, 0.0)


def gather_rows(src, tok):
    return jax.vmap(lambda s, t: s[t])(src, tok)


def gather_rows_heads(src, tok):
    per_head = jax.vmap(lambda s, t: s[t], in_axes=(1, 1), out_axes=1)
    return jax.vmap(per_head)(src, tok)


def _paged_rows_one(pool, pt, new, tok):
    past_len = pt.shape[0] * PAGE_SIZE
    tp = jnp.minimum(tok, past_len - 1)
    rows_past = pool[pt[tp // PAGE_SIZE], tp % PAGE_SIZE]
    rows_new = new[jnp.clip(tok - past_len, 0, new.shape[0] - 1)].astype(rows_past.dtype)
    is_past = (tok < past_len).reshape(tok.shape + (1,) * (rows_past.ndim - tok.ndim))
    return jnp.where(is_past, rows_past, rows_new)


def paged_rows(pool, page_table, new, tok):
    return jax.vmap(_paged_rows_one, in_axes=(None, 0, 0, 0))(pool, page_table, new, tok)


def paged_rows_heads(pool, page_table, new, tok):
    per_head = jax.vmap(_paged_rows_one, in_axes=(2, None, 1, 1), out_axes=1)
    return jax.vmap(per_head, in_axes=(None, 0, 0, 0))(pool, page_table, new, tok)


def logical_rows(pool, page_table, new):
    past = pool[page_table]
    past = past.reshape((past.shape[0], past.shape[1] * past.shape[2]) + past.shape[3:])
    return jnp.concatenate([past, new.astype(past.dtype)], axis=1)


def project(h, w_in, pos):
    B, S, _ = h.shape
    z = jnp.einsum('bsd,dp->bsp', h, w_in)
    offs = np.cumsum(SPLITS)[:-1].tolist()
    q_n, kv_c, kv_s, kv_w, g_n, q_d, kv_d, qi, ki, wi, g_m = jnp.split(z, offs, axis=-1)
    q_n = rope(q_n.reshape(B, S, KV_NSA, GRP_NSA, HEAD_DIM), pos)
    kv_c = rope_kv(kv_c.reshape(B, S, KV_NSA, 2, HEAD_DIM), pos)
    kv_s = rope_kv(kv_s.reshape(B, S, KV_NSA, 2, HEAD_DIM), pos)
    kv_w = rope_kv(kv_w.reshape(B, S, KV_NSA, 2, HEAD_DIM), pos)
    g_n = jax.nn.sigmoid(g_n.astype(jnp.float32)).astype(h.dtype).reshape(B, S, KV_NSA, GRP_NSA, 3)
    q_d = rope(q_d.reshape(B, S, KV_DSA, GRP_DSA, HEAD_DIM), pos)
    kv_d = rope_kv(kv_d.reshape(B, S, KV_DSA, 2, HEAD_DIM), pos)
    qi = rope(qi.reshape(B, S, IDX_HEADS, IDX_DIM), pos)
    ki = rope(ki, pos)
    g_m = jax.nn.sigmoid(g_m.astype(jnp.float32)).astype(h.dtype).reshape(B, S, 2, D_MODEL)
    return q_n, kv_c, kv_s, kv_w, g_n, q_d, kv_d, qi, ki, wi, g_m


def compress(kv, pos_emb, w1, w2):
    L = kv.shape[1]
    nc = (L - CMP_LEN) // CMP_STRIDE + 1
    idx = jnp.arange(nc)[:, None] * CMP_STRIDE + jnp.arange(CMP_LEN)[None, :]
    blocks = kv[:, idx] + jnp.transpose(pos_emb, (1, 0, 2))[:, None].astype(kv.dtype)
    hid = jax.nn.relu(jnp.einsum('bnlgcd,cldh->bngch', blocks, w1))
    return jnp.einsum('bngch,chd->bngcd', hid, w2)


def nsa_core(q, qpos, kvc, gather_sel, kvw, wpos, g, n_sel, L):
    B, Q = q.shape[0], q.shape[1]
    scale = HEAD_DIM ** -0.5
    kc, vc = kvc[..., 0, :], kvc[..., 1, :]
    blk = jnp.arange(kvc.shape[1])
    s_c = jnp.einsum('bqgrd,bngd->bqgrn', q, kc).astype(jnp.float32) * scale
    vis_c = (blk * CMP_STRIDE + CMP_LEN - 1)[None, :] <= qpos[:, None]
    p_c = masked_softmax(s_c, vis_c[None, :, None, None, :])
    o_c = jnp.einsum('bqgrn,bngd->bqgrd', p_c.astype(vc.dtype), vc)
    sb = jnp.arange(n_sel)
    overlap = ((blk[:, None] * CMP_STRIDE < (sb[None, :] + 1) * SEL_LEN)
               & (blk[:, None] * CMP_STRIDE + CMP_LEN > sb[None, :] * SEL_LEN)).astype(jnp.float32)
    imp = jnp.einsum('bqgrn,nj->bqgj', p_c, overlap)
    cur = qpos // SEL_LEN
    vis_s = sb[None, :] <= cur[:, None]
    forced = (sb[None, :] == 0) | (sb[None, :] == cur[:, None]) | (sb[None, :] == cur[:, None] - 1)
    score = jnp.where(vis_s[None, :, None, :], imp + jnp.where(forced, FORCE_BONUS, 0.0)[None, :, None, :], NEG_INF)
    n_top = min(SEL_TOP, n_sel)
    _, top_blk = lax.top_k(score, n_top)
    tok = (top_blk[..., None] * SEL_LEN + jnp.arange(SEL_LEN)).reshape(top_blk.shape[:3] + (n_top * SEL_LEN,))
    valid = tok <= qpos[None, :, None, None]
    kvs = gather_sel(jnp.minimum(tok, L - 1))
    ks, vs = kvs[..., 0, :], kvs[..., 1, :]
    s_s = jnp.einsum('bqgrd,bqgtd->bqgrt', q, ks).astype(jnp.float32) * scale
    p_s = masked_softmax(s_s, valid[:, :, :, None, :])
    o_s = jnp.einsum('bqgrt,bqgtd->bqgrd', p_s.astype(vs.dtype), vs)
    kw, vw = kvw[..., 0, :], kvw[..., 1, :]
    s_w = jnp.einsum('bqgrd,bkgd->bqgrk', q, kw).astype(jnp.float32) * scale
    dpos = qpos[:, None] - wpos[None, :]
    vis_w = (dpos >= 0) & (dpos < WINDOW) & (wpos[None, :] >= 0)
    p_w = masked_softmax(s_w, vis_w[None, :, None, None, :])
    o_w = jnp.einsum('bqgrk,bkgd->bqgrd', p_w.astype(vw.dtype), vw)
    o = g[..., 0:1] * o_c + g[..., 1:2] * o_s + g[..., 2:3] * o_w
    return o.reshape(B, Q, H_NSA * HEAD_DIM)


def dsa_core(q, qpos, qi, wi, ki, gather_kv):
    B, Q = q.shape[0], q.shape[1]
    L = ki.shape[1]
    topk = min(DSA_TOPK_MAX, L // 4)
    sc = jnp.einsum('bqhd,bsd->bqhs', qi, ki).astype(jnp.float32) * (IDX_DIM ** -0.5)
    w = wi.astype(jnp.float32) * (IDX_HEADS ** -0.5)
    idx_score = jnp.einsum('bqhs,bqh->bqs', jax.nn.relu(sc), w)
    vis = jnp.arange(L)[None, :] <= qpos[:, None]
    idx_score = jnp.where(vis[None], idx_score, NEG_INF)
    _, sel = lax.top_k(idx_score, topk)
    valid = sel <= qpos[None, :, None]
    kv = gather_kv(sel)
    k, v = kv[..., 0, :], kv[..., 1, :]
    s = jnp.einsum('bqgrd,bqkgd->bqgrk', q, k).astype(jnp.float32) * (HEAD_DIM ** -0.5)
    p = masked_softmax(s, valid[:, :, None, None, :])
    o = jnp.einsum('bqgrk,bqkgd->bqgrd', p.astype(v.dtype), v)
    return o.reshape(B, Q, H_DSA * HEAD_DIM)


def prompt_mixer(h, w_in_l, cmp_pos_l, cmp_w1_l, cmp_w2_l):
    B, S, _ = h.shape
    pos = jnp.arange(S)
    q_n, kv_c, kv_s, kv_w, g_n, q_d, kv_d, qi, ki, wi, g_m = project(h, w_in_l, pos)
    kvc = compress(kv_c, cmp_pos_l, cmp_w1_l, cmp_w2_l)
    kv_w_pad = jnp.pad(kv_w, ((0, 0), (WINDOW, 0), (0, 0), (0, 0), (0, 0)))
    n_sel = -(-S // SEL_LEN)

    def block(bi):
        start = bi * Q_BLOCK
        qpos = start + jnp.arange(Q_BLOCK)

        def sl(a):
            return lax.dynamic_slice_in_dim(a, start, Q_BLOCK, axis=1)

        kvw = lax.dynamic_slice_in_dim(kv_w_pad, start, WINDOW + Q_BLOCK, axis=1)
        wpos = start - WINDOW + jnp.arange(WINDOW + Q_BLOCK)
        o_n = nsa_core(sl(q_n), qpos, kvc, lambda tok: gather_rows_heads(kv_s, tok), kvw, wpos, sl(g_n), n_sel, S)
        o_d = dsa_core(sl(q_d), qpos, sl(qi), sl(wi), ki, lambda tok: gather_rows(kv_d, tok))
        return o_n, o_d

    o_n, o_d = lax.map(block, jnp.arange(S // Q_BLOCK))
    o_n = jnp.swapaxes(o_n, 0, 1).reshape(B, S, H_NSA * HEAD_DIM)
    o_d = jnp.swapaxes(o_d, 0, 1).reshape(B, S, H_DSA * HEAD_DIM)
    win = kv_w[:, S - min(WINDOW, S):]
    return o_n, o_d, g_m, kv_c, kv_s, kv_d, ki, win


def sample_mixer(h, page_table, c_cmp, c_sel, c_dkv, c_didx, s_win, w_in_l, cmp_pos_l, cmp_w1_l, cmp_w2_l):
    B, S, _ = h.shape
    past_len = page_table.shape[1] * PAGE_SIZE
    L = past_len + S
    pos = past_len + jnp.arange(S)
    q_n, kv_c, kv_s, kv_w, g_n, q_d, kv_d, qi, ki, wi, g_m = project(h, w_in_l, pos)
    kvc = compress(logical_rows(c_cmp, page_table, kv_c), cmp_pos_l, cmp_w1_l, cmp_w2_l)
    wb = s_win.shape[1]
    kvw = jnp.concatenate([s_win, kv_w.astype(s_win.dtype)], axis=1)
    wpos = past_len - wb + jnp.arange(wb + S)
    n_sel = -(-L // SEL_LEN)
    o_n = nsa_core(q_n, pos, kvc, lambda tok: paged_rows_heads(c_sel, page_table, kv_s, tok), kvw, wpos, g_n, n_sel, L)
    ki_all = logical_rows(c_didx, page_table, ki)
    o_d = dsa_core(q_d, pos, qi, wi, ki_all, lambda tok: paged_rows(c_dkv, page_table, kv_d, tok))
    win = kvw[:, S:]
    return o_n, o_d, g_m, kv_c, kv_s, kv_d, ki, win


def merge(o_n, o_d, g_m, w_bn, w_bd, w_out):
    mix = g_m[:, :, 0] * (o_n @ w_bn) + g_m[:, :, 1] * (o_d @ w_bd)
    return mix @ w_out


def layer_tail(x, mix_out, p, norm_mlp, w_up, w_down, norm_ple, w_ple_gate, w_ple):
    x = x + mix_out
    u = rmsnorm(x, norm_mlp) @ w_up
    x = x + jnp.square(jax.nn.relu(u)) @ w_down
    gate = jax.nn.sigmoid((rmsnorm(x, norm_ple) @ w_ple_gate).astype(jnp.float32)).astype(x.dtype)
    return x + gate * (p @ w_ple)


def setup_inputs(seed: int = 0) -> dict:
    key = jax.random.key(seed)
    ks = jax.random.split(key, 26)
    f32 = jnp.float32
    n_pages = PAST_LEN // PAGE_SIZE
    n_used = DEC_BATCH * n_pages
    n_pool = n_used + -(-n_used // 4)
    win_buf = min(WINDOW, PAST_LEN)

    def nrm(k, shape, scale=1.0):
        return jax.random.normal(k, shape, f32) * scale

    def gain(k, shape):
        return 1.0 + 0.02 * jax.random.normal(k, shape, f32)

    page_table = jax.random.permutation(ks[7], n_pool)[:n_used].reshape(DEC_BATCH, n_pages).astype(jnp.int32)
    return {
        'x_prompt': nrm(ks[0], (BATCH, SEQ, D_MODEL)),
        'x_sample': nrm(ks[1], (DEC_BATCH, DEC_SEQ, D_MODEL)),
        'cache_nsa_cmp': nrm(ks[2], (DEPTH, n_pool, PAGE_SIZE, KV_NSA, 2, HEAD_DIM)),
        'cache_nsa_sel': nrm(ks[3], (DEPTH, n_pool, PAGE_SIZE, KV_NSA, 2, HEAD_DIM)),
        'cache_dsa_kv': nrm(ks[4], (DEPTH, n_pool, PAGE_SIZE, KV_DSA, 2, HEAD_DIM)),
        'cache_dsa_idx': nrm(ks[5], (DEPTH, n_pool, PAGE_SIZE, IDX_DIM)),
        'state_nsa_win': nrm(ks[6], (DEPTH, DEC_BATCH, win_buf, KV_NSA, 2, HEAD_DIM)),
        'page_table': page_table,
        'p_prompt': nrm(ks[8], (DEPTH, BATCH, SEQ, PLE_DIM)),
        'p_sample': nrm(ks[9], (DEPTH, DEC_BATCH, DEC_SEQ, PLE_DIM)),
        'norm_mix': gain(ks[10], (DEPTH, D_MODEL)),
        'w_in': nrm(ks[11], (DEPTH, D_MODEL, PROJ_W), D_MODEL ** -0.5),
        'cmp_pos': nrm(ks[12], (DEPTH, 2, CMP_LEN, HEAD_DIM), 0.1),
        'cmp_w1': nrm(ks[13], (DEPTH, 2, CMP_LEN, HEAD_DIM, CMP_HID), (CMP_LEN * HEAD_DIM) ** -0.5),
        'cmp_w2': nrm(ks[14], (DEPTH, 2, CMP_HID, HEAD_DIM), CMP_HID ** -0.5),
        'w_branch_nsa': nrm(ks[15], (DEPTH, H_NSA * HEAD_DIM, D_MODEL), (H_NSA * HEAD_DIM) ** -0.5),
        'w_branch_dsa': nrm(ks[16], (DEPTH, H_DSA * HEAD_DIM, D_MODEL), (H_DSA * HEAD_DIM) ** -0.5),
        'w_out': nrm(ks[17], (DEPTH, D_MODEL, D_MODEL), D_MODEL ** -0.5),
        'norm_mlp': gain(ks[18], (DEPTH, D_MODEL)),
        'w_up': nrm(ks[19], (DEPTH, D_MODEL, D_FF), D_MODEL ** -0.5),
        'w_down': nrm(ks[20], (DEPTH, D_FF, D_MODEL), D_FF ** -0.5),
        'norm_ple': gain(ks[21], (DEPTH, D_MODEL)),
        'w_ple_gate': nrm(ks[22], (DEPTH, D_MODEL, D_MODEL), D_MODEL ** -0.5),
        'w_ple': nrm(ks[23], (DEPTH, PLE_DIM, D_MODEL), PLE_DIM ** -0.5),
        'norm_final': gain(ks[24], (D_MODEL,)),
    }


def reference(x_prompt, x_sample, cache_nsa_cmp, cache_nsa_sel, cache_dsa_kv, cache_dsa_idx, state_nsa_win,
              page_table, p_prompt, p_sample, norm_mix, w_in, cmp_pos, cmp_w1, cmp_w2, w_branch_nsa, w_branch_dsa,
              w_out, norm_mlp, w_up, w_down, norm_ple, w_ple_gate, w_ple, norm_final):
    x_p, x_s = x_prompt, x_sample
    cmp_p, cmp_s, sel_p, sel_s, dkv_p, dkv_s, didx_p, didx_s, win_p, win_s = ([] for _ in range(10))
    for i in range(DEPTH):
        o_n, o_d, g_m, kc, kss, kd, ki, wn = prompt_mixer(rmsnorm(x_p, norm_mix[i]), w_in[i], cmp_pos[i], cmp_w1[i], cmp_w2[i])
        x_p = layer_tail(x_p, merge(o_n, o_d, g_m, w_branch_nsa[i], w_branch_dsa[i], w_out[i]), p_prompt[i],
                         norm_mlp[i], w_up[i], w_down[i], norm_ple[i], w_ple_gate[i], w_ple[i])
        cmp_p.append(kc); sel_p.append(kss); dkv_p.append(kd); didx_p.append(ki); win_p.append(wn)
        o_n, o_d, g_m, kc, kss, kd, ki, wn = sample_mixer(rmsnorm(x_s, norm_mix[i]), page_table, cache_nsa_cmp[i],
                                                          cache_nsa_sel[i], cache_dsa_kv[i], cache_dsa_idx[i],
                                                          state_nsa_win[i], w_in[i], cmp_pos[i], cmp_w1[i], cmp_w2[i])
        x_s = layer_tail(x_s, merge(o_n, o_d, g_m, w_branch_nsa[i], w_branch_dsa[i], w_out[i]), p_sample[i],
                         norm_mlp[i], w_up[i], w_down[i], norm_ple[i], w_ple_gate[i], w_ple[i])
        cmp_s.append(kc); sel_s.append(kss); dkv_s.append(kd); didx_s.append(ki); win_s.append(wn)
    y_prompt = rmsnorm(x_p, norm_final)
    y_sample = rmsnorm(x_s, norm_final)
    return (y_prompt, y_sample,
            jnp.stack(cmp_p), jnp.stack(cmp_s), jnp.stack(sel_p), jnp.stack(sel_s),
            jnp.stack(dkv_p), jnp.stack(dkv_s), jnp.stack(didx_p), jnp.stack(didx_s),
            jnp.stack(win_p), jnp.stack(win_s))
```

```python
import math
import numpy as np
from contextlib import ExitStack
import concourse.bass as bass
import concourse.mybir as mybir
from concourse.bass_utils import run_bass_kernel_spmd

F32 = mybir.dt.float32
BF16 = mybir.dt.bfloat16
I32 = mybir.dt.int32
ALU = mybir.AluOpType
AF = mybir.ActivationFunctionType
AX = mybir.AxisListType

ENGS = ("pe", "act", "dve", "pool", "sp")
N_DMA_SEMS = 24
PI = math.pi

D = 4096
SEQ = 4096
NT = 32
PROJ_W = 16000
C_QN, C_KVC, C_KVS, C_KVW, C_GN, C_QD, C_KVD, C_QI, C_KI, C_WI, C_GM = (
    0, 2048, 2560, 3072, 3584, 3632, 5680, 6704, 7728, 7792, 7808)


class Dep:
    __slots__ = ("w", "r")

    def __init__(self):
        self.w = None
        self.r = {}


class Buf:
    def __init__(self, t, tracked=True):
        self.t = t
        self.d = Dep()
        self.tracked = tracked


class Ring:
    def __init__(self, bufs):
        self.bufs = bufs
        self.i = 0

    def next(self):
        b = self.bufs[self.i]
        self.i = (self.i + 1) % len(self.bufs)
        return b


class Prog:
    def __init__(self, nc, es):
        self.nc = nc
        self.es = es
        self.aes = es
        self.q = {e: [] for e in ENGS}
        self.cnt = {e: 0 for e in ENGS}
        self.seen = {e: {} for e in ENGS}
        self.sems = {}
        self.ekey = {}
        for e in ENGS:
            self.ekey[e] = (e, 0)
            self.sems[(e, 0)] = es.enter_context(nc.semaphore("s_" + e))
        self.dsem_cnt = [0] * N_DMA_SEMS
        for j in range(N_DMA_SEMS):
            self.sems[("d", j)] = es.enter_context(nc.semaphore("sd%d" % j))
        self.dnext = 0

    def sb(self, name, shape, dt):
        self.nalloc = getattr(self, "nalloc", 0) + 1
        return Buf(self.aes.enter_context(self.nc.sbuf_tensor("%s_%d" % (name, self.nalloc), list(shape), dt)))

    def barrier(self):
        snap = [(self.ekey[e], self.cnt[e]) for e in ENGS] + [(("d", j), self.dsem_cnt[j]) for j in range(N_DMA_SEMS)]
        for eng in ENGS:
            need = []
            for kk, v in snap:
                if v > self.seen[eng].get(kk, 0):
                    self.seen[eng][kk] = v
                    need.append((kk, v))
            self.q[eng].append((need, None, None))
        for e in ENGS:
            if self.cnt[e] > 12000:
                ep = self.ekey[e][1] + 1
                self.ekey[e] = (e, ep)
                self.sems[(e, ep)] = self.es.enter_context(self.nc.semaphore("s_%s_%d" % (e, ep)))
                self.cnt[e] = 0

    def scope(self):
        prog = self

        class _S:
            def __enter__(s_):
                s_.old = prog.aes
                s_.st = ExitStack()
                s_.st.__enter__()
                prog.aes = s_.st
                return s_

            def __exit__(s_, *a):
                prog.barrier()
                prog.aes = s_.old
                return s_.st.__exit__(*a)
        return _S()

    def ps(self, name, shape, dt):
        return Buf(self.es.enter_context(self.nc.psum_tensor(name, list(shape), dt)))

    def _deps(self, eng, reads, writes):
        waits = {}

        def add(kv):
            if kv is not None and kv[1] > waits.get(kv[0], 0):
                waits[kv[0]] = kv[1]

        for t in reads:
            if t.tracked:
                add(t.d.w)
        for t in writes:
            if not t.tracked:
                continue
            add(t.d.w)
            for kv in t.d.r.items():
                add(kv)
        need = []
        seen = self.seen[eng]
        for k, v in waits.items():
            if k[0] == "pe" and eng == "pe":
                continue
            if v > seen.get(k, 0):
                seen[k] = v
                need.append((k, v))
        return need

    def op(self, eng, fn, reads=(), writes=()):
        need = self._deps(eng, reads, writes)
        self.cnt[eng] += 1
        my = self.cnt[eng]
        key = self.ekey[eng]
        self.q[eng].append((need, fn, (key, 1)))
        for t in reads:
            t.d.r[key] = my
        for t in writes:
            t.d.w = (key, my)
            t.d.r = {}

    def dma(self, fn, reads=(), writes=(), eng="sp"):
        j = self.dnext
        self.dnext = (self.dnext + 1) % N_DMA_SEMS
        key = ("d", j)
        need = self._deps(eng, reads, writes)
        prev = self.dsem_cnt[j]
        if prev > self.seen[eng].get(key, 0):
            self.seen[eng][key] = prev
            need.append((key, prev))
        self.dsem_cnt[j] += 16
        tgt = self.dsem_cnt[j]
        self.q[eng].append((need, fn, (key, 16)))
        for t in reads:
            t.d.r[key] = tgt
        for t in writes:
            t.d.w = (key, tgt)
            t.d.r = {}

    def finish(self, final_bufs):
        self.barrier()
        nc, sems, q = self.nc, self.sems, self.q
        with nc.Block() as block:
            def run(engname):
                def body(e):
                    for need, fn, inc in q[engname]:
                        for k, v in need:
                            e.wait_ge(sems[k], v)
                        if fn is not None:
                            fn(e).then_inc(sems[inc[0]], inc[1])
                return body
            block.tensor(run("pe"))
            block.scalar(run("act"))
            block.vector(run("dve"))
            block.gpsimd(run("pool"))
            block.sync(run("sp"))


class KB:
    def __init__(self, nc, es):
        self.nc = nc
        self.P = Prog(nc, es)

    def dma(self, out, in_, reads, writes, eng="sp"):
        self.P.dma(lambda e: e.dma_start(out=out, in_=in_), reads, writes, eng=eng)

    def mm(self, out, lhsT, rhs, start, stop, reads, writes, sgc=False):
        self.P.op("pe", lambda e: e.matmul(out, lhsT=lhsT, rhs=rhs, start=start, stop=stop,
                                           skip_group_check=sgc), reads, writes)

    def tr(self, out, in_, ident, reads, writes):
        self.P.op("pe", lambda e: e.transpose(out=out, in_=in_, identity=ident), reads, writes)

    def act(self, out, in_, func, reads, writes, **kw):
        self.P.op("act", lambda e: e.activation(out=out, in_=in_, func=func, **kw), reads, writes)

    def ts(self, eng, out, in0, s1, s2, op0, op1, reads, writes, **kw):
        if op1 is None:
            self.P.op(eng, lambda e: e.tensor_single_scalar(out=out, in_=in0, scalar=s1, op=op0), reads, writes)
        else:
            self.P.op(eng, lambda e: e.tensor_scalar(out=out, in0=in0, scalar1=s1, scalar2=s2, op0=op0, op1=op1,
                                                     **kw), reads, writes)

    def tt(self, eng, out, in0, in1, op, reads, writes):
        self.P.op(eng, lambda e: e.tensor_tensor(out=out, in0=in0, in1=in1, op=op), reads, writes)

    def stt(self, eng, out, in0, scalar, in1, op0, op1, reads, writes):
        self.P.op(eng, lambda e: e.scalar_tensor_tensor(out=out, in0=in0, scalar=scalar, in1=in1, op0=op0,
                                                        op1=op1), reads, writes)

    def cp(self, eng, out, in_, reads, writes):
        if eng == "act":
            self.P.op("act", lambda e: e.copy(out=out, in_=in_), reads, writes)
        else:
            self.P.op(eng, lambda e: e.tensor_copy(out=out, in_=in_), reads, writes)

    def memset(self, eng, ap, val, writes):
        self.P.op(eng, lambda e: e.memset(ap, val), (), writes)


SCALE = 128.0 ** -0.5
NSLOT = 9
DEBUG = False


def build_program():
    nc = bass.Bass("TRN2", target_bir_lowering=False)

    def din(name, shape, dt=F32):
        return Buf(nc.dram_tensor(name, list(shape), dt, kind="ExternalInput").ap(), tracked=False)

    def dout(name, shape, dt=F32):
        return Buf(nc.dram_tensor(name, list(shape), dt, kind="ExternalOutput").ap(), tracked=False)

    def dscr(name, shape, dt):
        return Buf(nc.dram_tensor(name, list(shape), dt, kind="Internal").ap(), tracked=False)

    xb = din("xb", [SEQ, D])
    xq = din("xq", [NSLOT * 128, D])
    pos = din("pos", [128, 64])
    cmk = din("cmk", [128, 32])
    invf = din("invf", [1, 96])
    w_in = din("w_in", [D, PROJ_W])
    norm_mix = din("norm_mix", [128, 32])
    cmp_pos = din("cmp_pos", [2, 32, 128])
    cmp_w1 = din("cmp_w1", [2, 32, 128, 256])
    cmp_w2 = din("cmp_w2", [2, 256, 128])
    pq = din("pq", [NSLOT * 128, 256])
    ptab = din("ptab", [1, 256], I32)
    c_cmp = din("c_cmp", [2560 * 128, 512])
    c_sel = din("c_sel", [2560 * 128, 512])
    c_dkv = din("c_dkv", [2560 * 128, 1024])
    c_didx = din("c_didx", [2560 * 128, 64])
    state_win = din("state_win", [4, 512, 512])
    w_bn = din("w_bn", [2048, D])
    w_bd = din("w_bd", [2048, D])
    w_out = din("w_out", [D, D])
    norm_mlp = din("norm_mlp", [128, 32])
    w_up = din("w_up", [D, 4 * D])
    w_down = din("w_down", [4 * D, D])
    norm_ple = din("norm_ple", [128, 32])
    w_pg = din("w_pg", [D, D])
    w_ple = din("w_ple", [256, D])
    norm_final = din("norm_final", [1, D])

    o_cmp_p = dout("cmp_p", [SEQ, 512])
    o_sel_p = dout("sel_p", [SEQ, 512])
    o_dkv_p = dout("dkv_p", [SEQ, 1024])
    o_didx_p = dout("didx_p", [SEQ, 64])
    o_win_p = dout("win_p", [512, 512])
    o_cmp_s = dout("cmp_s", [4, 512])
    o_sel_s = dout("sel_s", [4, 512])
    o_dkv_s = dout("dkv_s", [4, 1024])
    o_didx_s = dout("didx_s", [4, 64])
    o_win_s = dout("win_s", [4, 512, 512])
    o_y = dout("y_own", [NSLOT * 128, D])
    outs = [o_cmp_p, o_sel_p, o_dkv_p, o_didx_p, o_win_p, o_cmp_s, o_sel_s, o_dkv_s, o_didx_s, o_win_s, o_y]

    kT_sel = dscr("kT_sel", [2, 128, SEQ], BF16)
    kT_win = dscr("kT_win", [2, 128, SEQ], BF16)
    kT_dsa = dscr("kT_dsa", [4, 128, SEQ], BF16)
    kT_idx = dscr("kT_idx", [64, SEQ], BF16)
    kvT_cmp = dscr("kvT_cmp", [4, 128, SEQ], BF16)
    v_sel = dscr("v_sel", [SEQ, 2, 128], BF16)
    v_win = dscr("v_win", [SEQ, 2, 128], BF16)
    v_dsa = dscr("v_dsa", [SEQ, 4, 128], BF16)
    qT_n = dscr("qT_n", [NSLOT, 128, 16, 128], BF16)
    qT_d = dscr("qT_d", [NSLOT, 128, 16, 128], BF16)
    qiT = dscr("qiT", [NSLOT, 128, 8, 128], BF16)
    gn_s = dscr("gn_s", [NSLOT, 128, 48], F32)
    wi_s = dscr("wi_s", [NSLOT, 128, 16], F32)
    gmT = dscr("gmT", [64, 128, NSLOT * 128], BF16)
    x1s = dscr("x1s", [NSLOT * 128, D], F32)
    snew_sel = dscr("snew_sel", [4, 512], F32)
    snew_dsa = dscr("snew_dsa", [4, 1024], F32)
    snew_idx = dscr("snew_idx", [4, 64], F32)
    x2s = dscr("x2s", [NSLOT * 128, D], F32)
    x3s = dscr("x3s", [NSLOT * 128, D], F32)
    uT_s = dscr("uT_s", [128, 128, NSLOT * 128], BF16)
    if DEBUG:
        oT_s = dout("oT_s", [NSLOT, 128, 32, 128], BF16)
        outs.append(oT_s)
        dbg_sc = dout("dbg_sc", [128, 512]); outs.append(dbg_sc)
        dbg_sc2 = dout("dbg_sc2", [128, 512]); outs.append(dbg_sc2)
        dbg_selm = dout("dbg_selm", [128, 512], BF16); outs.append(dbg_selm)
        dbg_mT = dout("dbg_mT", [128, 4, 128], BF16); outs.append(dbg_mT)
        dbg_qd = dout("dbg_qd", [128, 16, 128], BF16); outs.append(dbg_qd)
        dbg_obf = dout("dbg_obf", [128, 32, 128], BF16); outs.append(dbg_obf)
        dbg_acc = dout("dbg_acc", [128, 4, 256]); outs.append(dbg_acc)
        dbg_kt = dout("dbg_kt", [128, 512], BF16); outs.append(dbg_kt)
        dbg_v1 = dout("dbg_v1", [128, 4, 136], BF16); outs.append(dbg_v1)
    else:
        oT_s = dscr("oT_s", [NSLOT, 128, 32, 128], BF16)

    with ExitStack() as es:
        k = KB(nc, es)
        P = k.P
        ident = P.sb("ident", [128, 128], BF16)
        io_t = P.sb("io_t", [128, 128], F32)
        P.op("pool", lambda e: e.iota(io_t.t[:], pattern=[[1, 128]], base=0, channel_multiplier=-1,
                                      allow_small_or_imprecise_dtypes=True), (), [io_t])
        k.ts("dve", ident.t[:], io_t.t[:], 0.0, None, ALU.is_equal, None, [io_t], [ident])
        pos_t = P.sb("pos_t", [128, 64], F32)
        k.dma(pos_t.t[:], pos.t, [pos], [pos_t])
        cmk_t = P.sb("cmk_t", [128, 32], F32)
        k.dma(cmk_t.t[:], cmk.t, [cmk], [cmk_t])
        inv_t = P.sb("inv_t", [128, 96], F32)
        k.dma(inv_t.t[:], invf.t.broadcast_to([128, 96]), [invf], [inv_t])
        gT = P.sb("gT", [128, 32], F32)
        k.dma(gT.t[:], norm_mix.t, [norm_mix], [gT])
        for s_ in range(4):
            k.dma(o_win_s.t[s_, 0:511, :], state_win.t[s_, 1:512, :], [state_win], [o_win_s])
        pmm = Ring([P.ps("pmm%d" % i, [128, 512], F32) for i in range(4)])
        ptr = Ring([P.ps("ptr%d" % i, [128, 1024], BF16) for i in range(2)])
        pacc = P.ps("pacc", [128, 4, 256], F32)

        with P.scope():
            hT = P.sb("hT", [128, 32, NSLOT * 128], BF16)
            hT_d = [Buf(None) for _ in range(NSLOT)]
            xt_ring = Ring([P.sb("xt", [128, D], F32) for i in range(2)])
            xn = P.sb("xn", [128, D], BF16)
            ssq = Ring([P.sb("ssq", [128, 1], F32) for i in range(2)])
            wsb_ring = Ring([P.sb("wsb", [128, 32, 512], BF16) for i in range(2)])
            stage_ring = Ring([P.sb("stg", [128, 512], F32) for i in range(3)])
            ob_ring = Ring([P.sb("ob", [128, 512], BF16) for i in range(2)])
            tmp_ring = Ring([P.sb("rtmp", [128, 512], F32) for i in range(4)])
            ktsb_ring = Ring([P.sb("ktsb", [128, 512], BF16) for i in range(2)])
            gsb_ring = Ring([P.sb("gsb", [128, 384], BF16) for i in range(3)])
            trig_all = P.sb("trig_all", [128, NSLOT, 192], F32)
            trig_d = [Buf(None) for _ in range(NSLOT)]
            tg_a = P.sb("tg_a", [128, 192], F32)
            tg_i = P.sb("tg_i", [128, 192], I32)
            tg_f = P.sb("tg_f", [128, 192], F32)

            def make_trig(pos_col, slot):
                k.ts("dve", tg_a.t[:, 0:96], inv_t.t[:], pos_t.t[:, pos_col:pos_col + 1], 1.0 / (2 * PI),
                     ALU.mult, ALU.mult, [inv_t, pos_t], [tg_a])
                k.ts("dve", tg_a.t[:, 96:192], tg_a.t[:, 0:96], 0.25, None, ALU.add, None, [tg_a], [tg_a])
                k.cp("dve", tg_i.t[:], tg_a.t[:], [tg_a], [tg_i])
                k.cp("dve", tg_f.t[:], tg_i.t[:], [tg_i], [tg_f])
                k.tt("dve", tg_a.t[:], tg_a.t[:], tg_f.t[:], ALU.subtract, [tg_a, tg_f], [tg_a])
                k.stt("dve", tg_f.t[:], tg_a.t[:], 0.5, tg_a.t[:], ALU.is_gt, ALU.subtract, [tg_a], [tg_f])
                k.act(trig_all.t[:, slot, :], tg_f.t[:], AF.Sin, [tg_f], [trig_d[slot]], scale=-2 * PI)

            def rope(src_ps, src_ap, dst, dst_ap, nh, half, slot, foff):
                td = trig_d[slot]
                cb = trig_all.t[:, slot, 96 + foff:96 + foff + half].unsqueeze(1).broadcast_to([128, nh, half])
                sb_ = trig_all.t[:, slot, foff:foff + half].unsqueeze(1).broadcast_to([128, nh, half])
                x1 = src_ap[:, :, 0:half]
                x2 = src_ap[:, :, half:2 * half]
                t1 = tmp_ring.next(); t2 = tmp_ring.next()
                a1 = t1.t[:, 0:nh * half].rearrange("p (h d) -> p h d", h=nh)
                a2 = t2.t[:, 0:nh * half].rearrange("p (h d) -> p h d", h=nh)
                k.tt("dve", a1, x1, cb, ALU.mult, [src_ps, td], [t1])
                k.tt("dve", a2, x2, sb_, ALU.mult, [src_ps, td], [t2])
                k.tt("dve", dst_ap[:, :, 0:half], a1, a2, ALU.subtract, [t1, t2], [dst])
                t3 = tmp_ring.next(); t4 = tmp_ring.next()
                a3 = t3.t[:, 0:nh * half].rearrange("p (h d) -> p h d", h=nh)
                a4 = t4.t[:, 0:nh * half].rearrange("p (h d) -> p h d", h=nh)
                k.tt("dve", a3, x1, sb_, ALU.mult, [src_ps, td], [t3])
                k.tt("dve", a4, x2, cb, ALU.mult, [src_ps, td], [t4])
                k.tt("dve", dst_ap[:, :, half:2 * half], a3, a4, ALU.add, [t3, t4], [dst])

            def norm_tile(src_rows_ap, src_buf, slot):
                xt = xt_ring.next(); sq = ssq.next()
                k.dma(xt.t[:], src_rows_ap, [src_buf], [xt])
                k.act(xn.t[:], xt.t[:], AF.Square, [xt], [xn, sq], accum_out=sq.t[:])
                k.ts("dve", sq.t[:], sq.t[:], 1.0 / D, 1e-6, ALU.mult, ALU.add, [sq], [sq])
                k.act(sq.t[:], sq.t[:], AF.Sqrt, [sq], [sq])
                P.op("dve", lambda e: e.reciprocal(out=sq.t[:], in_=sq.t[:]), [sq], [sq])
                k.ts("dve", xn.t[:], xt.t[:], sq.t[:, 0:1], None, ALU.mult, None, [xt, sq], [xn])
                for kc4 in range(8):
                    pt = ptr.next()
                    for u in range(4):
                        kc = kc4 * 4 + u
                        k.tr(pt.t[:, u * 128:(u + 1) * 128], xn.t[:, kc * 128:(kc + 1) * 128], ident.t[:],
                             [xn, ident], [pt])
                    for u in range(4):
                        kc = kc4 * 4 + u
                        k.act(hT.t[:, kc, slot * 128:(slot + 1) * 128], pt.t[:, u * 128:(u + 1) * 128], AF.Copy,
                              [pt, gT], [hT_d[slot]], scale=gT.t[:, kc:kc + 1])

            def load_w(wbuf, c0, ncols):
                w = wsb_ring.next()
                if not hasattr(w, "hb"):
                    w.hb = Buf(w.t)
                src = wbuf.t[:, c0:c0 + ncols].rearrange("(kc p) c -> p kc c", p=128)
                for half in range(2):
                    k.dma(w.t[:, half * 16:(half + 1) * 16, 0:ncols], src[:, half * 16:(half + 1) * 16, :],
                          [wbuf], [w if half == 0 else w.hb], eng="pool")
                return w

            def proj_tile(w, ncols, slot):
                ps = pmm.next()
                for kc in range(32):
                    k.mm(ps.t[:, 0:ncols], hT.t[:, kc, slot * 128:(slot + 1) * 128], w.t[:, kc, 0:ncols],
                         kc == 0, kc == 31, [hT_d[slot], w if kc < 16 else w.hb], [ps])
                return ps

            kv_groups = [
                ("cmp", C_KVC, 512), ("sel", C_KVS, 512), ("win", C_KVW, 512),
                ("dsa0", C_KVD, 512), ("dsa1", C_KVD + 512, 512), ("idx", C_KI, 64),
            ]
            for sbk in range(4):
                tiles = [(sbk * 8 + u, xb.t[(sbk * 8 + u) * 128:(sbk * 8 + u + 1) * 128, :], xb, sbk * 8 + u)
                         for u in range(8)]
                if sbk == 0:
                    tiles.append((-1, xq.t[1024:1152, :], xq, 40))
                for slot, (tix, rows, src, pcol) in enumerate(tiles):
                    norm_tile(rows, src, slot)
                    make_trig(pcol, slot)
                wnext = load_w(w_in, kv_groups[0][1], kv_groups[0][2])
                for gi, (gname, c0, ncols) in enumerate(kv_groups):
                    w = wnext
                    if gi + 1 < len(kv_groups):
                        wnext = load_w(w_in, kv_groups[gi + 1][1], kv_groups[gi + 1][2])
                    for slot, (tix, rows, src, pcol) in enumerate(tiles):
                        ps = proj_tile(w, ncols, slot)
                        st = stage_ring.next()
                        ob = ob_ring.next()
                        r0 = tix * 128
                        if gname == "idx":
                            rope(ps, ps.t[:, 0:64].rearrange("p (h d) -> p h d", h=1), st,
                                 st.t[:, 0:64].rearrange("p (h d) -> p h d", h=1), 1, 32, slot, 64)
                            if tix >= 0:
                                k.dma(o_didx_p.t[r0:r0 + 128, :], st.t[:, 0:64], [st], [o_didx_p])
                                k.cp("pool", ob.t[:, 0:64], st.t[:, 0:64], [st], [ob])
                                pt = ptr.next()
                                k.tr(pt.t[0:64, 0:128], ob.t[:, 0:64], ident.t[:], [ob, ident], [pt])
                                kt = ktsb_ring.next()
                                k.cp("act", kt.t[0:64, 0:128], pt.t[0:64, 0:128], [pt], [kt])
                                k.dma(kT_idx.t[:, r0:r0 + 128], kt.t[0:64, 0:128], [kt], [kT_idx])
                            else:
                                k.dma(o_didx_s.t[:, :], st.t[0:4, 0:64], [st], [o_didx_s])
                                k.dma(snew_idx.t[:, :], st.t[0:4, 0:64], [st], [snew_idx])
                            continue
                        psv = ps.t[:, :].rearrange("p (h c d) -> p h c d", h=2, c=2)
                        stv = st.t[:, :].rearrange("p (h c d) -> p h c d", h=2, c=2)
                        rope(ps, psv[:, :, 0, :], st, stv[:, :, 0, :], 2, 64, slot, 0)
                        k.cp("act", stv[:, :, 1, :], psv[:, :, 1, :], [ps], [st])
                        if tix < 0:
                            if gname == "win":
                                k.dma(o_win_s.t[:, 511, :], st.t[0:4, :], [st], [o_win_s])
                                continue
                            od = {"cmp": (o_cmp_s, 0), "sel": (o_sel_s, 0),
                                  "dsa0": (o_dkv_s, 0), "dsa1": (o_dkv_s, 512)}[gname]
                            k.dma(od[0].t[:, od[1]:od[1] + 512], st.t[0:4, :], [st], [od[0]])
                            if gname == "sel":
                                k.dma(snew_sel.t[:, :], st.t[0:4, :], [st], [snew_sel])
                            elif gname in ("dsa0", "dsa1"):
                                k.dma(snew_dsa.t[:, od[1]:od[1] + 512], st.t[0:4, :], [st], [snew_dsa])
                            continue
                        if gname == "cmp":
                            k.dma(o_cmp_p.t[r0:r0 + 128, :], st.t[:, :], [st], [o_cmp_p])
                        elif gname == "sel":
                            k.dma(o_sel_p.t[r0:r0 + 128, :], st.t[:, :], [st], [o_sel_p])
                        elif gname == "win":
                            if tix >= 28:
                                k.dma(o_win_p.t[(tix - 28) * 128:(tix - 27) * 128, :], st.t[:, :], [st], [o_win_p])
                        elif gname == "dsa0":
                            k.dma(o_dkv_p.t[r0:r0 + 128, 0:512], st.t[:, :], [st], [o_dkv_p])
                        elif gname == "dsa1":
                            k.dma(o_dkv_p.t[r0:r0 + 128, 512:1024], st.t[:, :], [st], [o_dkv_p])
                        k.cp("pool", ob.t[:, :], st.t[:, :], [st], [ob])
                        obv = ob.t[:, :].rearrange("p (h c d) -> p h c d", h=2, c=2)
                        pt = ptr.next()
                        kt = ktsb_ring.next()
                        if gname == "cmp":
                            for h in range(2):
                                for c in range(2):
                                    q4 = h * 2 + c
                                    k.tr(pt.t[:, q4 * 128:(q4 + 1) * 128], obv[:, h, c, :], ident.t[:],
                                         [ob, ident], [pt])
                            k.cp("act", kt.t[:, 0:512], pt.t[:, 0:512], [pt], [kt])
                            k.dma(kvT_cmp.t[:, :, r0:r0 + 128].rearrange("f p t -> p f t"),
                                  kt.t[:, 0:512].rearrange("p (f t) -> p f t", f=4), [kt], [kvT_cmp])
                        else:
                            for h in range(2):
                                k.tr(pt.t[:, h * 128:(h + 1) * 128], obv[:, h, 0, :], ident.t[:], [ob, ident], [pt])
                            k.cp("act", kt.t[:, 0:256], pt.t[:, 0:256], [pt], [kt])
                            ktd, vd, h0 = {"sel": (kT_sel, v_sel, 0), "win": (kT_win, v_win, 0),
                                           "dsa0": (kT_dsa, v_dsa, 0), "dsa1": (kT_dsa, v_dsa, 2)}[gname]
                            k.dma(ktd.t[h0:h0 + 2, :, r0:r0 + 128].rearrange("f p t -> p f t"),
                                  kt.t[:, 0:256].rearrange("p (f t) -> p f t", f=2), [kt], [ktd])
                            k.dma(vd.t[r0:r0 + 128, h0:h0 + 2, :], obv[:, :, 1, :], [ob], [vd])

            for slot in range(NSLOT):
                norm_tile(xq.t[slot * 128:(slot + 1) * 128, :], xq, slot)
                make_trig(32 + slot, slot)
            own_groups = ([("qn", C_QN + 512 * i, 512, i) for i in range(4)]
                          + [("qd", C_QD + 512 * i, 512, i) for i in range(4)]
                          + [("qi", C_QI + 512 * i, 512, i) for i in range(2)]
                          + [("gn", C_GN, 48, 0), ("wi", C_WI, 16, 0)]
                          + [("gm", C_GM + 512 * i, 512, i) for i in range(16)])
            wnext = load_w(w_in, own_groups[0][1], own_groups[0][2])
            for gi, (gname, c0, ncols, gidx) in enumerate(own_groups):
                w = wnext
                if gi + 1 < len(own_groups):
                    wnext = load_w(w_in, own_groups[gi + 1][1], own_groups[gi + 1][2])
                if gname == "gm":
                    for cc in range(4):
                        chunk = gidx * 4 + cc
                        for tc in range(3):
                            ps = pmm.next()
                            for kc in range(32):
                                k.mm(ps.t[:, 0:384], w.t[:, kc, cc * 128:(cc + 1) * 128],
                                     hT.t[:, kc, tc * 384:(tc + 1) * 384], kc == 0, kc == 31,
                                     [w if kc < 16 else w.hb] + hT_d[tc * 3:tc * 3 + 3], [ps])
                            gs = gsb_ring.next()
                            k.act(gs.t[:, :], ps.t[:, 0:384], AF.Sigmoid, [ps], [gs])
                            k.dma(gmT.t[chunk, :, tc * 384:(tc + 1) * 384], gs.t[:, :], [gs], [gmT])
                    continue
                for slot in range(NSLOT):
                    ps = proj_tile(w, ncols, slot)
                    if gname in ("qn", "qd", "qi"):
                        ob = ob_ring.next()
                        if gname == "qi":
                            rope(ps, ps.t[:, :].rearrange("p (h d) -> p h d", h=8), ob,
                                 ob.t[:, :].rearrange("p (h d) -> p h d", h=8), 8, 32, slot, 64)
                        else:
                            rope(ps, ps.t[:, :].rearrange("p (h d) -> p h d", h=4), ob,
                                 ob.t[:, :].rearrange("p (h d) -> p h d", h=4), 4, 64, slot, 0)
                        pt = ptr.next()
                        kt = ktsb_ring.next()
                        for u in range(4):
                            k.tr(pt.t[:, u * 128:(u + 1) * 128], ob.t[:, u * 128:(u + 1) * 128], ident.t[:],
                                 [ob, ident], [pt])
                        k.cp("act", kt.t[:, 0:512], pt.t[:, 0:512], [pt], [kt])
                        dst = {"qn": qT_n, "qd": qT_d, "qi": qiT}[gname]
                        k.dma(dst.t[slot, :, gidx * 4:gidx * 4 + 4, :],
                              kt.t[:, 0:512].rearrange("p (f t) -> p f t", f=4), [kt], [dst])
                    elif gname == "gn":
                        st = stage_ring.next()
                        k.act(st.t[:, 0:48], ps.t[:, 0:48], AF.Sigmoid, [ps], [st])
                        k.dma(gn_s.t[slot, :, :], st.t[:, 0:48], [st], [gn_s])
                    elif gname == "wi":
                        st = stage_ring.next()
                        k.ts("dve", st.t[:, 0:16], ps.t[:, 0:16], 1.0 / 32.0, None, ALU.mult, None, [ps], [st])
                        k.dma(wi_s.t[slot, :, :], st.t[:, 0:16], [st], [wi_s])

        with P.scope():
            kcT = P.sb("kcT", [128, 2, 256], BF16)
            vc1 = P.sb("vc1", [128, 2, 2, 200], BF16)
            k.memset("dve", kcT.t[:], 0.0, [kcT])
            k.memset("dve", vc1.t[:], 0.0, [vc1])
            with P.scope():
                w1sb = [P.sb("w1sb", [128, 32, 256], BF16) for c in range(2)]
                w2sb = P.sb("w2sb", [128, 2, 2, 128], BF16)
                posb = P.sb("posb", [32, 2, 128], BF16)
                posT = P.sb("posT", [128, 2, 32], BF16)
                bias_sb = P.sb("bias_sb", [128, 4], F32)
                hidT = P.sb("hidT", [128, 2, 256], BF16)
                kv_ring = Ring([P.sb("kvc", [128, SEQ], BF16) for i in range(2)])
                k.memset("dve", hidT.t[:], 0.0, [hidT])
                for c in range(2):
                    k.dma(w1sb[c].t[:], cmp_w1.t[c].rearrange("l d h -> d l h"), [cmp_w1], [w1sb[c]], eng="pool")
                k.dma(w2sb.t[:], cmp_w2.t.rearrange("c (hh p) d -> p c hh d", p=128), [cmp_w2], [w2sb], eng="pool")
                k.dma(posb.t[:], cmp_pos.t.rearrange("c l d -> l c d"), [cmp_pos], [posb], eng="pool")
                for c in range(2):
                    pt = ptr.next()
                    k.tr(pt.t[:, 0:32], posb.t[:, c, :], ident.t[0:32, 0:32], [posb, ident], [pt])
                    k.cp("act", posT.t[:, c, :], pt.t[:, 0:32], [pt], [posT])
                for c in range(2):
                    for hh in range(2):
                        ps = pmm.next()
                        for l in range(32):
                            k.mm(ps.t[:, 0:1], w1sb[c].t[:, l, hh * 128:(hh + 1) * 128], posT.t[:, c, l:l + 1],
                                 l == 0, l == 31, [w1sb[c], posT], [ps])
                        k.cp("act", bias_sb.t[:, c * 2 + hh:c * 2 + hh + 1], ps.t[:, 0:1], [ps], [bias_sb])
                for g in range(2):
                    for c in range(2):
                        kv = kv_ring.next()
                        k.dma(kv.t[:, :], kvT_cmp.t[g * 2 + c], [kvT_cmp], [kv])
                        for hh in range(2):
                            ps = pmm.next()
                            for l in range(32):
                                k.mm(ps.t[:, 0:255], w1sb[c].t[:, l, hh * 128:(hh + 1) * 128],
                                     kv.t[:, l:l + 16 * 254 + 1:16], l == 0, l == 31, [w1sb[c], kv], [ps])
                            k.act(hidT.t[:, hh, 0:255], ps.t[:, 0:255], AF.Relu, [ps, bias_sb], [hidT],
                                  bias=bias_sb.t[:, c * 2 + hh:c * 2 + hh + 1])
                        if c == 0:
                            ps = pmm.next()
                            for hh in range(2):
                                k.mm(ps.t[:, 0:255], w2sb.t[:, 0, hh, :], hidT.t[:, hh, 0:255], hh == 0, hh == 1,
                                     [w2sb, hidT], [ps])
                            k.cp("act", kcT.t[:, g, 0:255], ps.t[:, 0:255], [ps], [kcT])
                        else:
                            for nt in range(2):
                                ps = pmm.next()
                                for hh in range(2):
                                    k.mm(ps.t[:, 0:128], hidT.t[:, hh, nt * 128:(nt + 1) * 128], w2sb.t[:, 1, hh, :],
                                         hh == 0, hh == 1, [w2sb, hidT], [ps])
                                k.cp("act", vc1.t[:, g, nt, 0:128], ps.t[:, 0:128], [ps], [vc1])
            ovl_v = P.sb("ovl_v", [128, 64], F32)
            ovl_a = P.sb("ovl_a", [128, 64], F32)
            for nt in range(2):
                P.op("pool", lambda e, nt=nt: e.iota(ovl_v.t[:], pattern=[[4, 64]], base=-128 * nt,
                                                     channel_multiplier=-1, allow_small_or_imprecise_dtypes=True),
                     (), [ovl_v])
                k.ts("dve", ovl_a.t[:], ovl_v.t[:], -3.0, None, ALU.is_ge, None, [ovl_v], [ovl_a])
                for g in range(2):
                    k.stt("dve", vc1.t[:, g, nt, 129:193], ovl_v.t[:], 1.0, ovl_a.t[:], ALU.is_le, ALU.mult,
                          [ovl_v, ovl_a], [vc1])
                    k.memset("dve", vc1.t[:, g, nt, 128:129], 1.0, [vc1])

            iota_k = P.sb("iota_k", [128, SEQ], F32)
            P.op("pool", lambda e: e.iota(iota_k.t[:], pattern=[[1, SEQ]], base=0, channel_multiplier=0,
                                          allow_small_or_imprecise_dtypes=True), (), [iota_k])
            iota_j = P.sb("iota_j", [128, 64], F32)
            P.op("pool", lambda e: e.iota(iota_j.t[:], pattern=[[1, 64]], base=0, channel_multiplier=0,
                                          allow_small_or_imprecise_dtypes=True), (), [iota_j])
            io_q = P.sb("io_q", [128, 128], F32)
            P.op("pool", lambda e: e.iota(io_q.t[:], pattern=[[1, 128]], base=0, channel_multiplier=0,
                                          allow_small_or_imprecise_dtypes=True), (), [io_q])
            thr_n = P.sb("thr_n", [128, 2], F32)
            P.op("pool", lambda e: e.iota(thr_n.t[:], pattern=[[2048, 2]], base=31, channel_multiplier=16,
                                          allow_small_or_imprecise_dtypes=True), (), [thr_n])
            e0 = P.sb("e0", [128, 64], F32)
            k.ts("dve", e0.t[:], iota_j.t[:], 0.0, None, ALU.is_equal, None, [iota_j], [e0])
            tri = P.sb("tri", [128, 128], F32)
            k.ts("dve", tri.t[:], io_t.t[:], 0.0, None, ALU.is_ge, None, [io_t], [tri])
            CW = P.sb("CW", [128, 12, 128], BF16)
            for m in range(12):
                k.ts("dve", CW.t[:, m, :], tri.t[:], cmk_t.t[:, 8 + 2 * m:9 + 2 * m], cmk_t.t[:, 9 + 2 * m:10 + 2 * m],
                     ALU.mult, ALU.add, [tri, cmk_t], [CW])
            Ex = P.sb("Ex", [64, 32, 128], BF16)
            with P.scope():
                Ex_v = P.sb("Ex_v", [64, 32, 128], F32)
                Ex_a = P.sb("Ex_a", [64, 32, 128], F32)
                P.op("pool", lambda e: e.iota(Ex_v.t[:], pattern=[[128, 32], [1, 128]], base=0,
                                              channel_multiplier=-64, allow_small_or_imprecise_dtypes=True),
                     (), [Ex_v])
                k.ts("dve", Ex_a.t[:], Ex_v.t[:], 0.0, None, ALU.is_ge, None, [Ex_v], [Ex_a])
                k.stt("dve", Ex.t[:], Ex_v.t[:], 63.0, Ex_a.t[:], ALU.is_le, ALU.mult, [Ex_v, Ex_a], [Ex])
            kidx2 = P.sb("kidx2", [128, SEQ], BF16)
            k.dma(kidx2.t[0:64, :], kT_idx.t, [kT_idx], [kidx2])
            k.dma(kidx2.t[64:128, :], kT_idx.t, [kT_idx], [kidx2])

            qn_ring = Ring([P.sb("qn", [128, 16, 128], BF16) for i in range(2)])
            qd_ring = Ring([P.sb("qd", [128, 16, 128], BF16) for i in range(2)])
            qi_ring = Ring([P.sb("qi", [128, 8, 128], BF16) for i in range(2)])
            gn_ring = Ring([P.sb("gn", [128, 48], F32) for i in range(2)])
            wi_ring = Ring([P.sb("wi", [128, 16], F32) for i in range(2)])
            sc = P.sb("sc", [128, SEQ], F32)
            relu_ring = Ring([P.sb("rl", [128, 512], F32) for i in range(3)])
            m8_ring = Ring([P.sb("m8", [128, 8], F32) for i in range(4)])
            selm = P.sb("selm", [128, SEQ], BF16)
            mT = P.sb("mT", [128, 32, 128], BF16)
            ms = P.sb("ms", [128, 32, 128], BF16)
            mc = P.sb("mc", [128, 2, 128], BF16)
            kt_ring = Ring([P.sb("ktb", [128, SEQ], BF16) for i in range(3)])
            v1_ring = Ring([P.sb("v1b", [128, 32, 136], BF16) for i in range(3)])
            for vb in v1_ring.bufs:
                k.memset("pool", vb.t[:, :, 128:136], 1.0, [vb])
            e_ring = Ring([P.sb("eb", [128, 512], BF16) for i in range(4)])
            rd_ring = Ring([P.sb("rd", [128, 4], F32) for i in range(4)])
            o32 = P.sb("o32", [128, 16, 128], F32)
            obf = P.sb("obf", [128, 32, 128], BF16)
            oTsb = P.sb("oTsb", [128, 32, 128], BF16)
            impu = P.sb("impu", [128, 8, 64], F32)
            imp = P.sb("imp", [128, 64], F32)
            sw = [P.sb("sw%d" % i, [128, 64], F32) for i in range(4)]
            selb = P.sb("selb", [128, 64], BF16)
            selT = P.sb("selT", [64, 128], BF16)
            curm1 = P.sb("curm1", [128, 1], F32)
            mmul_eng = ["dve", "pool"]
            cnt = [0]

            def attend(kts, K_of, V_of, ncols, q_ap, q_buf, M_of):
                n = len(kts)
                ebs = {}

                def stage1(idx):
                    kt = kts[idx]
                    ps = pmm.next()
                    kap, kbuf = K_of(kt)
                    k.mm(ps.t[:, 0:512], kap, q_ap, True, True, [kbuf, q_buf], [ps])
                    eb = e_ring.next()
                    k.act(eb.t[:, :], ps.t[:, 0:512], AF.Exp, [ps], [eb], scale=SCALE)
                    map_, mbuf = M_of(kt)
                    eng = mmul_eng[cnt[0] % 2]; cnt[0] += 1
                    ev = eb.t[:, :].rearrange("p (r q) -> p r q", r=4)
                    k.tt(eng, ev, ev, map_.unsqueeze(1).broadcast_to([128, 4, 128]), ALU.mult, [eb, mbuf], [eb])
                    ebs[idx] = eb

                def stage2(idx):
                    eb = ebs.pop(idx)
                    vap, vbuf = V_of(kts[idx])
                    for r in range(4):
                        k.mm(pacc.t[:, r, 0:ncols], eb.t[:, r * 128:(r + 1) * 128], vap, idx == 0 and r % 2 == 0,
                             idx == n - 1, [eb, vbuf], [pacc], sgc=True)
                stage1(0)
                for idx in range(n):
                    if idx + 1 < n:
                        stage1(idx + 1)
                    stage2(idx)

            def recip_den():
                rd = rd_ring.next()
                k.ts("dve", rd.t[:, :], pacc.t[:, :, 128], 1e-30, None, ALU.max, None, [pacc], [rd])
                P.op("dve", lambda e: e.reciprocal(out=rd.t[:, :], in_=rd.t[:, :]), [rd], [rd])
                return rd

            def load_kv(ktd, vd, head, lo, hi):
                ktb = kt_ring.next(); v1b = v1_ring.next()
                nk = hi - lo
                k.dma(ktb.t[:, 0:nk * 128], ktd.t[head, :, lo * 128:hi * 128], [ktd], [ktb])
                k.dma(v1b.t[:, 0:nk, 0:128],
                      vd.t[lo * 128:hi * 128, head, :].rearrange("(kt p) d -> p kt d", p=128), [vd], [v1b])
                return ktb, v1b

            for j in range(8):
                nkt = 4 * j + 4
                Lk = nkt * 128
                qpos = pos_t.t[:, 32 + j:33 + j]
                cur = pos_t.t[:, 41 + j:42 + j]
                qn = qn_ring.next(); qd = qd_ring.next(); qi = qi_ring.next(); gn = gn_ring.next(); wi = wi_ring.next()
                k.dma(qn.t[:], qT_n.t[j], [qT_n], [qn])
                k.dma(qd.t[:], qT_d.t[j], [qT_d], [qd])
                k.dma(qi.t[:], qiT.t[j], [qiT], [qi])
                k.dma(gn.t[:], gn_s.t[j], [gn_s], [gn])
                k.dma(wi.t[:], wi_s.t[j], [wi_s], [wi])
                gn3 = gn.t[:, :].rearrange("p (h b) -> p h b", b=3)

                k.ts("dve", sc.t[:, 0:Lk], iota_k.t[:, 0:Lk], qpos, -1e30, ALU.is_gt, ALU.mult, [iota_k, pos_t], [sc])
                for c in range(nkt // 4):
                    for h in range(16):
                        hp = (h % 2) * 64
                        ps = pmm.next()
                        k.mm(ps.t[:, 0:512], qi.t[hp:hp + 64, h // 2, :], kidx2.t[hp:hp + 64, c * 512:(c + 1) * 512],
                             True, True, [qi, kidx2], [ps])
                        rl = relu_ring.next()
                        k.act(rl.t[:, :], ps.t[:, 0:512], AF.Relu, [ps], [rl])
                        k.stt("dve", sc.t[:, c * 512:(c + 1) * 512], rl.t[:, :], wi.t[:, h:h + 1],
                              sc.t[:, c * 512:(c + 1) * 512], ALU.mult, ALU.add, [rl, wi, sc], [sc])
                if DEBUG and j == 0:
                    k.dma(dbg_sc.t, sc.t[:, 0:512], [sc], [dbg_sc])
                    k.dma(dbg_qd.t, qd.t[:, :, :], [qd], [dbg_qd])
                for it in range(32):
                    m8 = m8_ring.next()
                    P.op("dve", lambda e, m8=m8, Lk=Lk: e.max(out=m8.t[:, :], in_=sc.t[:, 0:Lk]), [sc], [m8])
                    P.op("dve", lambda e, m8=m8, Lk=Lk: e.match_replace(out=sc.t[:, 0:Lk], in_to_replace=m8.t[:, :],
                                                                      in_values=sc.t[:, 0:Lk], imm_value=-3e38),
                         [sc, m8], [sc])
                k.ts("dve", selm.t[:, 0:Lk], sc.t[:, 0:Lk], -1e37, None, ALU.is_lt, None, [sc], [selm])
                k.stt("dve", selm.t[:, 0:Lk], iota_k.t[:, 0:Lk], qpos, selm.t[:, 0:Lk], ALU.is_le, ALU.mult,
                      [iota_k, pos_t, selm], [selm])
                if DEBUG and j == 0:
                    k.dma(dbg_sc2.t, sc.t[:, 0:512], [sc], [dbg_sc2])
                    k.dma(dbg_selm.t, selm.t[:, 0:512], [selm], [dbg_selm])
                for kt8 in range(0, nkt, 8):
                    pt = ptr.next()
                    nn = min(8, nkt - kt8)
                    for u in range(nn):
                        k.tr(pt.t[:, u * 128:(u + 1) * 128], selm.t[:, (kt8 + u) * 128:(kt8 + u + 1) * 128], ident.t[:],
                             [selm, ident], [pt])
                    k.cp("act", mT.t[:, kt8:kt8 + nn, :], pt.t[:, 0:nn * 128].rearrange("p (a q) -> p a q", a=nn),
                         [pt], [mT])
                for g in range(4):
                    ktb, v1b = load_kv(kT_dsa, v_dsa, g, 0, nkt)
                    attend(list(range(nkt)),
                           lambda kt, ktb=ktb: (ktb.t[:, kt * 128:(kt + 1) * 128], ktb),
                           lambda kt, v1b=v1b: (v1b.t[:, kt, 0:129], v1b), 129,
                           qd.t[:, g * 4:(g + 1) * 4, :].rearrange("p r q -> p (r q)"), qd,
                           lambda kt: (mT.t[:, kt, :], mT))
                    if DEBUG and j == 0 and g == 0:
                        dacc = P.sb("dacc", [128, 4, 256], F32)
                        k.cp("dve", dacc.t[:, 0:2, :], pacc.t[:, 0:2, :], [pacc], [dacc])
                        k.cp("dve", dacc.t[:, 2:4, :], pacc.t[:, 2:4, :], [pacc], [dacc])
                        k.dma(dbg_acc.t, dacc.t[:, :, :], [dacc], [dbg_acc])
                        k.dma(dbg_mT.t, mT.t[:, 0:4, :], [mT], [dbg_mT])
                        k.dma(dbg_kt.t, ktb.t[:, 0:512], [ktb], [dbg_kt])
                        k.dma(dbg_v1.t, v1b.t[:, 0:4, :], [v1b], [dbg_v1])
                    rd = recip_den()
                    k.tt("dve", obf.t[:, 16 + g * 4:16 + (g + 1) * 4, :], pacc.t[:, :, 0:128],
                         rd.t[:, :].unsqueeze(2).broadcast_to([128, 4, 128]), ALU.mult, [pacc, rd], [obf])

                for nt in range(2):
                    k.ts("dve", mc.t[:, nt, :], io_q.t[:, :], cmk_t.t[:, j:j + 1], thr_n.t[:, nt:nt + 1],
                         ALU.add, ALU.is_ge, [io_q, cmk_t, thr_n], [mc])
                k.ts("dve", curm1.t[:, :], cur, -1.0, None, ALU.add, None, [pos_t], [curm1])
                for g in range(2):
                    for sub in range(2):
                        h0 = g * 8 + sub * 4
                        attend([0, 1],
                               lambda nt, g=g: (kcT.t[:, g, nt * 128:(nt + 1) * 128], kcT),
                               lambda nt, g=g: (vc1.t[:, g, nt, 0:193], vc1), 193,
                               qn.t[:, h0:h0 + 4, :].rearrange("p r q -> p (r q)"), qn,
                               lambda nt: (mc.t[:, nt, :], mc))
                        rd = recip_den()
                        cf = rd_ring.next()
                        k.tt("dve", cf.t[:, :], rd.t[:, :], gn3[:, h0:h0 + 4, 0], ALU.mult, [rd, gn], [cf])
                        k.tt("dve", o32.t[:, h0:h0 + 4, :], pacc.t[:, :, 0:128],
                             cf.t[:, :].unsqueeze(2).broadcast_to([128, 4, 128]), ALU.mult, [pacc, cf], [o32])
                        k.tt("dve", impu.t[:, sub * 4:(sub + 1) * 4, :], pacc.t[:, :, 129:193],
                             rd.t[:, :].unsqueeze(2).broadcast_to([128, 4, 64]), ALU.mult, [pacc, rd], [impu])
                    P.op("dve", lambda e: e.tensor_reduce(out=imp.t[:, :], in_=impu.t[:, :, :].rearrange("p r j -> p j r"),
                                                          axis=AX.X, op=ALU.add), [impu], [imp])
                    k.ts("dve", sw[0].t[:, :], iota_j.t[:, :], cur, None, ALU.is_equal, None, [iota_j, pos_t], [sw[0]])
                    k.ts("dve", sw[1].t[:, :], iota_j.t[:, :], curm1.t[:, 0:1], None, ALU.is_equal, None,
                         [iota_j, curm1], [sw[1]])
                    k.tt("dve", sw[0].t[:, :], sw[0].t[:, :], sw[1].t[:, :], ALU.add, [sw[0], sw[1]], [sw[0]])
                    k.tt("dve", sw[0].t[:, :], sw[0].t[:, :], e0.t[:, :], ALU.add, [sw[0], e0], [sw[0]])
                    k.stt("dve", sw[1].t[:, :], sw[0].t[:, :], 1e4, imp.t[:, :], ALU.mult, ALU.add, [sw[0], imp], [sw[1]])
                    k.ts("dve", sw[2].t[:, :], iota_j.t[:, :], cur, None, ALU.is_le, None, [iota_j, pos_t], [sw[2]])
                    k.tt("dve", sw[1].t[:, :], sw[1].t[:, :], sw[2].t[:, :], ALU.mult, [sw[1], sw[2]], [sw[1]])
                    k.ts("dve", sw[2].t[:, :], sw[2].t[:, :], -1.0, 1e30, ALU.add, ALU.mult, [sw[2]], [sw[2]])
                    k.tt("dve", sw[1].t[:, :], sw[1].t[:, :], sw[2].t[:, :], ALU.add, [sw[1], sw[2]], [sw[1]])
                    ma = m8_ring.next(); mb = m8_ring.next()
                    P.op("dve", lambda e, ma=ma: e.max(out=ma.t[:, :], in_=sw[1].t[:, :]), [sw[1]], [ma])
                    P.op("dve", lambda e, ma=ma: e.match_replace(out=sw[3].t[:, :], in_to_replace=ma.t[:, :],
                                                                in_values=sw[1].t[:, :], imm_value=-3e38),
                         [sw[1], ma], [sw[3]])
                    P.op("dve", lambda e, mb=mb: e.max(out=mb.t[:, :], in_=sw[3].t[:, :]), [sw[3]], [mb])
                    k.ts("dve", selb.t[:, :], sw[1].t[:, :], mb.t[:, 7:8], None, ALU.is_ge, None, [sw[1], mb], [selb])
                    pt = ptr.next()
                    k.tr(pt.t[0:64, 0:128], selb.t[:, :], ident.t[:], [selb, ident], [pt])
                    k.cp("act", selT.t[:, :], pt.t[0:64, 0:128], [pt], [selT])
                    for kt in range(nkt):
                        ps = pmm.next()
                        k.mm(ps.t[:, 0:128], Ex.t[:, kt, :], selT.t[:, :], True, True, [Ex, selT], [ps])
                        if kt < 4 * j:
                            k.cp("act", ms.t[:, kt, :], ps.t[:, 0:128], [ps], [ms])
                        else:
                            k.tt("dve", ms.t[:, kt, :], ps.t[:, 0:128], CW.t[:, kt - 4 * j, :], ALU.mult, [ps, CW], [ms])
                    ktb_s, v1b_s = load_kv(kT_sel, v_sel, g, 0, nkt)
                    wlo = max(0, 4 * j - 4)
                    ktb_w, v1b_w = load_kv(kT_win, v_win, g, wlo, nkt)
                    for sub in range(2):
                        h0 = g * 8 + sub * 4
                        q_ap = qn.t[:, h0:h0 + 4, :].rearrange("p r q -> p (r q)")
                        attend(list(range(nkt)),
                               lambda kt, b=ktb_s: (b.t[:, kt * 128:(kt + 1) * 128], b),
                               lambda kt, b=v1b_s: (b.t[:, kt, 0:129], b), 129, q_ap, qn,
                               lambda kt: (ms.t[:, kt, :], ms))
                        rd = recip_den()
                        cf = rd_ring.next()
                        k.tt("dve", cf.t[:, :], rd.t[:, :], gn3[:, h0:h0 + 4, 1], ALU.mult, [rd, gn], [cf])
                        for r in range(4):
                            k.stt("dve", o32.t[:, h0 + r, :], pacc.t[:, r, 0:128], cf.t[:, r:r + 1], o32.t[:, h0 + r, :],
                                  ALU.mult, ALU.add, [pacc, cf, o32], [o32])
                        attend(list(range(wlo, nkt)),
                               lambda kt, b=ktb_w, wlo=wlo: (b.t[:, (kt - wlo) * 128:(kt - wlo + 1) * 128], b),
                               lambda kt, b=v1b_w, wlo=wlo: (b.t[:, kt - wlo, 0:129], b), 129, q_ap, qn,
                               lambda kt, j=j: (CW.t[:, 4 + (kt - 4 * j + 4), :], CW))
                        rd = recip_den()
                        cf = rd_ring.next()
                        k.tt("dve", cf.t[:, :], rd.t[:, :], gn3[:, h0:h0 + 4, 2], ALU.mult, [rd, gn], [cf])
                        for r in range(4):
                            k.stt("dve", o32.t[:, h0 + r, :], pacc.t[:, r, 0:128], cf.t[:, r:r + 1], o32.t[:, h0 + r, :],
                                  ALU.mult, ALU.add, [pacc, cf, o32], [o32])
                k.cp("pool", obf.t[:, 0:16, :], o32.t[:, :, :], [o32], [obf])
                if DEBUG and j == 0:
                    k.dma(dbg_obf.t, obf.t[:, :, :], [obf], [dbg_obf])
                for h8 in range(0, 32, 8):
                    pt = ptr.next()
                    for u in range(8):
                        k.tr(pt.t[:, u * 128:(u + 1) * 128], obf.t[:, h8 + u, :], ident.t[:], [obf, ident], [pt])
                    k.cp("act", oTsb.t[:, h8:h8 + 8, :], pt.t[:, :].rearrange("p (a q) -> p a q", a=8), [pt], [oTsb])
                k.dma(oT_s.t[j], oTsb.t[:, :, :], [oTsb], [oT_s])

        NPG = 64
        with P.scope():
            ptab_i = P.sb("ptab_i", [128, 4 * NPG], I32)
            k.dma(ptab_i.t[:, :], ptab.t.broadcast_to([128, 4 * NPG]), [ptab], [ptab_i])
            ptab_f = P.sb("ptab_f", [128, 4 * NPG], F32)
            k.cp("dve", ptab_f.t[:, :], ptab_i.t[:, :], [ptab_i], [ptab_f])
            pcol = P.sb("pcol", [128, 1], F32)
            P.op("pool", lambda e: e.iota(pcol.t[:, :], pattern=[[0, 1]], base=0, channel_multiplier=1,
                                          allow_small_or_imprecise_dtypes=True), (), [pcol])
            offs = P.sb("offs", [128, 4 * NPG], I32)
            k.ts("dve", offs.t[:, :], ptab_f.t[:, :], 128.0, pcol.t[:, 0:1], ALU.mult, ALU.add, [ptab_f, pcol], [offs])
            qn8 = P.sb("qn8", [128, 16, 128], BF16)
            qd8 = P.sb("qd8", [128, 16, 128], BF16)
            qi8 = P.sb("qi8", [128, 8, 128], BF16)
            k.dma(qn8.t[:], qT_n.t[8], [qT_n], [qn8])
            k.dma(qd8.t[:], qT_d.t[8], [qT_d], [qd8])
            k.dma(qi8.t[:], qiT.t[8], [qiT], [qi8])
            qns = P.sb("qns", [128, 4, 16], BF16)
            qds = P.sb("qds", [128, 4, 16], BF16)
            k.cp("dve", qns.t[:, :, :], qn8.t[:, :, 0:4].rearrange("p h s -> p s h"), [qn8], [qns])
            k.cp("dve", qds.t[:, :, :], qd8.t[:, :, 0:4].rearrange("p h s -> p s h"), [qd8], [qds])
            Qblk = P.sb("Qblk", [128, 4, 8, 2], BF16)
            k.memset("dve", Qblk.t[:], 0.0, [Qblk])
            for hf in range(2):
                k.cp("dve", Qblk.t[hf * 64:(hf + 1) * 64, :, :, hf],
                     qi8.t[hf * 64:(hf + 1) * 64, :, 0:4].rearrange("p a s -> p s a"), [qi8, Qblk], [Qblk])
            gateT = P.sb("gateT", [8, 4, 2, 3], F32)
            P.dma(lambda e: e.dma_start(out=gateT.t[:], in_=gn_s.t[8, 0:4, :].rearrange("s (g r b) -> r s g b", g=2, r=8),
                                        allow_slow_non_contiguous=True), [gn_s], [gateT])
            wiT = P.sb("wiT", [16, 4], F32)
            P.dma(lambda e: e.dma_start(out=wiT.t[:], in_=wi_s.t[8, 0:4, :].rearrange("s h -> h s"),
                                        allow_slow_non_contiguous=True), [wi_s], [wiT])
            W4 = P.sb("W4", [16, 4, 4], BF16)
            k.memset("dve", W4.t[:], 0.0, [W4])
            for s in range(4):
                k.cp("dve", W4.t[:, s, s:s + 1], wiT.t[:, s:s + 1], [wiT, W4], [W4])
            ones14 = P.sb("ones14", [4, 4], BF16)
            k.cp("dve", ones14.t[:, :], ident.t[0:4, 0:4], [ident], [ones14])
            one11 = P.sb("one11", [1, 1], BF16)
            k.memset("dve", one11.t[:], 1.0, [one11])
            e1col = P.sb("e1col", [128, 1], BF16)
            k.ts("dve", e1col.t[:, :], pcol.t[:, :], 0.0, None, ALU.is_equal, None, [pcol], [e1col])
            maskD = P.sb("maskD", [128, NPG, 4], BF16)
            maskS = P.sb("maskS", [128, NPG, 8], BF16)
            ockeep = P.sb("ockeep", [8, 8, 129], F32)
            selbT = P.sb("selbT", [128, 8], BF16)
            on_all = P.sb("on_all", [8, 4, 2, 128], F32)
            od_all = P.sb("od_all", [4, 4, 4, 128], F32)
            oT8 = P.sb("oT8", [128, 32, 128], BF16)
            k.memset("pool", oT8.t[:], 0.0, [oT8])
            pg_ring = Ring([P.sb("pg", [128, 1024], BF16) for i in range(8)])
            ktp_ring = Ring([P.sb("ktp", [128, 512], BF16) for i in range(4)])
            es_ring = Ring([P.sb("es", [128, 16], BF16) for i in range(4)])
            rd8_ring = Ring([P.sb("rd8", [8, 4], F32) for i in range(4)])
            v1n_ring = Ring([P.sb("v1n", [128, 4, 136], BF16) for i in range(3)])
            for v1_ in v1n_ring.bufs:
                k.memset("dve", v1_.t[:, :, 128:136], 1.0, [v1_])

            def gather(pool_buf, s, j, ncols):
                pg = pg_ring.next()
                P.dma(lambda e: e.indirect_dma_start(
                    out=pg.t[:, 0:ncols], out_offset=None, in_=pool_buf.t,
                    in_offset=bass.IndirectOffsetOnAxis(ap=offs.t[:, s * NPG + j:s * NPG + j + 1], axis=0)),
                    [pool_buf, offs], [pg], eng="pool")
                return pg

            def newrow_page(src_buf, s, ncols):
                pg = pg_ring.next()
                k.memset("dve", pg.t[:, 0:ncols], 0.0, [pg])
                k.dma(pg.t[0:1, 0:ncols], src_buf.t[s:s + 1, 0:ncols], [src_buf], [pg], eng="pool")
                return pg

            with P.scope():
                isc = P.sb("isc", [4, 8704], F32)
                k.memset("dve", isc.t[:, :], -1e30, [isc])
                selm4 = P.sb("selm4", [4, 8704], BF16)
                rls_ring = Ring([P.sb("rls", [16, 512], BF16) for i in range(3)])
                psi_ap = pacc.t[0:4, 0:2, :].rearrange("p a b -> p (a b)")
                for c in range(17):
                    ncol = 512 if c < 16 else 1
                    for s in range(4):
                        pt = ptr.next()
                        if c < 16:
                            for u in range(4):
                                pg = gather(c_didx, s, c * 4 + u, 64)
                                k.cp("dve", pg.t[:, 64:128], pg.t[:, 0:64], [pg], [pg])
                                k.tr(pt.t[:, u * 128:(u + 1) * 128], pg.t[:, 0:128], ident.t[:], [pg, ident], [pt])
                        else:
                            pg = newrow_page(snew_idx, s, 64)
                            k.cp("dve", pg.t[:, 64:128], pg.t[:, 0:64], [pg], [pg])
                            k.tr(pt.t[:, 0:128], pg.t[:, 0:128], ident.t[:], [pg, ident], [pt])
                        ktp = ktp_ring.next()
                        k.cp("act", ktp.t[:, 0:ncol], pt.t[:, 0:ncol], [pt], [ktp])
                        ps = pmm.next()
                        k.mm(ps.t[0:16, 0:ncol], Qblk.t[:, s, :, :].rearrange("p a b -> p (a b)"), ktp.t[:, 0:ncol],
                             True, True, [Qblk, ktp], [ps])
                        rl = rls_ring.next()
                        k.act(rl.t[:, 0:ncol], ps.t[0:16, 0:ncol], AF.Relu, [ps], [rl])
                        k.mm(psi_ap[:, 0:ncol], W4.t[:, s, :], rl.t[:, 0:ncol], s == 0, s == 3, [W4, rl], [pacc])
                    k.cp("act", isc.t[:, c * 512:c * 512 + ncol], psi_ap[:, 0:ncol], [pacc], [isc])
                m8s = Ring([P.sb("m8s", [4, 8], F32) for i in range(4)])
                for it in range(32):
                    m8 = m8s.next()
                    P.op("dve", lambda e, m8=m8: e.max(out=m8.t[:, :], in_=isc.t[:, :]), [isc], [m8])
                    P.op("dve", lambda e, m8=m8: e.match_replace(out=isc.t[:, :], in_to_replace=m8.t[:, :],
                                                                in_values=isc.t[:, :], imm_value=-3e38), [isc, m8], [isc])
                k.ts("dve", selm4.t[:, :], isc.t[:, :], -1e37, None, ALU.is_lt, None, [isc], [selm4])
                psm = pmm.next()
                for j in range(NPG):
                    k.mm(psm.t[:, j * 4:(j + 1) * 4], selm4.t[:, j * 128:(j + 1) * 128], ones14.t[:, :], True, True,
                         [selm4, ones14], [psm], sgc=True)
                k.cp("act", maskD.t[:, :, :], psm.t[:, 0:256].rearrange("p (j s) -> p j s", s=4), [psm], [maskD])

            with P.scope():
                w1sb = [P.sb("w1sbs", [128, 32, 256], BF16) for c in range(2)]
                w2sb = P.sb("w2sbs", [128, 2, 2, 128], BF16)
                posb = P.sb("posbs", [32, 2, 128], BF16)
                posT = P.sb("posTs", [128, 2, 32], BF16)
                bias_sb = P.sb("bias_sbs", [128, 4], F32)
                hidT = P.sb("hidTs", [128, 2, 512], BF16)
                kvTs = P.sb("kvTs", [128, 4, 8192], BF16)
                kcTs = P.sb("kcTs", [128, 512], BF16)
                vc1s = P.sb("vc1s", [128, 4, 136], BF16)
                ovls = P.sb("ovls", [128, 4, 132], BF16)
                ovl_vs = P.sb("ovl_vs", [128, 132], F32)
                ovl_as = P.sb("ovl_as", [128, 132], F32)
                iota_j2 = P.sb("iota_j2", [1, 132], F32)
                sws = [P.sb("sws%d" % i, [1, 132], F32) for i in range(4)]
                selrow = P.sb("selrow", [1, 132], BF16)
                impu_s = P.sb("impu_s", [8, 132], BF16)
                rdc = P.sb("rdc", [8, 1], BF16)
                k.memset("dve", hidT.t[:], 0.0, [hidT])
                k.memset("dve", kcTs.t[:], 0.0, [kcTs])
                k.memset("dve", vc1s.t[:], 0.0, [vc1s])
                P.op("pool", lambda e: e.iota(iota_j2.t[:, :], pattern=[[1, 132]], base=0, channel_multiplier=0,
                                              allow_small_or_imprecise_dtypes=True), (), [iota_j2])
                for c in range(2):
                    k.dma(w1sb[c].t[:], cmp_w1.t[c].rearrange("l d h -> d l h"), [cmp_w1], [w1sb[c]], eng="pool")
                k.dma(w2sb.t[:], cmp_w2.t.rearrange("c (hh p) d -> p c hh d", p=128), [cmp_w2], [w2sb], eng="pool")
                k.dma(posb.t[:], cmp_pos.t.rearrange("c l d -> l c d"), [cmp_pos], [posb], eng="pool")
                for c in range(2):
                    pt = ptr.next()
                    k.tr(pt.t[:, 0:32], posb.t[:, c, :], ident.t[0:32, 0:32], [posb, ident], [pt])
                    k.cp("act", posT.t[:, c, :], pt.t[:, 0:32], [pt], [posT])
                for c in range(2):
                    for hh in range(2):
                        ps = pmm.next()
                        for l in range(32):
                            k.mm(ps.t[:, 0:1], w1sb[c].t[:, l, hh * 128:(hh + 1) * 128], posT.t[:, c, l:l + 1],
                                 l == 0, l == 31, [w1sb[c], posT], [ps])
                        k.cp("act", bias_sb.t[:, c * 2 + hh:c * 2 + hh + 1], ps.t[:, 0:1], [ps], [bias_sb])
                for nt in range(4):
                    P.op("pool", lambda e, nt=nt: e.iota(ovl_vs.t[:], pattern=[[4, 132]], base=-128 * nt,
                                                         channel_multiplier=-1, allow_small_or_imprecise_dtypes=True),
                         (), [ovl_vs])
                    k.ts("dve", ovl_as.t[:], ovl_vs.t[:], -3.0, None, ALU.is_ge, None, [ovl_vs], [ovl_as])
                    k.stt("dve", ovls.t[:, nt, :], ovl_vs.t[:], 1.0, ovl_as.t[:], ALU.is_le, ALU.mult,
                          [ovl_vs, ovl_as], [ovls])
                padm = P.sb("padm", [128, 1], BF16)
                k.ts("dve", padm.t[:, :], pcol.t[:, :], 127.0, None, ALU.is_lt, None, [pcol], [padm])
                for s in range(4):
                    for j in range(NPG):
                        pg = gather(c_cmp, s, j, 512)
                        pt = ptr.next()
                        for f in range(4):
                            k.tr(pt.t[:, f * 128:(f + 1) * 128], pg.t[:, f * 128:(f + 1) * 128], ident.t[:], [pg, ident], [pt])
                        k.cp("act", kvTs.t[:, :, j * 128:(j + 1) * 128], pt.t[:, 0:512].rearrange("p (f t) -> p f t", f=4),
                             [pt], [kvTs])
                    for g in range(2):
                        for c in range(2):
                            for hh in range(2):
                                ps = pmm.next()
                                for l in range(32):
                                    k.mm(ps.t[:, 0:511], w1sb[c].t[:, l, hh * 128:(hh + 1) * 128],
                                         kvTs.t[:, g * 2 + c, l:l + 16 * 510 + 1:16], l == 0, l == 31, [w1sb[c], kvTs], [ps])
                                k.act(hidT.t[:, hh, 0:511], ps.t[:, 0:511], AF.Relu, [ps, bias_sb], [hidT],
                                      bias=bias_sb.t[:, c * 2 + hh:c * 2 + hh + 1])
                            if c == 0:
                                ps = pmm.next()
                                for hh in range(2):
                                    k.mm(ps.t[:, 0:511], w2sb.t[:, 0, hh, :], hidT.t[:, hh, 0:511], hh == 0, hh == 1,
                                         [w2sb, hidT], [ps])
                                k.cp("act", kcTs.t[:, 0:511], ps.t[:, 0:511], [ps], [kcTs])
                            else:
                                ps = pmm.next()
                                for nt in range(4):
                                    for hh in range(2):
                                        k.mm(ps.t[:, nt * 128:(nt + 1) * 128], hidT.t[:, hh, nt * 128:(nt + 1) * 128],
                                             w2sb.t[:, 1, hh, :], hh == 0 and nt == 0, hh == 1, [w2sb, hidT], [ps], sgc=True)
                                k.cp("act", vc1s.t[:, :, 0:128], ps.t[:, 0:512].rearrange("p (a d) -> p a d", a=4), [ps], [vc1s])
                                k.memset("dve", vc1s.t[:, :, 128:129], 1.0, [vc1s])
                        m = s * 2 + g
                        qap = qns.t[:, s, g * 8:(g + 1) * 8]
                        for nt in range(4):
                            ps = pmm.next()
                            k.mm(ps.t[:, 0:8], kcTs.t[:, nt * 128:(nt + 1) * 128], qap, True, True, [kcTs, qns], [ps])
                            eb = es_ring.next()
                            k.act(eb.t[:, 0:8], ps.t[:, 0:8], AF.Exp, [ps], [eb], scale=SCALE)
                            if nt == 3:
                                k.ts("dve", eb.t[:, 0:8], eb.t[:, 0:8], padm.t[:, 0:1], None, ALU.mult, None, [eb, padm], [eb])
                            k.mm(pacc.t[0:8, 0, 0:129], eb.t[:, 0:8], vc1s.t[:, nt, 0:129], nt == 0, nt == 3, [eb, vc1s], [pacc],
                                 sgc=True)
                            k.mm(pacc.t[0:8, 2, 0:129], eb.t[:, 0:8], ovls.t[:, nt, 0:129], nt == 0, nt == 3, [eb, ovls], [pacc],
                                 sgc=True)
                        k.cp("act", ockeep.t[:, m, :], pacc.t[0:8, 0, 0:129], [pacc], [ockeep])
                        rd = rd8_ring.next()
                        k.ts("dve", rd.t[:, 0:1], ockeep.t[:, m, 128:129], 1e-30, None, ALU.max, None, [ockeep], [rd])
                        P.op("dve", lambda e, rd=rd: e.reciprocal(out=rd.t[:, 0:1], in_=rd.t[:, 0:1]), [rd], [rd])
                        k.cp("dve", rdc.t[:, :], rd.t[:, 0:1], [rd], [rdc])
                        k.cp("act", impu_s.t[:, 0:129], pacc.t[0:8, 2, 0:129], [pacc], [impu_s])
                        ps = pmm.next()
                        k.mm(ps.t[0:1, 0:129], rdc.t[:, :], impu_s.t[:, 0:129], True, True, [rdc, impu_s], [ps])
                        k.ts("dve", sws[0].t[:, 0:129], iota_j2.t[:, 0:129], 0.0, None, ALU.is_equal, None, [iota_j2], [sws[0]])
                        k.ts("dve", sws[1].t[:, 0:129], iota_j2.t[:, 0:129], 127.0, None, ALU.is_ge, None, [iota_j2], [sws[1]])
                        k.tt("dve", sws[0].t[:, 0:129], sws[0].t[:, 0:129], sws[1].t[:, 0:129], ALU.add, [sws[0], sws[1]], [sws[0]])
                        k.stt("dve", sws[1].t[:, 0:129], sws[0].t[:, 0:129], 1e4, ps.t[0:1, 0:129], ALU.mult, ALU.add,
                              [sws[0], ps], [sws[1]])
                        ma8 = P.sb("ma8", [1, 8], F32); mb8 = P.sb("mb8", [1, 8], F32)
                        P.op("dve", lambda e, ma8=ma8: e.max(out=ma8.t[:, :], in_=sws[1].t[:, 0:129]), [sws[1]], [ma8])
                        P.op("dve", lambda e, ma8=ma8: e.match_replace(out=sws[2].t[:, 0:129], in_to_replace=ma8.t[:, :],
                                                                      in_values=sws[1].t[:, 0:129], imm_value=-3e38),
                             [sws[1], ma8], [sws[2]])
                        P.op("dve", lambda e, mb8=mb8: e.max(out=mb8.t[:, :], in_=sws[2].t[:, 0:129]), [sws[2]], [mb8])
                        k.ts("dve", selrow.t[:, 0:128], sws[1].t[:, 0:128], mb8.t[:, 7:8], None, ALU.is_ge, None,
                             [sws[1], mb8], [selrow])
                        ps2 = pmm.next()
                        k.mm(ps2.t[:, 0:1], selrow.t[0:1, 0:128], one11.t[:, :], True, True, [selrow, one11], [ps2])
                        k.cp("act", selbT.t[:, m:m + 1], ps2.t[:, 0:1], [ps2], [selbT])

            with P.scope():
                ExS = P.sb("ExS", [128, NPG, 128], BF16)
                with P.scope():
                    ExV = P.sb("ExV", [128, NPG, 128], F32)
                    P.op("pool", lambda e: e.iota(ExV.t[:].rearrange("p j (b i) -> p j b i", b=2),
                                                  pattern=[[-2, NPG], [-1, 2], [0, 64]], base=0, channel_multiplier=1,
                                                  allow_small_or_imprecise_dtypes=True), (), [ExV])
                    k.ts("dve", ExS.t[:], ExV.t[:], 0.0, None, ALU.is_equal, None, [ExV], [ExS])
                psm = pmm.next()
                for j in range(NPG):
                    k.mm(psm.t[:, j * 8:(j + 1) * 8], ExS.t[:, j, :], selbT.t[:, :], True, True, [ExS, selbT], [psm], sgc=True)
                k.cp("act", maskS.t[:, :, :], psm.t[:, 0:512].rearrange("p (j m) -> p j m", m=8), [psm], [maskS])

            def finish_branch(nrow, ngrp, dst_of_g, coef_of_g, first):
                rd = rd8_ring.next()
                k.ts("dve", rd.t[0:nrow, 0:ngrp], pacc.t[0:nrow, 0:ngrp, 128], 1e-30, None, ALU.max, None, [pacc], [rd])
                P.op("dve", lambda e: e.reciprocal(out=rd.t[0:nrow, 0:ngrp], in_=rd.t[0:nrow, 0:ngrp]), [rd], [rd])
                for g in range(ngrp):
                    dap, dbuf = dst_of_g(g)
                    cf = coef_of_g(g)
                    if cf is not None:
                        cap, cbuf = cf
                        k.tt("dve", rd.t[0:nrow, g:g + 1], rd.t[0:nrow, g:g + 1], cap, ALU.mult, [rd, cbuf], [rd])
                    if first:
                        k.ts("dve", dap, pacc.t[0:nrow, g, 0:128], rd.t[0:nrow, g:g + 1], None, ALU.mult, None, [pacc, rd], [dbuf])
                    else:
                        k.stt("dve", dap, pacc.t[0:nrow, g, 0:128], rd.t[0:nrow, g:g + 1], dap, ALU.mult, ALU.add,
                              [pacc, rd, dbuf], [dbuf])

            for s in range(4):
                for g in range(2):
                    m = s * 2 + g
                    rd = rd8_ring.next()
                    k.ts("dve", rd.t[:, 0:1], ockeep.t[:, m, 128:129], 1e-30, None, ALU.max, None, [ockeep], [rd])
                    P.op("dve", lambda e, rd=rd: e.reciprocal(out=rd.t[:, 0:1], in_=rd.t[:, 0:1]), [rd], [rd])
                    k.tt("dve", rd.t[:, 0:1], rd.t[:, 0:1], gateT.t[:, s, g, 0:1], ALU.mult, [rd, gateT], [rd])
                    k.ts("dve", on_all.t[:, s, g, :], ockeep.t[:, m, 0:128], rd.t[:, 0:1], None, ALU.mult, None,
                         [ockeep, rd], [on_all])
                def run_pages(pool_buf, newsrc, ncols, **kw):
                    n = NPG + 1
                    st_ = {}

                    def get_page(idx):
                        if idx < NPG:
                            return gather(pool_buf, s, idx, ncols)
                        return newrow_page(newsrc, s, ncols)
                    st_[0] = attend_stage1(get_page(0), 0, **kw)
                    for idx in range(n):
                        if idx + 1 < n:
                            st_[idx + 1] = attend_stage1(get_page(idx + 1), idx + 1, **kw)
                        attend_stage2(st_.pop(idx), idx == 0, idx == n - 1, kw["nrow"], kw["ngrp"])

                def attend_stage1(pg, pj, nrow, ngrp, q_of_g, kcol_of_g, vcol_of_g, mask_of_page):
                    pt = ptr.next()
                    for g in range(ngrp):
                        c0 = kcol_of_g(g)
                        k.tr(pt.t[:, g * 128:(g + 1) * 128], pg.t[:, c0:c0 + 128], ident.t[:], [pg, ident], [pt])
                    ktp = ktp_ring.next()
                    k.cp("act", ktp.t[:, 0:ngrp * 128], pt.t[:, 0:ngrp * 128], [pt], [ktp])
                    ps = pmm.next()
                    for g in range(ngrp):
                        qap, qb = q_of_g(g)
                        k.mm(ps.t[:, g * nrow:(g + 1) * nrow], ktp.t[:, g * 128:(g + 1) * 128], qap, True, True,
                             [ktp, qb], [ps], sgc=True)
                    eb = es_ring.next()
                    k.act(eb.t[:, 0:ngrp * nrow], ps.t[:, 0:ngrp * nrow], AF.Exp, [ps], [eb], scale=SCALE)
                    mk = mask_of_page(pj)
                    if mk is not None:
                        map_, mbuf = mk
                        ev = eb.t[:, 0:ngrp * nrow].rearrange("p (g r) -> p g r", g=ngrp)
                        k.tt("dve", ev, ev, map_, ALU.mult, [eb, mbuf], [eb])
                    v1 = v1n_ring.next()
                    for g in range(ngrp):
                        c0 = vcol_of_g(g)
                        k.cp("dve" if g % 2 == 0 else "act", v1.t[:, g, 0:128], pg.t[:, c0:c0 + 128], [pg], [v1])
                    return eb, v1

                def attend_stage2(stv, first, last, nrow, ngrp):
                    eb, v1 = stv
                    for g in range(ngrp):
                        k.mm(pacc.t[0:nrow, g, 0:129], eb.t[:, g * nrow:(g + 1) * nrow], v1.t[:, g, 0:129],
                             first and g % 2 == 0, last, [eb, v1], [pacc], sgc=True)

                def mask_sel(pj, s=s):
                    if pj < NPG:
                        return (maskS.t[:, pj, s * 2:s * 2 + 2].unsqueeze(2).broadcast_to([128, 2, 8]), maskS)
                    return (e1col.t[:, 0:1].unsqueeze(1).broadcast_to([128, 2, 8]), e1col)
                run_pages(c_sel, snew_sel, 512, nrow=8, ngrp=2,
                          q_of_g=lambda g, s=s: (qns.t[:, s, g * 8:(g + 1) * 8], qns),
                          kcol_of_g=lambda g: g * 256, vcol_of_g=lambda g: g * 256 + 128, mask_of_page=mask_sel)
                finish_branch(8, 2, lambda g, s=s: (on_all.t[:, s, g, :], on_all),
                              lambda g, s=s: (gateT.t[:, s, g, 1:2], gateT), False)
                for wi_ in range(4):
                    pg = pg_ring.next()
                    k.dma(pg.t[:, 0:512], o_win_s.t[s, wi_ * 128:(wi_ + 1) * 128, :], [o_win_s], [pg], eng="pool")
                    stv = attend_stage1(pg, wi_, nrow=8, ngrp=2,
                                        q_of_g=lambda g, s=s: (qns.t[:, s, g * 8:(g + 1) * 8], qns),
                                        kcol_of_g=lambda g: g * 256, vcol_of_g=lambda g: g * 256 + 128,
                                        mask_of_page=lambda pj: None)
                    attend_stage2(stv, wi_ == 0, wi_ == 3, 8, 2)
                finish_branch(8, 2, lambda g, s=s: (on_all.t[:, s, g, :], on_all),
                              lambda g, s=s: (gateT.t[:, s, g, 2:3], gateT), False)

                def mask_dsa(pj, s=s):
                    if pj < NPG:
                        return (maskD.t[:, pj, s:s + 1].unsqueeze(1).broadcast_to([128, 4, 4]), maskD)
                    return (e1col.t[:, 0:1].unsqueeze(1).broadcast_to([128, 4, 4]), e1col)
                run_pages(c_dkv, snew_dsa, 1024, nrow=4, ngrp=4,
                          q_of_g=lambda g, s=s: (qds.t[:, s, g * 4:(g + 1) * 4], qds),
                          kcol_of_g=lambda g: g * 256, vcol_of_g=lambda g: g * 256 + 128, mask_of_page=mask_dsa)
                finish_branch(4, 4, lambda g, s=s: (od_all.t[:, s, g, :], od_all), lambda g: None, True)

            onb = P.sb("onb", [8, 4, 2, 128], BF16)
            odb = P.sb("odb", [4, 4, 4, 128], BF16)
            k.cp("dve", onb.t[:], on_all.t[:], [on_all], [onb])
            k.cp("dve", odb.t[:], od_all.t[:], [od_all], [odb])
            for s in range(4):
                pt = ptr.next()
                for g in range(2):
                    k.tr(pt.t[:, g * 8:(g + 1) * 8], onb.t[:, s, g, :], ident.t[0:8, 0:8], [onb, ident], [pt])
                for g in range(4):
                    k.tr(pt.t[:, 16 + g * 4:16 + (g + 1) * 4], odb.t[:, s, g, :], ident.t[0:4, 0:4], [odb, ident], [pt])
                k.cp("act", oT8.t[:, :, s], pt.t[:, 0:32], [pt], [oT8])
            k.dma(oT_s.t[8], oT8.t[:, :, :], [oT8], [oT_s])

        NTOK = NSLOT * 128

        def gen_norm(rows_ap, src_buf, slot, hTt, hTd, gainT, xt_ring, xn, ssq):
            xt = xt_ring.next(); sq = ssq.next()
            k.dma(xt.t[:], rows_ap, [src_buf], [xt])
            k.act(xn.t[:], xt.t[:], AF.Square, [xt], [xn, sq], accum_out=sq.t[:])
            k.ts("dve", sq.t[:], sq.t[:], 1.0 / D, 1e-6, ALU.mult, ALU.add, [sq], [sq])
            k.act(sq.t[:], sq.t[:], AF.Sqrt, [sq], [sq])
            P.op("dve", lambda e: e.reciprocal(out=sq.t[:], in_=sq.t[:]), [sq], [sq])
            k.ts("dve", xn.t[:], xt.t[:], sq.t[:, 0:1], None, ALU.mult, None, [xt, sq], [xn])
            for kc4 in range(8):
                pt = ptr.next()
                for u in range(4):
                    kc = kc4 * 4 + u
                    k.tr(pt.t[:, u * 128:(u + 1) * 128], xn.t[:, kc * 128:(kc + 1) * 128], ident.t[:], [xn, ident], [pt])
                for u in range(4):
                    kc = kc4 * 4 + u
                    k.act(hTt.t[:, kc, slot * 128:(slot + 1) * 128], pt.t[:, u * 128:(u + 1) * 128], AF.Copy,
                          [pt, gainT], [hTd[slot]], scale=gainT.t[:, kc:kc + 1])

        def gen_load_w(ring, wbuf, r0, nkc, c0, ncols):
            w = ring.next()
            if not hasattr(w, "hb"):
                w.hb = Buf(w.t)
            src = wbuf.t[r0:r0 + nkc * 128, c0:c0 + ncols].rearrange("(kc p) c -> p kc c", p=128)
            h = max(1, nkc // 2)
            w.half = h
            for a in range(0, nkc, h):
                k.dma(w.t[:, a:a + h, 0:ncols], src[:, a:a + h, :], [wbuf], [w if a == 0 else w.hb], eng="pool")
            return w

        def wd_(w, kc):
            return w if kc < w.half else w.hb

        with P.scope():
            mixT = P.sb("mixT", [128, 32, NTOK], BF16)
            mix_d = [Buf(None) for _ in range(3)]
            with P.scope():
                oT_all = P.sb("oT_all", [128, 32, NTOK], BF16)
                for slot in range(NSLOT):
                    k.dma(oT_all.t[:, :, slot * 128:(slot + 1) * 128], oT_s.t[slot], [oT_s], [oT_all])
                wb_ring = Ring([P.sb("wb", [128, 32, 128], BF16) for i in range(2)])
                gm_ring = Ring([P.sb("gmb", [128, 2, NTOK], BF16) for i in range(2)])
                t_ring = Ring([P.sb("mt", [128, 384], F32) for i in range(4)])

                def load_c1(c):
                    wb = wb_ring.next(); gb = gm_ring.next()
                    k.dma(wb.t[:, 0:16, :], w_bn.t[:, c * 128:(c + 1) * 128].rearrange("(kc p) c -> p kc c", p=128),
                          [w_bn], [wb], eng="pool")
                    k.dma(wb.t[:, 16:32, :], w_bd.t[:, c * 128:(c + 1) * 128].rearrange("(kc p) c -> p kc c", p=128),
                          [w_bd], [wb], eng="pool")
                    k.dma(gb.t[:, 0, :], gmT.t[c], [gmT], [gb])
                    k.dma(gb.t[:, 1, :], gmT.t[32 + c], [gmT], [gb])
                    return wb, gb
                nxt = load_c1(0)
                for c in range(32):
                    wb, gb = nxt
                    if c + 1 < 32:
                        nxt = load_c1(c + 1)
                    for tc in range(3):
                        tsl = slice(tc * 384, (tc + 1) * 384)
                        psn = pmm.next(); psd = pmm.next()
                        for kc in range(16):
                            k.mm(psn.t[:, 0:384], wb.t[:, kc, :], oT_all.t[:, kc, tsl], kc == 0, kc == 15, [wb, oT_all], [psn])
                        for kc in range(16, 32):
                            k.mm(psd.t[:, 0:384], wb.t[:, kc, :], oT_all.t[:, kc, tsl], kc == 16, kc == 31, [wb, oT_all], [psd])
                        t1 = t_ring.next(); t2 = t_ring.next()
                        k.tt("dve", t1.t[:, :], psn.t[:, 0:384], gb.t[:, 0, tsl], ALU.mult, [psn, gb], [t1])
                        k.tt("dve", t2.t[:, :], psd.t[:, 0:384], gb.t[:, 1, tsl], ALU.mult, [psd, gb], [t2])
                        k.tt("pool", mixT.t[:, c, tsl], t1.t[:, :], t2.t[:, :], ALU.add, [t1, t2], [mix_d[tc]])
            with P.scope():
                w_ring = Ring([P.sb("wo", [128, 32, 512], BF16) for i in range(2)])
                xc_ring = Ring([P.sb("xc", [128, 512], F32) for i in range(3)])
                nxt = gen_load_w(w_ring, w_out, 0, 32, 0, 512)
                for n in range(8):
                    w = nxt
                    if n + 1 < 8:
                        nxt = gen_load_w(w_ring, w_out, 0, 32, (n + 1) * 512, 512)
                    for slot in range(NSLOT):
                        ps = pmm.next()
                        for kc in range(32):
                            k.mm(ps.t[:, 0:512], mixT.t[:, kc, slot * 128:(slot + 1) * 128], w.t[:, kc, :], kc == 0, kc == 31,
                                 [mix_d[slot // 3], wd_(w, kc)], [ps])
                        xc = xc_ring.next()
                        k.dma(xc.t[:, :], xq.t[slot * 128:(slot + 1) * 128, n * 512:(n + 1) * 512], [xq], [xc])
                        k.tt("dve", xc.t[:, :], xc.t[:, :], ps.t[:, 0:512], ALU.add, [xc, ps], [xc])
                        k.dma(x1s.t[slot * 128:(slot + 1) * 128, n * 512:(n + 1) * 512], xc.t[:, :], [xc], [x1s])

        with P.scope():
            hT2 = P.sb("hT2", [128, 32, NTOK], BF16)
            hT2_d = [Buf(None) for _ in range(NSLOT)]
            gT2 = P.sb("gT2", [128, 32], F32)
            k.dma(gT2.t[:], norm_mlp.t, [norm_mlp], [gT2])
            with P.scope():
                xt_ring = Ring([P.sb("xt", [128, D], F32) for i in range(2)])
                xn = P.sb("xn", [128, D], BF16)
                ssq = Ring([P.sb("ssq", [128, 1], F32) for i in range(2)])
                for slot in range(NSLOT):
                    gen_norm(x1s.t[slot * 128:(slot + 1) * 128, :], x1s, slot, hT2, hT2_d, gT2, xt_ring, xn, ssq)
            with P.scope():
                w_ring = Ring([P.sb("wu", [128, 32, 512], BF16) for i in range(2)])
                r_ring = Ring([P.sb("rr", [128, 384], F32) for i in range(3)])
                u_ring = Ring([P.sb("ub", [128, 384], BF16) for i in range(3)])
                nxt = gen_load_w(w_ring, w_up, 0, 32, 0, 512)
                ecnt = 0
                for fg in range(32):
                    w = nxt
                    if fg + 1 < 32:
                        nxt = gen_load_w(w_ring, w_up, 0, 32, (fg + 1) * 512, 512)
                    for cc in range(4):
                        f = fg * 4 + cc
                        for tc in range(3):
                            ps = pmm.next()
                            for kc in range(32):
                                k.mm(ps.t[:, 0:384], w.t[:, kc, cc * 128:(cc + 1) * 128], hT2.t[:, kc, tc * 384:(tc + 1) * 384],
                                     kc == 0, kc == 31, [wd_(w, kc)] + hT2_d[tc * 3:tc * 3 + 3], [ps])
                            rr = r_ring.next(); ub = u_ring.next()
                            k.act(rr.t[:, :], ps.t[:, 0:384], AF.Relu, [ps], [rr])
                            k.tt("dve" if ecnt % 2 == 0 else "pool", ub.t[:, :], rr.t[:, :], rr.t[:, :], ALU.mult, [rr], [ub])
                            ecnt += 1
                            k.dma(uT_s.t[f, :, tc * 384:(tc + 1) * 384], ub.t[:, :], [ub], [uT_s])
        with P.scope():
            wd_ring = Ring([P.sb("wd", [128, 16, 256], BF16) for i in range(2)])
            ub_ring = Ring([P.sb("ubk", [128, 16, NTOK], BF16) for i in range(2)])
            xc_ring = Ring([P.sb("xc2", [128, 256], F32) for i in range(3)])

            def acc_region(slot):
                if slot < 8:
                    b_ = pmm.bufs[slot // 2]
                    return b_, b_.t[:, (slot % 2) * 256:(slot % 2) * 256 + 256]
                return pacc, pacc.t[:, 0, :]

            def load_c2(n, kq):
                wd = gen_load_w(wd_ring, w_down, kq * 2048, 16, n * 256, 256)
                ubk = ub_ring.next()
                k.dma(ubk.t[:, :, :], uT_s.t[kq * 16:(kq + 1) * 16].rearrange("f p t -> p f t"), [uT_s], [ubk])
                return wd, ubk
            seq = [(n, kq) for n in range(16) for kq in range(8)]
            nxt = load_c2(*seq[0])
            for si, (n, kq) in enumerate(seq):
                wd, ubk = nxt
                if si + 1 < len(seq):
                    nxt = load_c2(*seq[si + 1])
                for slot in range(NSLOT):
                    rb, rap = acc_region(slot)
                    for fc in range(16):
                        first = (kq == 0 and fc == 0)
                        k.mm(rap, ubk.t[:, fc, slot * 128:(slot + 1) * 128], wd.t[:, fc, :],
                             first and (slot % 2 == 0), kq == 7 and fc == 15, [ubk, wd_(wd, fc)], [rb], sgc=True)
                if kq == 7:
                    for slot in range(NSLOT):
                        rb, rap = acc_region(slot)
                        xc = xc_ring.next()
                        k.dma(xc.t[:, :], x1s.t[slot * 128:(slot + 1) * 128, n * 256:(n + 1) * 256], [x1s], [xc])
                        k.tt("dve", xc.t[:, :], xc.t[:, :], rap, ALU.add, [xc, rb], [xc])
                        k.dma(x2s.t[slot * 128:(slot + 1) * 128, n * 256:(n + 1) * 256], xc.t[:, :], [xc], [x2s])

        with P.scope():
            hT3 = P.sb("hT3", [128, 32, NTOK], BF16)
            hT3_d = [Buf(None) for _ in range(NSLOT)]
            pT = P.sb("pT", [128, 2, NTOK], BF16)
            gT3 = P.sb("gT3", [128, 32], F32)
            k.dma(gT3.t[:], norm_ple.t, [norm_ple], [gT3])
            ssx = P.sb("ssx", [128, NSLOT, 8], F32)
            with P.scope():
                xt_ring = Ring([P.sb("xt", [128, D], F32) for i in range(2)])
                xn = P.sb("xn", [128, D], BF16)
                ssq = Ring([P.sb("ssq", [128, 1], F32) for i in range(2)])
                pin = Ring([P.sb("pin", [128, 256], BF16) for i in range(2)])
                for slot in range(NSLOT):
                    gen_norm(x2s.t[slot * 128:(slot + 1) * 128, :], x2s, slot, hT3, hT3_d, gT3, xt_ring, xn, ssq)
                    pb = pin.next()
                    k.dma(pb.t[:, :], pq.t[slot * 128:(slot + 1) * 128, :], [pq], [pb], eng="pool")
                    pt = ptr.next()
                    for u in range(2):
                        k.tr(pt.t[:, u * 128:(u + 1) * 128], pb.t[:, u * 128:(u + 1) * 128], ident.t[:], [pb, ident], [pt])
                    k.cp("act", pT.t[:, :, slot * 128:(slot + 1) * 128], pt.t[:, 0:256].rearrange("p (a q) -> p a q", a=2),
                         [pt], [pT])
            with P.scope():
                w_ring = Ring([P.sb("wg", [128, 32, 512], BF16) for i in range(2)])
                wp_ring = Ring([P.sb("wp", [128, 2, 512], BF16) for i in range(2)])
                xc_ring = Ring([P.sb("xc3", [128, 512], F32) for i in range(3)])
                g_ring = Ring([P.sb("gt", [128, 512], F32) for i in range(3)])
                junk = P.sb("junk", [128, 512], BF16)

                def load_c3(n):
                    return (gen_load_w(w_ring, w_pg, 0, 32, n * 512, 512), gen_load_w(wp_ring, w_ple, 0, 2, n * 512, 512))
                nxt = load_c3(0)
                for n in range(8):
                    w, wp = nxt
                    if n + 1 < 8:
                        nxt = load_c3(n + 1)
                    for slot in range(NSLOT):
                        psg = pmm.next(); psp = pmm.next()
                        for kc in range(32):
                            k.mm(psg.t[:, 0:512], hT3.t[:, kc, slot * 128:(slot + 1) * 128], w.t[:, kc, :], kc == 0, kc == 31,
                                 [hT3_d[slot], wd_(w, kc)], [psg])
                        for k2 in range(2):
                            k.mm(psp.t[:, 0:512], pT.t[:, k2, slot * 128:(slot + 1) * 128], wp.t[:, k2, :], k2 == 0, k2 == 1,
                                 [pT, wd_(wp, k2)], [psp])
                        gt = g_ring.next(); xc = xc_ring.next()
                        k.act(gt.t[:, :], psg.t[:, 0:512], AF.Sigmoid, [psg], [gt])
                        k.tt("dve", gt.t[:, :], gt.t[:, :], psp.t[:, 0:512], ALU.mult, [gt, psp], [gt])
                        k.dma(xc.t[:, :], x2s.t[slot * 128:(slot + 1) * 128, n * 512:(n + 1) * 512], [x2s], [xc])
                        k.tt("pool", xc.t[:, :], xc.t[:, :], gt.t[:, :], ALU.add, [xc, gt], [xc])
                        k.act(junk.t[:, :], xc.t[:, :], AF.Square, [xc], [junk, ssx], accum_out=ssx.t[:, slot, n:n + 1])
                        k.dma(x3s.t[slot * 128:(slot + 1) * 128, n * 512:(n + 1) * 512], xc.t[:, :], [xc], [x3s])
            with P.scope():
                gfin = P.sb("gfin", [128, D], F32)
                k.dma(gfin.t[:, :], norm_final.t.broadcast_to([128, D]), [norm_final], [gfin])
                xt_ring = Ring([P.sb("xt", [128, D], F32) for i in range(2)])
                rs = P.sb("rs", [128, NSLOT], F32)
                P.op("dve", lambda e: e.tensor_reduce(out=rs.t[:, :], in_=ssx.t[:, :, :], axis=AX.X, op=ALU.add), [ssx], [rs])
                k.ts("dve", rs.t[:, :], rs.t[:, :], 1.0 / D, 1e-6, ALU.mult, ALU.add, [rs], [rs])
                k.act(rs.t[:, :], rs.t[:, :], AF.Sqrt, [rs], [rs])
                P.op("dve", lambda e: e.reciprocal(out=rs.t[:, :], in_=rs.t[:, :]), [rs], [rs])
                for slot in range(NSLOT):
                    xt = xt_ring.next()
                    k.dma(xt.t[:, :], x3s.t[slot * 128:(slot + 1) * 128, :], [x3s], [xt])
                    k.stt("dve", xt.t[:, :], xt.t[:, :], rs.t[:, slot:slot + 1], gfin.t[:, :], ALU.mult, ALU.mult, [xt, rs, gfin], [xt])
                    k.dma(o_y.t[slot * 128:(slot + 1) * 128, :], xt.t[:, :], [xt], [o_y])

        P.finish(outs)
    return nc


_NC_CACHE = {}


def kernel(**inputs):
    x_prompt = np.asarray(inputs["x_prompt"], dtype=np.float32)
    x_sample = np.asarray(inputs["x_sample"], dtype=np.float32)
    w_in = np.ascontiguousarray(np.asarray(inputs["w_in"], dtype=np.float32)[0])
    norm_mix = np.ascontiguousarray(np.asarray(inputs["norm_mix"], dtype=np.float32)[0].reshape(32, 128).T)
    state_win = np.asarray(inputs["state_nsa_win"], dtype=np.float32)
    cmp_pos = np.ascontiguousarray(np.asarray(inputs["cmp_pos"], dtype=np.float32)[0])
    cmp_w1 = np.ascontiguousarray(np.asarray(inputs["cmp_w1"], dtype=np.float32)[0])
    cmp_w2 = np.ascontiguousarray(np.asarray(inputs["cmp_w2"], dtype=np.float32)[0])
    g1 = lambda n: np.ascontiguousarray(np.asarray(inputs[n], dtype=np.float32)[0])
    gTl = lambda n: np.ascontiguousarray(np.asarray(inputs[n], dtype=np.float32)[0].reshape(32, 128).T)
    p_prompt = np.asarray(inputs["p_prompt"], dtype=np.float32)[0]
    p_sample = np.asarray(inputs["p_sample"], dtype=np.float32)[0]
    shared = {"w_bn": g1("w_branch_nsa"), "w_bd": g1("w_branch_dsa"), "w_out": g1("w_out"), "norm_mlp": gTl("norm_mlp"),
              "w_up": g1("w_up"), "w_down": g1("w_down"), "norm_ple": gTl("norm_ple"), "w_pg": g1("w_ple_gate"),
              "w_ple": g1("w_ple"), "norm_final": np.ascontiguousarray(np.asarray(inputs["norm_final"], dtype=np.float32)[None, :])}
    pools = {"c_cmp": np.asarray(inputs["cache_nsa_cmp"], dtype=np.float32)[0].reshape(2560 * 128, 512),
             "c_sel": np.asarray(inputs["cache_nsa_sel"], dtype=np.float32)[0].reshape(2560 * 128, 512),
             "c_dkv": np.asarray(inputs["cache_dsa_kv"], dtype=np.float32)[0].reshape(2560 * 128, 1024),
             "c_didx": np.asarray(inputs["cache_dsa_idx"], dtype=np.float32)[0].reshape(2560 * 128, 64)}
    page_table = np.asarray(inputs["page_table"]).astype(np.int32)
    if "nc" not in _NC_CACHE:
        _NC_CACHE["nc"] = build_program()
    nc = _NC_CACHE["nc"]

    invf = np.concatenate([
        (10000.0 ** (-np.arange(64, dtype=np.float32) / np.float32(64))).astype(np.float32),
        (10000.0 ** (-np.arange(32, dtype=np.float32) / np.float32(32))).astype(np.float32)])[None, :]
    in_maps = []
    for c in range(8):
        b, i = c // 4, c % 4
        xq = np.zeros((NSLOT * 128, D), np.float32)
        pos = np.zeros((128, 64), np.float32)
        cmk = np.zeros((128, 32), np.float32)
        for t in range(32):
            pos[:, t] = 128 * t + np.arange(128)
        for j in range(8):
            t = 4 * j + i
            xq[j * 128:(j + 1) * 128] = x_prompt[b, t * 128:(t + 1) * 128]
            pos[:, 32 + j] = 128 * t + np.arange(128)
            pos[:, 41 + j] = (128 * t + np.arange(128)) // 64
            cmk[:, j] = 128 * t
        xq[1024:1028] = x_sample[4 * c:4 * c + 4, 0]
        pqa = np.zeros((NSLOT * 128, 256), np.float32)
        for j in range(8):
            t = 4 * j + i
            pqa[j * 128:(j + 1) * 128] = p_prompt[b, t * 128:(t + 1) * 128]
        pqa[1024:1028] = p_sample[4 * c:4 * c + 4, 0]
        pos[:, 40] = 8192
        for rel in range(4):
            dl = i - rel
            a, bb = (0.0, 1.0) if dl > 0 else ((1.0, 0.0) if dl == 0 else (0.0, 0.0))
            cmk[:, 8 + 2 * rel] = a
            cmk[:, 9 + 2 * rel] = bb
        for m in range(8):
            dl = i - (m - 4)
            if dl == 0:
                a, bb = 1.0, 0.0
            elif 1 <= dl <= 3:
                a, bb = 0.0, 1.0
            elif dl == 4:
                a, bb = -1.0, 1.0
            else:
                a, bb = 0.0, 0.0
            cmk[:, 16 + 2 * m] = a
            cmk[:, 17 + 2 * m] = bb
        in_maps.append({"xb": np.ascontiguousarray(x_prompt[b]), "xq": xq, "pos": pos, "cmk": cmk,
                        "invf": invf.astype(np.float32), "w_in": w_in, "norm_mix": norm_mix,
                        "cmp_pos": cmp_pos, "cmp_w1": cmp_w1, "cmp_w2": cmp_w2, "pq": pqa,
                        "ptab": np.ascontiguousarray(page_table[4 * c:4 * c + 4].reshape(1, 256)),
                        "state_win": np.ascontiguousarray(state_win[0, 4 * c:4 * c + 4].reshape(4, 512, 512)),
                        **pools, **shared})
    res = run_bass_kernel_spmd(nc, in_maps, core_ids=list(range(8)))
    R = res.results
    _NC_CACHE["R"] = R

    def bcat(name, shape_tail):
        return np.stack([R[0][name], R[4][name]], 0).reshape((1, 2) + shape_tail)

    def scat(name, shape_tail):
        return np.concatenate([R[c][name] for c in range(8)], 0).reshape((1, 32, 1) + shape_tail)

    y_prompt = np.zeros((2, SEQ, D), np.float32)
    y_sample = np.zeros((32, 1, D), np.float32)
    for c in range(8):
        b, i = c // 4, c % 4
        yo = R[c]["y_own"]
        for j in range(8):
            t = 4 * j + i
            y_prompt[b, t * 128:(t + 1) * 128] = yo[j * 128:(j + 1) * 128]
        y_sample[4 * c:4 * c + 4, 0] = yo[1024:1028]
    cmp_p = bcat("cmp_p", (SEQ, 2, 2, 128))
    sel_p = bcat("sel_p", (SEQ, 2, 2, 128))
    dkv_p = bcat("dkv_p", (SEQ, 4, 2, 128))
    didx_p = bcat("didx_p", (SEQ, 64))
    win_p = bcat("win_p", (512, 2, 2, 128))
    cmp_s = scat("cmp_s", (2, 2, 128))
    sel_s = scat("sel_s", (2, 2, 128))
    dkv_s = scat("dkv_s", (4, 2, 128))
    didx_s = scat("didx_s", (64,))
    win_s = np.concatenate([R[c]["win_s"] for c in range(8)], 0).reshape(1, 32, 512, 2, 2, 128)
    return (y_prompt, y_sample, cmp_p, cmp_s, sel_p, sel_s, dkv_p, dkv_s, didx_p, didx_s, win_p, win_s)
```

```python
import math
import numpy as np
from contextlib import ExitStack
import concourse.bass as bass
import concourse.mybir as mybir
from concourse.bass_utils import run_bass_kernel_spmd

F32 = mybir.dt.float32
BF16 = mybir.dt.bfloat16
I32 = mybir.dt.int32
ALU = mybir.AluOpType
AF = mybir.ActivationFunctionType
AX = mybir.AxisListType

ENGS = ("pe", "act", "dve", "pool", "sp")
N_DMA_SEMS = 24
PI = math.pi

D = 4096
SEQ = 4096
NT = 32
PROJ_W = 16000
C_QN, C_KVC, C_KVS, C_KVW, C_GN, C_QD, C_KVD, C_QI, C_KI, C_WI, C_GM = (
    0, 2048, 2560, 3072, 3584, 3632, 5680, 6704, 7728, 7792, 7808)


class Dep:
    __slots__ = ("w", "r")

    def __init__(self):
        self.w = None
        self.r = {}


class Buf:
    def __init__(self, t, tracked=True):
        self.t = t
        self.d = Dep()
        self.tracked = tracked


class Ring:
    def __init__(self, bufs):
        self.bufs = bufs
        self.i = 0

    def next(self):
        b = self.bufs[self.i]
        self.i = (self.i + 1) % len(self.bufs)
        return b


class Prog:
    def __init__(self, nc, es):
        self.nc = nc
        self.es = es
        self.aes = es
        self.q = {e: [] for e in ENGS}
        self.cnt = {e: 0 for e in ENGS}
        self.seen = {e: {} for e in ENGS}
        self.sems = {}
        self.ekey = {}
        for e in ENGS:
            self.ekey[e] = (e, 0)
            self.sems[(e, 0)] = es.enter_context(nc.semaphore("s_" + e))
        self.dsem_cnt = [0] * N_DMA_SEMS
        for j in range(N_DMA_SEMS):
            self.sems[("d", j)] = es.enter_context(nc.semaphore("sd%d" % j))
        self.dnext = 0

    def sb(self, name, shape, dt):
        self.nalloc = getattr(self, "nalloc", 0) + 1
        return Buf(self.aes.enter_context(self.nc.sbuf_tensor("%s_%d" % (name, self.nalloc), list(shape), dt)))

    def barrier(self):
        snap = [(self.ekey[e], self.cnt[e]) for e in ENGS] + [(("d", j), self.dsem_cnt[j]) for j in range(N_DMA_SEMS)]
        for eng in ENGS:
            need = []
            for kk, v in snap:
                if v > self.seen[eng].get(kk, 0):
                    self.seen[eng][kk] = v
                    need.append((kk, v))
            self.q[eng].append((need, None, None))
        for e in ENGS:
            if self.cnt[e] > 12000:
                ep = self.ekey[e][1] + 1
                self.ekey[e] = (e, ep)
                self.sems[(e, ep)] = self.es.enter_context(self.nc.semaphore("s_%s_%d" % (e, ep)))
                self.cnt[e] = 0

    def scope(self):
        prog = self

        class _S:
            def __enter__(s_):
                s_.old = prog.aes
                s_.st = ExitStack()
                s_.st.__enter__()
                prog.aes = s_.st
                return s_

            def __exit__(s_, *a):
                prog.barrier()
                prog.aes = s_.old
                return s_.st.__exit__(*a)
        return _S()

    def ps(self, name, shape, dt):
        return Buf(self.es.enter_context(self.nc.psum_tensor(name, list(shape), dt)))

    def _deps(self, eng, reads, writes):
        waits = {}

        def add(kv):
            if kv is not None and kv[1] > waits.get(kv[0], 0):
                waits[kv[0]] = kv[1]

        for t in reads:
            if t.tracked:
                add(t.d.w)
        for t in writes:
            if not t.tracked:
                continue
            add(t.d.w)
            for kv in t.d.r.items():
                add(kv)
        need = []
        seen = self.seen[eng]
        for k, v in waits.items():
            if k[0] == "pe" and eng == "pe":
                continue
            if v > seen.get(k, 0):
                seen[k] = v
                need.append((k, v))
        return need

    def op(self, eng, fn, reads=(), writes=()):
        need = self._deps(eng, reads, writes)
        self.cnt[eng] += 1
        my = self.cnt[eng]
        key = self.ekey[eng]
        self.q[eng].append((need, fn, (key, 1)))
        for t in reads:
            t.d.r[key] = my
        for t in writes:
            t.d.w = (key, my)
            t.d.r = {}

    def dma(self, fn, reads=(), writes=(), eng="sp"):
        j = self.dnext
        self.dnext = (self.dnext + 1) % N_DMA_SEMS
        key = ("d", j)
        need = self._deps(eng, reads, writes)
        prev = self.dsem_cnt[j]
        if prev > self.seen[eng].get(key, 0):
            self.seen[eng][key] = prev
            need.append((key, prev))
        self.dsem_cnt[j] += 16
        tgt = self.dsem_cnt[j]
        self.q[eng].append((need, fn, (key, 16)))
        for t in reads:
            t.d.r[key] = tgt
        for t in writes:
            t.d.w = (key, tgt)
            t.d.r = {}

    def finish(self, final_bufs):
        self.barrier()
        nc, sems, q = self.nc, self.sems, self.q
        with nc.Block() as block:
            def run(engname):
                def body(e):
                    for need, fn, inc in q[engname]:
                        for k, v in need:
                            e.wait_ge(sems[k], v)
                        if fn is not None:
                            fn(e).then_inc(sems[inc[0]], inc[1])
                return body
            block.tensor(run("pe"))
            block.scalar(run("act"))
            block.vector(run("dve"))
            block.gpsimd(run("pool"))
            block.sync(run("sp"))


class KB:
    def __init__(self, nc, es):
        self.nc = nc
        self.P = Prog(nc, es)

    def dma(self, out, in_, reads, writes, eng="sp"):
        self.P.dma(lambda e: e.dma_start(out=out, in_=in_), reads, writes, eng=eng)

    def mm(self, out, lhsT, rhs, start, stop, reads, writes, sgc=False):
        self.P.op("pe", lambda e: e.matmul(out, lhsT=lhsT, rhs=rhs, start=start, stop=stop,
                                           skip_group_check=sgc), reads, writes)

    def tr(self, out, in_, ident, reads, writes):
        self.P.op("pe", lambda e: e.transpose(out=out, in_=in_, identity=ident), reads, writes)

    def act(self, out, in_, func, reads, writes, **kw):
        self.P.op("act", lambda e: e.activation(out=out, in_=in_, func=func, **kw), reads, writes)

    def ts(self, eng, out, in0, s1, s2, op0, op1, reads, writes, **kw):
        if op1 is None:
            self.P.op(eng, lambda e: e.tensor_single_scalar(out=out, in_=in0, scalar=s1, op=op0), reads, writes)
        else:
            self.P.op(eng, lambda e: e.tensor_scalar(out=out, in0=in0, scalar1=s1, scalar2=s2, op0=op0, op1=op1,
                                                     **kw), reads, writes)

    def tt(self, eng, out, in0, in1, op, reads, writes):
        self.P.op(eng, lambda e: e.tensor_tensor(out=out, in0=in0, in1=in1, op=op), reads, writes)

    def stt(self, eng, out, in0, scalar, in1, op0, op1, reads, writes):
        self.P.op(eng, lambda e: e.scalar_tensor_tensor(out=out, in0=in0, scalar=scalar, in1=in1, op0=op0,
                                                        op1=op1), reads, writes)

    def cp(self, eng, out, in_, reads, writes):
        if eng == "act":
            self.P.op("act", lambda e: e.copy(out=out, in_=in_), reads, writes)
        else:
            self.P.op(eng, lambda e: e.tensor_copy(out=out, in_=in_), reads, writes)

    def memset(self, eng, ap, val, writes):
        self.P.op(eng, lambda e: e.memset(ap, val), (), writes)


SCALE = 128.0 ** -0.5
NSLOT = 9
DEBUG = False


def build_program():
    nc = bass.Bass("TRN2", target_bir_lowering=False)

    def din(name, shape, dt=F32):
        return Buf(nc.dram_tensor(name, list(shape), dt, kind="ExternalInput").ap(), tracked=False)

    def dout(name, shape, dt=F32):
        return Buf(nc.dram_tensor(name, list(shape), dt, kind="ExternalOutput").ap(), tracked=False)

    def dscr(name, shape, dt):
        return Buf(nc.dram_tensor(name, list(shape), dt, kind="Internal").ap(), tracked=False)

    xb = din("xb", [SEQ, D])
    xq = din("xq", [NSLOT * 128, D])
    pos = din("pos", [128, 64])
    cmk = din("cmk", [128, 32])
    invf = din("invf", [1, 96])
    w_in = din("w_in", [D, PROJ_W])
    norm_mix = din("norm_mix", [128, 32])
    cmp_pos = din("cmp_pos", [2, 32, 128])
    cmp_w1 = din("cmp_w1", [2, 32, 128, 256])
    cmp_w2 = din("cmp_w2", [2, 256, 128])
    pq = din("pq", [NSLOT * 128, 256])
    ptab = din("ptab", [1, 256], I32)
    c_cmp = din("c_cmp", [2560 * 128, 512])
    c_sel = din("c_sel", [2560 * 128, 512])
    c_dkv = din("c_dkv", [2560 * 128, 1024])
    c_didx = din("c_didx", [2560 * 128, 64])
    state_win = din("state_win", [4, 512, 512])
    w_bn = din("w_bn", [2048, D])
    w_bd = din("w_bd", [2048, D])
    w_out = din("w_out", [D, D])
    norm_mlp = din("norm_mlp", [128, 32])
    w_up = din("w_up", [D, 4 * D])
    w_down = din("w_down", [4 * D, D])
    norm_ple = din("norm_ple", [128, 32])
    w_pg = din("w_pg", [D, D])
    w_ple = din("w_ple", [256, D])
    norm_final = din("norm_final", [1, D])

    o_cmp_p = dout("cmp_p", [SEQ, 512])
    o_sel_p = dout("sel_p", [SEQ, 512])
    o_dkv_p = dout("dkv_p", [SEQ, 1024])
    o_didx_p = dout("didx_p", [SEQ, 64])
    o_win_p = dout("win_p", [512, 512])
    o_cmp_s = dout("cmp_s", [4, 512])
    o_sel_s = dout("sel_s", [4, 512])
    o_dkv_s = dout("dkv_s", [4, 1024])
    o_didx_s = dout("didx_s", [4, 64])
    o_win_s = dout("win_s", [4, 512, 512])
    o_y = dout("y_own", [NSLOT * 128, D])
    outs = [o_cmp_p, o_sel_p, o_dkv_p, o_didx_p, o_win_p, o_cmp_s, o_sel_s, o_dkv_s, o_didx_s, o_win_s, o_y]

    kT_sel = dscr("kT_sel", [2, 128, SEQ], BF16)
    kT_win = dscr("kT_win", [2, 128, SEQ], BF16)
    kT_dsa = dscr("kT_dsa", [4, 128, SEQ], BF16)
    kT_idx = dscr("kT_idx", [64, SEQ], BF16)
    kvT_cmp = dscr("kvT_cmp", [4, 128, SEQ], BF16)
    v_sel = dscr("v_sel", [SEQ, 2, 128], BF16)
    v_win = dscr("v_win", [SEQ, 2, 128], BF16)
    v_dsa = dscr("v_dsa", [SEQ, 4, 128], BF16)
    qT_n = dscr("qT_n", [NSLOT, 128, 16, 128], BF16)
    qT_d = dscr("qT_d", [NSLOT, 128, 16, 128], BF16)
    qiT = dscr("qiT", [NSLOT, 128, 8, 128], BF16)
    gn_s = dscr("gn_s", [NSLOT, 128, 48], F32)
    wi_s = dscr("wi_s", [NSLOT, 128, 16], F32)
    gmT = dscr("gmT", [64, 128, NSLOT * 128], BF16)
    x1s = dscr("x1s", [NSLOT * 128, D], F32)
    snew_sel = dscr("snew_sel", [4, 512], F32)
    snew_dsa = dscr("snew_dsa", [4, 1024], F32)
    snew_idx = dscr("snew_idx", [4, 64], F32)
    x2s = dscr("x2s", [NSLOT * 128, D], F32)
    x3s = dscr("x3s", [NSLOT * 128, D], F32)
    uT_s = dscr("uT_s", [128, 128, NSLOT * 128], BF16)
    if DEBUG:
        oT_s = dout("oT_s", [NSLOT, 128, 32, 128], BF16)
        outs.append(oT_s)
        dbg_sc = dout("dbg_sc", [128, 512]); outs.append(dbg_sc)
        dbg_sc2 = dout("dbg_sc2", [128, 512]); outs.append(dbg_sc2)
        dbg_selm = dout("dbg_selm", [128, 512], BF16); outs.append(dbg_selm)
        dbg_mT = dout("dbg_mT", [128, 4, 128], BF16); outs.append(dbg_mT)
        dbg_qd = dout("dbg_qd", [128, 16, 128], BF16); outs.append(dbg_qd)
        dbg_obf = dout("dbg_obf", [128, 32, 128], BF16); outs.append(dbg_obf)
        dbg_acc = dout("dbg_acc", [128, 4, 256]); outs.append(dbg_acc)
        dbg_kt = dout("dbg_kt", [128, 512], BF16); outs.append(dbg_kt)
        dbg_v1 = dout("dbg_v1", [128, 4, 136], BF16); outs.append(dbg_v1)
    else:
        oT_s = dscr("oT_s", [NSLOT, 128, 32, 128], BF16)

    with ExitStack() as es:
        k = KB(nc, es)
        P = k.P
        ident = P.sb("ident", [128, 128], BF16)
        io_t = P.sb("io_t", [128, 128], F32)
        P.op("pool", lambda e: e.iota(io_t.t[:], pattern=[[1, 128]], base=0, channel_multiplier=-1,
                                      allow_small_or_imprecise_dtypes=True), (), [io_t])
        k.ts("dve", ident.t[:], io_t.t[:], 0.0, None, ALU.is_equal, None, [io_t], [ident])
        pos_t = P.sb("pos_t", [128, 64], F32)
        k.dma(pos_t.t[:], pos.t, [pos], [pos_t])
        cmk_t = P.sb("cmk_t", [128, 32], F32)
        k.dma(cmk_t.t[:], cmk.t, [cmk], [cmk_t])
        inv_t = P.sb("inv_t", [128, 96], F32)
        k.dma(inv_t.t[:], invf.t.broadcast_to([128, 96]), [invf], [inv_t])
        gT = P.sb("gT", [128, 32], F32)
        k.dma(gT.t[:], norm_mix.t, [norm_mix], [gT])
        for s_ in range(4):
            k.dma(o_win_s.t[s_, 0:511, :], state_win.t[s_, 1:512, :], [state_win], [o_win_s])
        pmm = Ring([P.ps("pmm%d" % i, [128, 512], F32) for i in range(4)])
        ptr = Ring([P.ps("ptr%d" % i, [128, 1024], BF16) for i in range(2)])
        pacc = P.ps("pacc", [128, 4, 256], F32)

        with P.scope():
            hT = P.sb("hT", [128, 32, NSLOT * 128], BF16)
            hT_d = [Buf(None) for _ in range(NSLOT)]
            xt_ring = Ring([P.sb("xt", [128, D], F32) for i in range(2)])
            xn = P.sb("xn", [128, D], BF16)
            ssq = Ring([P.sb("ssq", [128, 1], F32) for i in range(2)])
            wsb_ring = Ring([P.sb("wsb", [128, 32, 512], BF16) for i in range(2)])
            stage_ring = Ring([P.sb("stg", [128, 512], F32) for i in range(3)])
            ob_ring = Ring([P.sb("ob", [128, 512], BF16) for i in range(3)])
            tmp_ring = Ring([P.sb("rtmp", [128, 512], F32) for i in range(4)])
            ktsb_ring = Ring([P.sb("ktsb", [128, 512], BF16) for i in range(2)])
            gsb_ring = Ring([P.sb("gsb", [128, 384], BF16) for i in range(2)])
            trig_all = P.sb("trig_all", [128, NSLOT, 192], F32)
            trig_d = [Buf(None) for _ in range(NSLOT)]
            tg_a = P.sb("tg_a", [128, 192], F32)
            tg_i = P.sb("tg_i", [128, 192], I32)
            tg_f = P.sb("tg_f", [128, 192], F32)

            def make_trig(pos_col, slot):
                k.ts("dve", tg_a.t[:, 0:96], inv_t.t[:], pos_t.t[:, pos_col:pos_col + 1], 1.0 / (2 * PI),
                     ALU.mult, ALU.mult, [inv_t, pos_t], [tg_a])
                k.ts("dve", tg_a.t[:, 96:192], tg_a.t[:, 0:96], 0.25, None, ALU.add, None, [tg_a], [tg_a])
                k.cp("dve", tg_i.t[:], tg_a.t[:], [tg_a], [tg_i])
                k.cp("dve", tg_f.t[:], tg_i.t[:], [tg_i], [tg_f])
                k.tt("dve", tg_a.t[:], tg_a.t[:], tg_f.t[:], ALU.subtract, [tg_a, tg_f], [tg_a])
                k.stt("dve", tg_f.t[:], tg_a.t[:], 0.5, tg_a.t[:], ALU.is_gt, ALU.subtract, [tg_a], [tg_f])
                k.act(trig_all.t[:, slot, :], tg_f.t[:], AF.Sin, [tg_f], [trig_d[slot]], scale=-2 * PI)

            def rope(src_ps, src_ap, dst, dst_ap, nh, half, slot, foff):
                td = trig_d[slot]
                cb = trig_all.t[:, slot, 96 + foff:96 + foff + half].unsqueeze(1).broadcast_to([128, nh, half])
                sb_ = trig_all.t[:, slot, foff:foff + half].unsqueeze(1).broadcast_to([128, nh, half])
                x1 = src_ap[:, :, 0:half]
                x2 = src_ap[:, :, half:2 * half]
                t1 = tmp_ring.next(); t2 = tmp_ring.next()
                a1 = t1.t[:, 0:nh * half].rearrange("p (h d) -> p h d", h=nh)
                a2 = t2.t[:, 0:nh * half].rearrange("p (h d) -> p h d", h=nh)
                k.tt("dve", a1, x1, cb, ALU.mult, [src_ps, td], [t1])
                k.tt("dve", a2, x2, sb_, ALU.mult, [src_ps, td], [t2])
                k.tt("dve", dst_ap[:, :, 0:half], a1, a2, ALU.subtract, [t1, t2], [dst])
                t3 = tmp_ring.next(); t4 = tmp_ring.next()
                a3 = t3.t[:, 0:nh * half].rearrange("p (h d) -> p h d", h=nh)
                a4 = t4.t[:, 0:nh * half].rearrange("p (h d) -> p h d", h=nh)
                k.tt("dve", a3, x1, sb_, ALU.mult, [src_ps, td], [t3])
                k.tt("dve", a4, x2, cb, ALU.mult, [src_ps, td], [t4])
                k.tt("dve", dst_ap[:, :, half:2 * half], a3, a4, ALU.add, [t3, t4], [dst])

            def norm_tile(src_rows_ap, src_buf, slot):
                xt = xt_ring.next(); sq = ssq.next()
                k.dma(xt.t[:], src_rows_ap, [src_buf], [xt])
                k.act(xn.t[:], xt.t[:], AF.Square, [xt], [xn, sq], accum_out=sq.t[:])
                k.ts("dve", sq.t[:], sq.t[:], 1.0 / D, 1e-6, ALU.mult, ALU.add, [sq], [sq])
                k.act(sq.t[:], sq.t[:], AF.Sqrt, [sq], [sq])
                P.op("dve", lambda e: e.reciprocal(out=sq.t[:], in_=sq.t[:]), [sq], [sq])
                k.ts("dve", xn.t[:], xt.t[:], sq.t[:, 0:1], None, ALU.mult, None, [xt, sq], [xn])
                for kc4 in range(8):
                    pt = ptr.next()
                    for u in range(4):
                        kc = kc4 * 4 + u
                        k.tr(pt.t[:, u * 128:(u + 1) * 128], xn.t[:, kc * 128:(kc + 1) * 128], ident.t[:],
                             [xn, ident], [pt])
                    for u in range(4):
                        kc = kc4 * 4 + u
                        k.act(hT.t[:, kc, slot * 128:(slot + 1) * 128], pt.t[:, u * 128:(u + 1) * 128], AF.Copy,
                              [pt, gT], [hT_d[slot]], scale=gT.t[:, kc:kc + 1])

            def load_w(wbuf, c0, ncols):
                w = wsb_ring.next()
                if not hasattr(w, "hb"):
                    w.hb = Buf(w.t)
                src = wbuf.t[:, c0:c0 + ncols].rearrange("(kc p) c -> p kc c", p=128)
                for half in range(2):
                    k.dma(w.t[:, half * 16:(half + 1) * 16, 0:ncols], src[:, half * 16:(half + 1) * 16, :],
                          [wbuf], [w if half == 0 else w.hb], eng="pool")
                return w

            def proj_tile(w, ncols, slot):
                ps = pmm.next()
                for kc in range(32):
                    k.mm(ps.t[:, 0:ncols], hT.t[:, kc, slot * 128:(slot + 1) * 128], w.t[:, kc, 0:ncols],
                         kc == 0, kc == 31, [hT_d[slot], w if kc < 16 else w.hb], [ps])
                return ps

            pend = [None]

            def kv_epilogue(gname, ps, slot, tix):
                st = stage_ring.next()
                ob = ob_ring.next()
                r0 = tix * 128
                if gname == "idx":
                    rope(ps, ps.t[:, 0:64].rearrange("p (h d) -> p h d", h=1), st,
                         st.t[:, 0:64].rearrange("p (h d) -> p h d", h=1), 1, 32, slot, 64)
                    if tix < 0:
                        k.dma(o_didx_s.t[:, :], st.t[0:4, 0:64], [st], [o_didx_s])
                        k.dma(snew_idx.t[:, :], st.t[0:4, 0:64], [st], [snew_idx])
                        return None
                    k.dma(o_didx_p.t[r0:r0 + 128, :], st.t[:, 0:64], [st], [o_didx_p])
                    k.cp("pool", ob.t[:, 0:64], st.t[:, 0:64], [st], [ob])

                    def part2():
                        pt = ptr.next()
                        k.tr(pt.t[0:64, 0:128], ob.t[:, 0:64], ident.t[:], [ob, ident], [pt])
                        kt = ktsb_ring.next()
                        k.cp("act", kt.t[0:64, 0:128], pt.t[0:64, 0:128], [pt], [kt])
                        k.dma(kT_idx.t[:, r0:r0 + 128], kt.t[0:64, 0:128], [kt], [kT_idx])
                    return part2
                psv = ps.t[:, :].rearrange("p (h c d) -> p h c d", h=2, c=2)
                stv = st.t[:, :].rearrange("p (h c d) -> p h c d", h=2, c=2)
                rope(ps, psv[:, :, 0, :], st, stv[:, :, 0, :], 2, 64, slot, 0)
                k.cp("act", stv[:, :, 1, :], psv[:, :, 1, :], [ps], [st])
                if tix < 0:
                    if gname == "win":
                        k.dma(o_win_s.t[:, 511, :], st.t[0:4, :], [st], [o_win_s])
                        return None
                    od = {"cmp": (o_cmp_s, 0), "sel": (o_sel_s, 0),
                          "dsa0": (o_dkv_s, 0), "dsa1": (o_dkv_s, 512)}[gname]
                    k.dma(od[0].t[:, od[1]:od[1] + 512], st.t[0:4, :], [st], [od[0]])
                    if gname == "sel":
                        k.dma(snew_sel.t[:, :], st.t[0:4, :], [st], [snew_sel])
                    elif gname in ("dsa0", "dsa1"):
                        k.dma(snew_dsa.t[:, od[1]:od[1] + 512], st.t[0:4, :], [st], [snew_dsa])
                    return None
                if gname == "cmp":
                    k.dma(o_cmp_p.t[r0:r0 + 128, :], st.t[:, :], [st], [o_cmp_p])
                elif gname == "sel":
                    k.dma(o_sel_p.t[r0:r0 + 128, :], st.t[:, :], [st], [o_sel_p])
                elif gname == "win":
                    if tix >= 28:
                        k.dma(o_win_p.t[(tix - 28) * 128:(tix - 27) * 128, :], st.t[:, :], [st], [o_win_p])
                elif gname == "dsa0":
                    k.dma(o_dkv_p.t[r0:r0 + 128, 0:512], st.t[:, :], [st], [o_dkv_p])
                elif gname == "dsa1":
                    k.dma(o_dkv_p.t[r0:r0 + 128, 512:1024], st.t[:, :], [st], [o_dkv_p])
                k.cp("pool", ob.t[:, :], st.t[:, :], [st], [ob])
                obv = ob.t[:, :].rearrange("p (h c d) -> p h c d", h=2, c=2)

                def part2():
                    pt = ptr.next()
                    kt = ktsb_ring.next()
                    if gname == "cmp":
                        for h in range(2):
                            for c in range(2):
                                q4 = h * 2 + c
                                k.tr(pt.t[:, q4 * 128:(q4 + 1) * 128], obv[:, h, c, :], ident.t[:], [ob, ident], [pt])
                        k.cp("act", kt.t[:, 0:512], pt.t[:, 0:512], [pt], [kt])
                        k.dma(kvT_cmp.t[:, :, r0:r0 + 128].rearrange("f p t -> p f t"),
                              kt.t[:, 0:512].rearrange("p (f t) -> p f t", f=4), [kt], [kvT_cmp])
                    else:
                        for h in range(2):
                            k.tr(pt.t[:, h * 128:(h + 1) * 128], obv[:, h, 0, :], ident.t[:], [ob, ident], [pt])
                        k.cp("act", kt.t[:, 0:256], pt.t[:, 0:256], [pt], [kt])
                        ktd, vd, h0 = {"sel": (kT_sel, v_sel, 0), "win": (kT_win, v_win, 0),
                                       "dsa0": (kT_dsa, v_dsa, 0), "dsa1": (kT_dsa, v_dsa, 2)}[gname]
                        k.dma(ktd.t[h0:h0 + 2, :, r0:r0 + 128].rearrange("f p t -> p f t"),
                              kt.t[:, 0:256].rearrange("p (f t) -> p f t", f=2), [kt], [ktd])
                        k.dma(vd.t[r0:r0 + 128, h0:h0 + 2, :], obv[:, :, 1, :], [ob], [vd])
                return part2

            kv_groups = [
                ("cmp", C_KVC, 512), ("sel", C_KVS, 512), ("win", C_KVW, 512),
                ("dsa0", C_KVD, 512), ("dsa1", C_KVD + 512, 512), ("idx", C_KI, 64),
            ]
            for sbk in range(4):
                tiles = [(sbk * 8 + u, xb.t[(sbk * 8 + u) * 128:(sbk * 8 + u + 1) * 128, :], xb, sbk * 8 + u)
                         for u in range(8)]
                if sbk == 0:
                    tiles.append((-1, xq.t[1024:1152, :], xq, 40))
                for slot, (tix, rows, src, pcol) in enumerate(tiles):
                    norm_tile(rows, src, slot)
                    make_trig(pcol, slot)
                wnext = load_w(w_in, kv_groups[0][1], kv_groups[0][2])
                for gi, (gname, c0, ncols) in enumerate(kv_groups):
                    w = wnext
                    if gi + 1 < len(kv_groups):
                        wnext = load_w(w_in, kv_groups[gi + 1][1], kv_groups[gi + 1][2])
                    for slot, (tix, rows, src, pcol) in enumerate(tiles):
                        ps = proj_tile(w, ncols, slot)
                        if pend[0] is not None:
                            pend[0]()
                            pend[0] = None
                        pend[0] = kv_epilogue(gname, ps, slot, tix)
                if pend[0] is not None:
                    pend[0]()
                    pend[0] = None

            for slot in range(NSLOT):
                norm_tile(xq.t[slot * 128:(slot + 1) * 128, :], xq, slot)
                make_trig(32 + slot, slot)
            own_groups = ([("qn", C_QN + 512 * i, 512, i) for i in range(4)]
                          + [("qd", C_QD + 512 * i, 512, i) for i in range(4)]
                          + [("qi", C_QI + 512 * i, 512, i) for i in range(2)]
                          + [("gn", C_GN, 48, 0), ("wi", C_WI, 16, 0)]
                          + [("gm", C_GM + 512 * i, 512, i) for i in range(16)])
            wnext = load_w(w_in, own_groups[0][1], own_groups[0][2])
            for gi, (gname, c0, ncols, gidx) in enumerate(own_groups):
                w = wnext
                if gi + 1 < len(own_groups):
                    wnext = load_w(w_in, own_groups[gi + 1][1], own_groups[gi + 1][2])
                if gname == "gm":
                    if pend[0] is not None:
                        pend[0]()
                        pend[0] = None
                    for cc in range(4):
                        chunk = gidx * 4 + cc
                        for tc in range(3):
                            ps = pmm.next()
                            for kc in range(32):
                                k.mm(ps.t[:, 0:384], w.t[:, kc, cc * 128:(cc + 1) * 128],
                                     hT.t[:, kc, tc * 384:(tc + 1) * 384], kc == 0, kc == 31,
                                     [w if kc < 16 else w.hb] + hT_d[tc * 3:tc * 3 + 3], [ps])
                            gs = gsb_ring.next()
                            k.act(gs.t[:, :], ps.t[:, 0:384], AF.Sigmoid, [ps], [gs])
                            k.dma(gmT.t[chunk, :, tc * 384:(tc + 1) * 384], gs.t[:, :], [gs], [gmT])
                    continue
                for slot in range(NSLOT):
                    ps = proj_tile(w, ncols, slot)
                    if pend[0] is not None:
                        pend[0]()
                        pend[0] = None
                    if gname in ("qn", "qd", "qi"):
                        ob = ob_ring.next()
                        if gname == "qi":
                            rope(ps, ps.t[:, :].rearrange("p (h d) -> p h d", h=8), ob,
                                 ob.t[:, :].rearrange("p (h d) -> p h d", h=8), 8, 32, slot, 64)
                        else:
                            rope(ps, ps.t[:, :].rearrange("p (h d) -> p h d", h=4), ob,
                                 ob.t[:, :].rearrange("p (h d) -> p h d", h=4), 4, 64, slot, 0)

                        def part2(ob=ob, gname=gname, slot=slot, gidx=gidx):
                            pt = ptr.next()
                            kt = ktsb_ring.next()
                            for u in range(4):
                                k.tr(pt.t[:, u * 128:(u + 1) * 128], ob.t[:, u * 128:(u + 1) * 128], ident.t[:],
                                     [ob, ident], [pt])
                            k.cp("act", kt.t[:, 0:512], pt.t[:, 0:512], [pt], [kt])
                            dst = {"qn": qT_n, "qd": qT_d, "qi": qiT}[gname]
                            k.dma(dst.t[slot, :, gidx * 4:gidx * 4 + 4, :],
                                  kt.t[:, 0:512].rearrange("p (f t) -> p f t", f=4), [kt], [dst])
                        pend[0] = part2
                    elif gname == "gn":
                        st = stage_ring.next()
                        k.act(st.t[:, 0:48], ps.t[:, 0:48], AF.Sigmoid, [ps], [st])
                        k.dma(gn_s.t[slot, :, :], st.t[:, 0:48], [st], [gn_s])
                    elif gname == "wi":
                        st = stage_ring.next()
                        k.ts("dve", st.t[:, 0:16], ps.t[:, 0:16], 1.0 / 32.0, None, ALU.mult, None, [ps], [st])
                        k.dma(wi_s.t[slot, :, :], st.t[:, 0:16], [st], [wi_s])

        with P.scope():
            kcT = P.sb("kcT", [128, 2, 256], BF16)
            vc1 = P.sb("vc1", [128, 2, 2, 200], BF16)
            k.memset("dve", kcT.t[:], 0.0, [kcT])
            k.memset("dve", vc1.t[:], 0.0, [vc1])
            with P.scope():
                w1sb = [P.sb("w1sb", [128, 32, 256], BF16) for c in range(2)]
                w2sb = P.sb("w2sb", [128, 2, 2, 128], BF16)
                posb = P.sb("posb", [32, 2, 128], BF16)
                posT = P.sb("posT", [128, 2, 32], BF16)
                bias_sb = P.sb("bias_sb", [128, 4], F32)
                hidT = P.sb("hidT", [128, 2, 256], BF16)
                kv_ring = Ring([P.sb("kvc", [128, SEQ], BF16) for i in range(2)])
                k.memset("dve", hidT.t[:], 0.0, [hidT])
                for c in range(2):
                    k.dma(w1sb[c].t[:], cmp_w1.t[c].rearrange("l d h -> d l h"), [cmp_w1], [w1sb[c]], eng="pool")
                k.dma(w2sb.t[:], cmp_w2.t.rearrange("c (hh p) d -> p c hh d", p=128), [cmp_w2], [w2sb], eng="pool")
                k.dma(posb.t[:], cmp_pos.t.rearrange("c l d -> l c d"), [cmp_pos], [posb], eng="pool")
                for c in range(2):
                    pt = ptr.next()
                    k.tr(pt.t[:, 0:32], posb.t[:, c, :], ident.t[0:32, 0:32], [posb, ident], [pt])
                    k.cp("act", posT.t[:, c, :], pt.t[:, 0:32], [pt], [posT])
                for c in range(2):
                    for hh in range(2):
                        ps = pmm.next()
                        for l in range(32):
                            k.mm(ps.t[:, 0:1], w1sb[c].t[:, l, hh * 128:(hh + 1) * 128], posT.t[:, c, l:l + 1],
                                 l == 0, l == 31, [w1sb[c], posT], [ps])
                        k.cp("act", bias_sb.t[:, c * 2 + hh:c * 2 + hh + 1], ps.t[:, 0:1], [ps], [bias_sb])
                for g in range(2):
                    for c in range(2):
                        kv = kv_ring.next()
                        k.dma(kv.t[:, :], kvT_cmp.t[g * 2 + c], [kvT_cmp], [kv])
                        for hh in range(2):
                            ps = pmm.next()
                            for l in range(32):
                                k.mm(ps.t[:, 0:255], w1sb[c].t[:, l, hh * 128:(hh + 1) * 128],
                                     kv.t[:, l:l + 16 * 254 + 1:16], l == 0, l == 31, [w1sb[c], kv], [ps])
                            k.act(hidT.t[:, hh, 0:255], ps.t[:, 0:255], AF.Relu, [ps, bias_sb], [hidT],
                                  bias=bias_sb.t[:, c * 2 + hh:c * 2 + hh + 1])
                        if c == 0:
                            ps = pmm.next()
                            for hh in range(2):
                                k.mm(ps.t[:, 0:255], w2sb.t[:, 0, hh, :], hidT.t[:, hh, 0:255], hh == 0, hh == 1,
                                     [w2sb, hidT], [ps])
                            k.cp("act", kcT.t[:, g, 0:255], ps.t[:, 0:255], [ps], [kcT])
                        else:
                            for nt in range(2):
                                ps = pmm.next()
                                for hh in range(2):
                                    k.mm(ps.t[:, 0:128], hidT.t[:, hh, nt * 128:(nt + 1) * 128], w2sb.t[:, 1, hh, :],
                                         hh == 0, hh == 1, [w2sb, hidT], [ps])
                                k.cp("act", vc1.t[:, g, nt, 0:128], ps.t[:, 0:128], [ps], [vc1])
            ovl_v = P.sb("ovl_v", [128, 64], F32)
            ovl_a = P.sb("ovl_a", [128, 64], F32)
            for nt in range(2):
                P.op("pool", lambda e, nt=nt: e.iota(ovl_v.t[:], pattern=[[4, 64]], base=-128 * nt,
                                                     channel_multiplier=-1, allow_small_or_imprecise_dtypes=True),
                     (), [ovl_v])
                k.ts("dve", ovl_a.t[:], ovl_v.t[:], -3.0, None, ALU.is_ge, None, [ovl_v], [ovl_a])
                for g in range(2):
                    k.stt("dve", vc1.t[:, g, nt, 129:193], ovl_v.t[:], 1.0, ovl_a.t[:], ALU.is_le, ALU.mult,
                          [ovl_v, ovl_a], [vc1])
                    k.memset("dve", vc1.t[:, g, nt, 128:129], 1.0, [vc1])

            iota_k = P.sb("iota_k", [128, SEQ], F32)
            P.op("pool", lambda e: e.iota(iota_k.t[:], pattern=[[1, SEQ]], base=0, channel_multiplier=0,
                                          allow_small_or_imprecise_dtypes=True), (), [iota_k])
            iota_j = P.sb("iota_j", [128, 64], F32)
            P.op("pool", lambda e: e.iota(iota_j.t[:], pattern=[[1, 64]], base=0, channel_multiplier=0,
                                          allow_small_or_imprecise_dtypes=True), (), [iota_j])
            io_q = P.sb("io_q", [128, 128], F32)
            P.op("pool", lambda e: e.iota(io_q.t[:], pattern=[[1, 128]], base=0, channel_multiplier=0,
                                          allow_small_or_imprecise_dtypes=True), (), [io_q])
            thr_n = P.sb("thr_n", [128, 2], F32)
            P.op("pool", lambda e: e.iota(thr_n.t[:], pattern=[[2048, 2]], base=31, channel_multiplier=16,
                                          allow_small_or_imprecise_dtypes=True), (), [thr_n])
            e0 = P.sb("e0", [128, 64], F32)
            k.ts("dve", e0.t[:], iota_j.t[:], 0.0, None, ALU.is_equal, None, [iota_j], [e0])
            tri = P.sb("tri", [128, 128], F32)
            k.ts("dve", tri.t[:], io_t.t[:], 0.0, None, ALU.is_ge, None, [io_t], [tri])
            CW = P.sb("CW", [128, 12, 128], BF16)
            for m in range(12):
                k.ts("dve", CW.t[:, m, :], tri.t[:], cmk_t.t[:, 8 + 2 * m:9 + 2 * m], cmk_t.t[:, 9 + 2 * m:10 + 2 * m],
                     ALU.mult, ALU.add, [tri, cmk_t], [CW])
            Ex = P.sb("Ex", [64, 32, 128], BF16)
            with P.scope():
                Ex_v = P.sb("Ex_v", [64, 32, 128], F32)
                Ex_a = P.sb("Ex_a", [64, 32, 128], F32)
                P.op("pool", lambda e: e.iota(Ex_v.t[:], pattern=[[128, 32], [1, 128]], base=0,
                                              channel_multiplier=-64, allow_small_or_imprecise_dtypes=True),
                     (), [Ex_v])
                k.ts("dve", Ex_a.t[:], Ex_v.t[:], 0.0, None, ALU.is_ge, None, [Ex_v], [Ex_a])
                k.stt("dve", Ex.t[:], Ex_v.t[:], 63.0, Ex_a.t[:], ALU.is_le, ALU.mult, [Ex_v, Ex_a], [Ex])
            kidx2 = P.sb("kidx2", [128, SEQ], BF16)
            k.dma(kidx2.t[0:64, :], kT_idx.t, [kT_idx], [kidx2])
            k.dma(kidx2.t[64:128, :], kT_idx.t, [kT_idx], [kidx2])

            qn_ring = Ring([P.sb("qn", [128, 16, 128], BF16) for i in range(2)])
            qd_ring = Ring([P.sb("qd", [128, 16, 128], BF16) for i in range(2)])
            qi_ring = Ring([P.sb("qi", [128, 8, 128], BF16) for i in range(2)])
            gn_ring = Ring([P.sb("gn", [128, 48], F32) for i in range(2)])
            wi_ring = Ring([P.sb("wi", [128, 16], F32) for i in range(2)])
            sc = P.sb("sc", [128, SEQ], F32)
            relu_ring = Ring([P.sb("rl", [128, 512], F32) for i in range(3)])
            m8_ring = Ring([P.sb("m8", [128, 8], F32) for i in range(4)])
            selm = P.sb("selm", [128, SEQ], BF16)
            mT = P.sb("mT", [128, 32, 128], BF16)
            ms = P.sb("ms", [128, 32, 128], BF16)
            mc = P.sb("mc", [128, 2, 128], BF16)
            kt_ring = Ring([P.sb("ktb", [128, SEQ], BF16) for i in range(3)])
            v1_ring = Ring([P.sb("v1b", [128, 32, 136], BF16) for i in range(3)])
            for vb in v1_ring.bufs:
                k.memset("pool", vb.t[:, :, 128:136], 1.0, [vb])
            e_ring = Ring([P.sb("eb", [128, 512], BF16) for i in range(4)])
            rd_ring = Ring([P.sb("rd", [128, 4], F32) for i in range(4)])
            o32 = P.sb("o32", [128, 16, 128], F32)
            obf = P.sb("obf", [128, 32, 128], BF16)
            oTsb = P.sb("oTsb", [128, 32, 128], BF16)
            impu = P.sb("impu", [128, 8, 64], F32)
            imp = P.sb("imp", [128, 64], F32)
            sw = [P.sb("sw%d" % i, [128, 64], F32) for i in range(4)]
            selb = P.sb("selb", [128, 64], BF16)
            selT = P.sb("selT", [64, 128], BF16)
            curm1 = P.sb("curm1", [128, 1], F32)
            mmul_eng = ["dve", "pool"]
            cnt = [0]

            def attend(kts, K_of, V_of, ncols, q_ap, q_buf, M_of):
                n = len(kts)
                ebs = {}

                def stage1(idx):
                    kt = kts[idx]
                    ps = pmm.next()
                    kap, kbuf = K_of(kt)
                    k.mm(ps.t[:, 0:512], kap, q_ap, True, True, [kbuf, q_buf], [ps])
                    eb = e_ring.next()
                    k.act(eb.t[:, :], ps.t[:, 0:512], AF.Exp, [ps], [eb], scale=SCALE)
                    map_, mbuf = M_of(kt)
                    eng = mmul_eng[cnt[0] % 2]; cnt[0] += 1
                    ev = eb.t[:, :].rearrange("p (r q) -> p r q", r=4)
                    k.tt(eng, ev, ev, map_.unsqueeze(1).broadcast_to([128, 4, 128]), ALU.mult, [eb, mbuf], [eb])
                    ebs[idx] = eb

                def stage2(idx):
                    eb = ebs.pop(idx)
                    vap, vbuf = V_of(kts[idx])
                    for r in range(4):
                        k.mm(pacc.t[:, r, 0:ncols], eb.t[:, r * 128:(r + 1) * 128], vap, idx == 0 and r % 2 == 0,
                             idx == n - 1, [eb, vbuf], [pacc], sgc=True)
                stage1(0)
                for idx in range(n):
                    if idx + 1 < n:
                        stage1(idx + 1)
                    stage2(idx)
                if tick_hook[0] is not None:
                    tick_hook[0]()

            def recip_den():
                rd = rd_ring.next()
                k.ts("dve", rd.t[:, :], pacc.t[:, :, 128], 1e-30, None, ALU.max, None, [pacc], [rd])
                P.op("dve", lambda e: e.reciprocal(out=rd.t[:, :], in_=rd.t[:, :]), [rd], [rd])
                return rd

            def load_kv(ktd, vd, head, lo, hi):
                ktb = kt_ring.next(); v1b = v1_ring.next()
                nk = hi - lo
                k.dma(ktb.t[:, 0:nk * 128], ktd.t[head, :, lo * 128:hi * 128], [ktd], [ktb])
                k.dma(v1b.t[:, 0:nk, 0:128],
                      vd.t[lo * 128:hi * 128, head, :].rearrange("(kt p) d -> p kt d", p=128), [vd], [v1b])
                return ktb, v1b

            tick_hook = [None]

            def tile_load_score(j):
                nkt = 4 * j + 4
                Lk = nkt * 128
                qpos = pos_t.t[:, 32 + j:33 + j]
                qn = qn_ring.next(); qd = qd_ring.next(); qi = qi_ring.next(); gn = gn_ring.next(); wi = wi_ring.next()
                k.dma(qn.t[:], qT_n.t[j], [qT_n], [qn])
                k.dma(qd.t[:], qT_d.t[j], [qT_d], [qd])
                k.dma(qi.t[:], qiT.t[j], [qiT], [qi])
                k.dma(gn.t[:], gn_s.t[j], [gn_s], [gn])
                k.dma(wi.t[:], wi_s.t[j], [wi_s], [wi])
                k.ts("dve", sc.t[:, 0:Lk], iota_k.t[:, 0:Lk], qpos, -1e30, ALU.is_gt, ALU.mult, [iota_k, pos_t], [sc])
                for c in range(nkt // 4):
                    for h in range(16):
                        hp = (h % 2) * 64
                        ps = pmm.next()
                        k.mm(ps.t[:, 0:512], qi.t[hp:hp + 64, h // 2, :], kidx2.t[hp:hp + 64, c * 512:(c + 1) * 512],
                             True, True, [qi, kidx2], [ps])
                        rl = relu_ring.next()
                        k.act(rl.t[:, :], ps.t[:, 0:512], AF.Relu, [ps], [rl])
                        k.stt("dve", sc.t[:, c * 512:(c + 1) * 512], rl.t[:, :], wi.t[:, h:h + 1],
                              sc.t[:, c * 512:(c + 1) * 512], ALU.mult, ALU.add, [rl, wi, sc], [sc])
                return dict(qn=qn, qd=qd, gn=gn)

            def topk_closures(j):
                Lk = (4 * j + 4) * 128

                def one():
                    m8 = m8_ring.next()
                    P.op("dve", lambda e, m8=m8, Lk=Lk: e.max(out=m8.t[:, :], in_=sc.t[:, 0:Lk]), [sc], [m8])
                    P.op("dve", lambda e, m8=m8, Lk=Lk: e.match_replace(out=sc.t[:, 0:Lk], in_to_replace=m8.t[:, :],
                                                                      in_values=sc.t[:, 0:Lk], imm_value=-3e38),
                         [sc, m8], [sc])
                return [one for _ in range(32)]

            def build_mask(j):
                nkt = 4 * j + 4
                Lk = nkt * 128
                qpos = pos_t.t[:, 32 + j:33 + j]
                k.ts("dve", selm.t[:, 0:Lk], sc.t[:, 0:Lk], -1e37, None, ALU.is_lt, None, [sc], [selm])
                k.stt("dve", selm.t[:, 0:Lk], iota_k.t[:, 0:Lk], qpos, selm.t[:, 0:Lk], ALU.is_le, ALU.mult,
                      [iota_k, pos_t, selm], [selm])
                for kt8 in range(0, nkt, 8):
                    pt = ptr.next()
                    nn = min(8, nkt - kt8)
                    for u in range(nn):
                        k.tr(pt.t[:, u * 128:(u + 1) * 128], selm.t[:, (kt8 + u) * 128:(kt8 + u + 1) * 128], ident.t[:],
                             [selm, ident], [pt])
                    k.cp("act", mT.t[:, kt8:kt8 + nn, :], pt.t[:, 0:nn * 128].rearrange("p (a q) -> p a q", a=nn),
                         [pt], [mT])

            def attention_part(j, ctx):
                nkt = 4 * j + 4
                Lk = nkt * 128
                qpos = pos_t.t[:, 32 + j:33 + j]
                cur = pos_t.t[:, 41 + j:42 + j]
                qn = ctx["qn"]; qd = ctx["qd"]; gn = ctx["gn"]
                gn3 = gn.t[:, :].rearrange("p (h b) -> p h b", b=3)
                for g in range(4):
                    ktb, v1b = load_kv(kT_dsa, v_dsa, g, 0, nkt)
                    attend(list(range(nkt)),
                           lambda kt, ktb=ktb: (ktb.t[:, kt * 128:(kt + 1) * 128], ktb),
                           lambda kt, v1b=v1b: (v1b.t[:, kt, 0:129], v1b), 129,
                           qd.t[:, g * 4:(g + 1) * 4, :].rearrange("p r q -> p (r q)"), qd,
                           lambda kt: (mT.t[:, kt, :], mT))
                    if DEBUG and j == 0 and g == 0:
                        dacc = P.sb("dacc", [128, 4, 256], F32)
                        k.cp("dve", dacc.t[:, 0:2, :], pacc.t[:, 0:2, :], [pacc], [dacc])
                        k.cp("dve", dacc.t[:, 2:4, :], pacc.t[:, 2:4, :], [pacc], [dacc])
                        k.dma(dbg_acc.t, dacc.t[:, :, :], [dacc], [dbg_acc])
                        k.dma(dbg_mT.t, mT.t[:, 0:4, :], [mT], [dbg_mT])
                        k.dma(dbg_kt.t, ktb.t[:, 0:512], [ktb], [dbg_kt])
                        k.dma(dbg_v1.t, v1b.t[:, 0:4, :], [v1b], [dbg_v1])
                    rd = recip_den()
                    k.tt("dve", obf.t[:, 16 + g * 4:16 + (g + 1) * 4, :], pacc.t[:, :, 0:128],
                         rd.t[:, :].unsqueeze(2).broadcast_to([128, 4, 128]), ALU.mult, [pacc, rd], [obf])

                for nt in range(2):
                    k.ts("dve", mc.t[:, nt, :], io_q.t[:, :], cmk_t.t[:, j:j + 1], thr_n.t[:, nt:nt + 1],
                         ALU.add, ALU.is_ge, [io_q, cmk_t, thr_n], [mc])
                k.ts("dve", curm1.t[:, :], cur, -1.0, None, ALU.add, None, [pos_t], [curm1])
                for g in range(2):
                    for sub in range(2):
                        h0 = g * 8 + sub * 4
                        attend([0, 1],
                               lambda nt, g=g: (kcT.t[:, g, nt * 128:(nt + 1) * 128], kcT),
                               lambda nt, g=g: (vc1.t[:, g, nt, 0:193], vc1), 193,
                               qn.t[:, h0:h0 + 4, :].rearrange("p r q -> p (r q)"), qn,
                               lambda nt: (mc.t[:, nt, :], mc))
                        rd = recip_den()
                        cf = rd_ring.next()
                        k.tt("dve", cf.t[:, :], rd.t[:, :], gn3[:, h0:h0 + 4, 0], ALU.mult, [rd, gn], [cf])
                        k.tt("dve", o32.t[:, h0:h0 + 4, :], pacc.t[:, :, 0:128],
                             cf.t[:, :].unsqueeze(2).broadcast_to([128, 4, 128]), ALU.mult, [pacc, cf], [o32])
                        k.tt("dve", impu.t[:, sub * 4:(sub + 1) * 4, :], pacc.t[:, :, 129:193],
                             rd.t[:, :].unsqueeze(2).broadcast_to([128, 4, 64]), ALU.mult, [pacc, rd], [impu])
                    P.op("dve", lambda e: e.tensor_reduce(out=imp.t[:, :], in_=impu.t[:, :, :].rearrange("p r j -> p j r"),
                                                          axis=AX.X, op=ALU.add), [impu], [imp])
                    k.ts("dve", sw[0].t[:, :], iota_j.t[:, :], cur, None, ALU.is_equal, None, [iota_j, pos_t], [sw[0]])
                    k.ts("dve", sw[1].t[:, :], iota_j.t[:, :], curm1.t[:, 0:1], None, ALU.is_equal, None,
                         [iota_j, curm1], [sw[1]])
                    k.tt("dve", sw[0].t[:, :], sw[0].t[:, :], sw[1].t[:, :], ALU.add, [sw[0], sw[1]], [sw[0]])
                    k.tt("dve", sw[0].t[:, :], sw[0].t[:, :], e0.t[:, :], ALU.add, [sw[0], e0], [sw[0]])
                    k.stt("dve", sw[1].t[:, :], sw[0].t[:, :], 1e4, imp.t[:, :], ALU.mult, ALU.add, [sw[0], imp], [sw[1]])
                    k.ts("dve", sw[2].t[:, :], iota_j.t[:, :], cur, None, ALU.is_le, None, [iota_j, pos_t], [sw[2]])
                    k.tt("dve", sw[1].t[:, :], sw[1].t[:, :], sw[2].t[:, :], ALU.mult, [sw[1], sw[2]], [sw[1]])
                    k.ts("dve", sw[2].t[:, :], sw[2].t[:, :], -1.0, 1e30, ALU.add, ALU.mult, [sw[2]], [sw[2]])
                    k.tt("dve", sw[1].t[:, :], sw[1].t[:, :], sw[2].t[:, :], ALU.add, [sw[1], sw[2]], [sw[1]])
                    ma = m8_ring.next(); mb = m8_ring.next()
                    P.op("dve", lambda e, ma=ma: e.max(out=ma.t[:, :], in_=sw[1].t[:, :]), [sw[1]], [ma])
                    P.op("dve", lambda e, ma=ma: e.match_replace(out=sw[3].t[:, :], in_to_replace=ma.t[:, :],
                                                                in_values=sw[1].t[:, :], imm_value=-3e38),
                         [sw[1], ma], [sw[3]])
                    P.op("dve", lambda e, mb=mb: e.max(out=mb.t[:, :], in_=sw[3].t[:, :]), [sw[3]], [mb])
                    k.ts("dve", selb.t[:, :], sw[1].t[:, :], mb.t[:, 7:8], None, ALU.is_ge, None, [sw[1], mb], [selb])
                    pt = ptr.next()
                    k.tr(pt.t[0:64, 0:128], selb.t[:, :], ident.t[:], [selb, ident], [pt])
                    k.cp("act", selT.t[:, :], pt.t[0:64, 0:128], [pt], [selT])
                    for kt in range(nkt):
                        ps = pmm.next()
                        k.mm(ps.t[:, 0:128], Ex.t[:, kt, :], selT.t[:, :], True, True, [Ex, selT], [ps])
                        if kt < 4 * j:
                            k.cp("act", ms.t[:, kt, :], ps.t[:, 0:128], [ps], [ms])
                        else:
                            k.tt("dve", ms.t[:, kt, :], ps.t[:, 0:128], CW.t[:, kt - 4 * j, :], ALU.mult, [ps, CW], [ms])
                    ktb_s, v1b_s = load_kv(kT_sel, v_sel, g, 0, nkt)
                    wlo = max(0, 4 * j - 4)
                    ktb_w, v1b_w = load_kv(kT_win, v_win, g, wlo, nkt)
                    for sub in range(2):
                        h0 = g * 8 + sub * 4
                        q_ap = qn.t[:, h0:h0 + 4, :].rearrange("p r q -> p (r q)")
                        attend(list(range(nkt)),
                               lambda kt, b=ktb_s: (b.t[:, kt * 128:(kt + 1) * 128], b),
                               lambda kt, b=v1b_s: (b.t[:, kt, 0:129], b), 129, q_ap, qn,
                               lambda kt: (ms.t[:, kt, :], ms))
                        rd = recip_den()
                        cf = rd_ring.next()
                        k.tt("dve", cf.t[:, :], rd.t[:, :], gn3[:, h0:h0 + 4, 1], ALU.mult, [rd, gn], [cf])
                        for r in range(4):
                            k.stt("dve", o32.t[:, h0 + r, :], pacc.t[:, r, 0:128], cf.t[:, r:r + 1], o32.t[:, h0 + r, :],
                                  ALU.mult, ALU.add, [pacc, cf, o32], [o32])
                        attend(list(range(wlo, nkt)),
                               lambda kt, b=ktb_w, wlo=wlo: (b.t[:, (kt - wlo) * 128:(kt - wlo + 1) * 128], b),
                               lambda kt, b=v1b_w, wlo=wlo: (b.t[:, kt - wlo, 0:129], b), 129, q_ap, qn,
                               lambda kt, j=j: (CW.t[:, 4 + (kt - 4 * j + 4), :], CW))
                        rd = recip_den()
                        cf = rd_ring.next()
                        k.tt("dve", cf.t[:, :], rd.t[:, :], gn3[:, h0:h0 + 4, 2], ALU.mult, [rd, gn], [cf])
                        for r in range(4):
                            k.stt("dve", o32.t[:, h0 + r, :], pacc.t[:, r, 0:128], cf.t[:, r:r + 1], o32.t[:, h0 + r, :],
                                  ALU.mult, ALU.add, [pacc, cf, o32], [o32])
                k.cp("pool", obf.t[:, 0:16, :], o32.t[:, :, :], [o32], [obf])
                if DEBUG and j == 0:
                    k.dma(dbg_obf.t, obf.t[:, :, :], [obf], [dbg_obf])
                for h8 in range(0, 32, 8):
                    pt = ptr.next()
                    for u in range(8):
                        k.tr(pt.t[:, u * 128:(u + 1) * 128], obf.t[:, h8 + u, :], ident.t[:], [obf, ident], [pt])
                    k.cp("act", oTsb.t[:, h8:h8 + 8, :], pt.t[:, :].rearrange("p (a q) -> p a q", a=8), [pt], [oTsb])
                k.dma(oT_s.t[j], oTsb.t[:, :, :], [oTsb], [oT_s])


            ctx_cur = tile_load_score(0)
            for f_ in topk_closures(0):
                f_()
            build_mask(0)
            for j in range(8):
                pending_tk = []
                if j + 1 < 8:
                    ctx_next = tile_load_score(j + 1)
                    pending_tk = topk_closures(j + 1)

                def tick(n=2):
                    for _ in range(n):
                        if pending_tk:
                            pending_tk.pop(0)()
                tick_hook[0] = tick
                attention_part(j, ctx_cur)
                tick_hook[0] = None
                while pending_tk:
                    pending_tk.pop(0)()
                if j + 1 < 8:
                    build_mask(j + 1)
                    ctx_cur = ctx_next

        NPG = 64
        with P.scope():
            ptab_i = P.sb("ptab_i", [128, 4 * NPG], I32)
            k.dma(ptab_i.t[:, :], ptab.t.broadcast_to([128, 4 * NPG]), [ptab], [ptab_i])
            ptab_f = P.sb("ptab_f", [128, 4 * NPG], F32)
            k.cp("dve", ptab_f.t[:, :], ptab_i.t[:, :], [ptab_i], [ptab_f])
            pcol = P.sb("pcol", [128, 1], F32)
            P.op("pool", lambda e: e.iota(pcol.t[:, :], pattern=[[0, 1]], base=0, channel_multiplier=1,
                                          allow_small_or_imprecise_dtypes=True), (), [pcol])
            offs = P.sb("offs", [128, 4 * NPG], I32)
            k.ts("dve", offs.t[:, :], ptab_f.t[:, :], 128.0, pcol.t[:, 0:1], ALU.mult, ALU.add, [ptab_f, pcol], [offs])
            qn8 = P.sb("qn8", [128, 16, 128], BF16)
            qd8 = P.sb("qd8", [128, 16, 128], BF16)
            qi8 = P.sb("qi8", [128, 8, 128], BF16)
            k.dma(qn8.t[:], qT_n.t[8], [qT_n], [qn8])
            k.dma(qd8.t[:], qT_d.t[8], [qT_d], [qd8])
            k.dma(qi8.t[:], qiT.t[8], [qiT], [qi8])
            qns = P.sb("qns", [128, 4, 16], BF16)
            qds = P.sb("qds", [128, 4, 16], BF16)
            k.cp("dve", qns.t[:, :, :], qn8.t[:, :, 0:4].rearrange("p h s -> p s h"), [qn8], [qns])
            k.cp("dve", qds.t[:, :, :], qd8.t[:, :, 0:4].rearrange("p h s -> p s h"), [qd8], [qds])
            Qblk = P.sb("Qblk", [128, 4, 8, 2], BF16)
            k.memset("dve", Qblk.t[:], 0.0, [Qblk])
            for hf in range(2):
                k.cp("dve", Qblk.t[hf * 64:(hf + 1) * 64, :, :, hf],
                     qi8.t[hf * 64:(hf + 1) * 64, :, 0:4].rearrange("p a s -> p s a"), [qi8, Qblk], [Qblk])
            gateT = P.sb("gateT", [8, 4, 2, 3], F32)
            P.dma(lambda e: e.dma_start(out=gateT.t[:], in_=gn_s.t[8, 0:4, :].rearrange("s (g r b) -> r s g b", g=2, r=8),
                                        allow_slow_non_contiguous=True), [gn_s], [gateT])
            wiT = P.sb("wiT", [16, 4], F32)
            P.dma(lambda e: e.dma_start(out=wiT.t[:], in_=wi_s.t[8, 0:4, :].rearrange("s h -> h s"),
                                        allow_slow_non_contiguous=True), [wi_s], [wiT])
            W4 = P.sb("W4", [16, 4, 4], BF16)
            k.memset("dve", W4.t[:], 0.0, [W4])
            for s in range(4):
                k.cp("dve", W4.t[:, s, s:s + 1], wiT.t[:, s:s + 1], [wiT, W4], [W4])
            ones14 = P.sb("ones14", [4, 4], BF16)
            k.cp("dve", ones14.t[:, :], ident.t[0:4, 0:4], [ident], [ones14])
            one11 = P.sb("one11", [1, 1], BF16)
            k.memset("dve", one11.t[:], 1.0, [one11])
            e1col = P.sb("e1col", [128, 1], BF16)
            k.ts("dve", e1col.t[:, :], pcol.t[:, :], 0.0, None, ALU.is_equal, None, [pcol], [e1col])
            maskD = P.sb("maskD", [128, NPG, 4], BF16)
            maskS = P.sb("maskS", [128, NPG, 8], BF16)
            ockeep = P.sb("ockeep", [8, 8, 129], F32)
            selbT = P.sb("selbT", [128, 8], BF16)
            on_all = P.sb("on_all", [8, 4, 2, 128], F32)
            od_all = P.sb("od_all", [4, 4, 4, 128], F32)
            oT8 = P.sb("oT8", [128, 32, 128], BF16)
            k.memset("pool", oT8.t[:], 0.0, [oT8])
            pg_ring = Ring([P.sb("pg", [128, 1024], BF16) for i in range(8)])
            ktp_ring = Ring([P.sb("ktp", [128, 512], BF16) for i in range(4)])
            es_ring = Ring([P.sb("es", [128, 16], BF16) for i in range(4)])
            rd8_ring = Ring([P.sb("rd8", [8, 4], F32) for i in range(4)])
            v1n_ring = Ring([P.sb("v1n", [128, 4, 136], BF16) for i in range(3)])
            for v1_ in v1n_ring.bufs:
                k.memset("dve", v1_.t[:, :, 128:136], 1.0, [v1_])

            def gather(pool_buf, s, j, ncols):
                pg = pg_ring.next()
                P.dma(lambda e: e.indirect_dma_start(
                    out=pg.t[:, 0:ncols], out_offset=None, in_=pool_buf.t,
                    in_offset=bass.IndirectOffsetOnAxis(ap=offs.t[:, s * NPG + j:s * NPG + j + 1], axis=0)),
                    [pool_buf, offs], [pg], eng="pool")
                return pg

            def newrow_page(src_buf, s, ncols):
                pg = pg_ring.next()
                k.memset("dve", pg.t[:, 0:ncols], 0.0, [pg])
                k.dma(pg.t[0:1, 0:ncols], src_buf.t[s:s + 1, 0:ncols], [src_buf], [pg], eng="pool")
                return pg

            with P.scope():
                isc = P.sb("isc", [4, 8704], F32)
                k.memset("dve", isc.t[:, :], -1e30, [isc])
                selm4 = P.sb("selm4", [4, 8704], BF16)
                rls_ring = Ring([P.sb("rls", [16, 512], BF16) for i in range(3)])
                psi_ap = pacc.t[0:4, 0:2, :].rearrange("p a b -> p (a b)")
                for c in range(17):
                    ncol = 512 if c < 16 else 1
                    for s in range(4):
                        pt = ptr.next()
                        if c < 16:
                            for u in range(4):
                                pg = gather(c_didx, s, c * 4 + u, 64)
                                k.cp("dve", pg.t[:, 64:128], pg.t[:, 0:64], [pg], [pg])
                                k.tr(pt.t[:, u * 128:(u + 1) * 128], pg.t[:, 0:128], ident.t[:], [pg, ident], [pt])
                        else:
                            pg = newrow_page(snew_idx, s, 64)
                            k.cp("dve", pg.t[:, 64:128], pg.t[:, 0:64], [pg], [pg])
                            k.tr(pt.t[:, 0:128], pg.t[:, 0:128], ident.t[:], [pg, ident], [pt])
                        ktp = ktp_ring.next()
                        k.cp("act", ktp.t[:, 0:ncol], pt.t[:, 0:ncol], [pt], [ktp])
                        ps = pmm.next()
                        k.mm(ps.t[0:16, 0:ncol], Qblk.t[:, s, :, :].rearrange("p a b -> p (a b)"), ktp.t[:, 0:ncol],
                             True, True, [Qblk, ktp], [ps])
                        rl = rls_ring.next()
                        k.act(rl.t[:, 0:ncol], ps.t[0:16, 0:ncol], AF.Relu, [ps], [rl])
                        k.mm(psi_ap[:, 0:ncol], W4.t[:, s, :], rl.t[:, 0:ncol], s == 0, s == 3, [W4, rl], [pacc])
                    k.cp("act", isc.t[:, c * 512:c * 512 + ncol], psi_ap[:, 0:ncol], [pacc], [isc])
                m8s = Ring([P.sb("m8s", [4, 8], F32) for i in range(4)])
                for it in range(32):
                    m8 = m8s.next()
                    P.op("dve", lambda e, m8=m8: e.max(out=m8.t[:, :], in_=isc.t[:, :]), [isc], [m8])
                    P.op("dve", lambda e, m8=m8: e.match_replace(out=isc.t[:, :], in_to_replace=m8.t[:, :],
                                                                in_values=isc.t[:, :], imm_value=-3e38), [isc, m8], [isc])
                k.ts("dve", selm4.t[:, :], isc.t[:, :], -1e37, None, ALU.is_lt, None, [isc], [selm4])
                psm = pmm.next()
                for j in range(NPG):
                    k.mm(psm.t[:, j * 4:(j + 1) * 4], selm4.t[:, j * 128:(j + 1) * 128], ones14.t[:, :], True, True,
                         [selm4, ones14], [psm], sgc=True)
                k.cp("act", maskD.t[:, :, :], psm.t[:, 0:256].rearrange("p (j s) -> p j s", s=4), [psm], [maskD])

            with P.scope():
                w1sb = [P.sb("w1sbs", [128, 32, 256], BF16) for c in range(2)]
                w2sb = P.sb("w2sbs", [128, 2, 2, 128], BF16)
                posb = P.sb("posbs", [32, 2, 128], BF16)
                posT = P.sb("posTs", [128, 2, 32], BF16)
                bias_sb = P.sb("bias_sbs", [128, 4], F32)
                hidT = P.sb("hidTs", [128, 2, 512], BF16)
                kvTs = P.sb("kvTs", [128, 4, 8192], BF16)
                kcTs = P.sb("kcTs", [128, 512], BF16)
                vc1s = P.sb("vc1s", [128, 4, 136], BF16)
                ovls = P.sb("ovls", [128, 4, 132], BF16)
                ovl_vs = P.sb("ovl_vs", [128, 132], F32)
                ovl_as = P.sb("ovl_as", [128, 132], F32)
                iota_j2 = P.sb("iota_j2", [1, 132], F32)
                sws = [P.sb("sws%d" % i, [1, 132], F32) for i in range(4)]
                selrow = P.sb("selrow", [1, 132], BF16)
                impu_s = P.sb("impu_s", [8, 132], BF16)
                rdc = P.sb("rdc", [8, 1], BF16)
                k.memset("dve", hidT.t[:], 0.0, [hidT])
                k.memset("dve", kcTs.t[:], 0.0, [kcTs])
                k.memset("dve", vc1s.t[:], 0.0, [vc1s])
                P.op("pool", lambda e: e.iota(iota_j2.t[:, :], pattern=[[1, 132]], base=0, channel_multiplier=0,
                                              allow_small_or_imprecise_dtypes=True), (), [iota_j2])
                for c in range(2):
                    k.dma(w1sb[c].t[:], cmp_w1.t[c].rearrange("l d h -> d l h"), [cmp_w1], [w1sb[c]], eng="pool")
                k.dma(w2sb.t[:], cmp_w2.t.rearrange("c (hh p) d -> p c hh d", p=128), [cmp_w2], [w2sb], eng="pool")
                k.dma(posb.t[:], cmp_pos.t.rearrange("c l d -> l c d"), [cmp_pos], [posb], eng="pool")
                for c in range(2):
                    pt = ptr.next()
                    k.tr(pt.t[:, 0:32], posb.t[:, c, :], ident.t[0:32, 0:32], [posb, ident], [pt])
                    k.cp("act", posT.t[:, c, :], pt.t[:, 0:32], [pt], [posT])
                for c in range(2):
                    for hh in range(2):
                        ps = pmm.next()
                        for l in range(32):
                            k.mm(ps.t[:, 0:1], w1sb[c].t[:, l, hh * 128:(hh + 1) * 128], posT.t[:, c, l:l + 1],
                                 l == 0, l == 31, [w1sb[c], posT], [ps])
                        k.cp("act", bias_sb.t[:, c * 2 + hh:c * 2 + hh + 1], ps.t[:, 0:1], [ps], [bias_sb])
                for nt in range(4):
                    P.op("pool", lambda e, nt=nt: e.iota(ovl_vs.t[:], pattern=[[4, 132]], base=-128 * nt,
                                                         channel_multiplier=-1, allow_small_or_imprecise_dtypes=True),
                         (), [ovl_vs])
                    k.ts("dve", ovl_as.t[:], ovl_vs.t[:], -3.0, None, ALU.is_ge, None, [ovl_vs], [ovl_as])
                    k.stt("dve", ovls.t[:, nt, :], ovl_vs.t[:], 1.0, ovl_as.t[:], ALU.is_le, ALU.mult,
                          [ovl_vs, ovl_as], [ovls])
                padm = P.sb("padm", [128, 1], BF16)
                k.ts("dve", padm.t[:, :], pcol.t[:, :], 127.0, None, ALU.is_lt, None, [pcol], [padm])
                for s in range(4):
                    for j in range(NPG):
                        pg = gather(c_cmp, s, j, 512)
                        pt = ptr.next()
                        for f in range(4):
                            k.tr(pt.t[:, f * 128:(f + 1) * 128], pg.t[:, f * 128:(f + 1) * 128], ident.t[:], [pg, ident], [pt])
                        k.cp("act", kvTs.t[:, :, j * 128:(j + 1) * 128], pt.t[:, 0:512].rearrange("p (f t) -> p f t", f=4),
                             [pt], [kvTs])
                    for g in range(2):
                        for c in range(2):
                            for hh in range(2):
                                ps = pmm.next()
                                for l in range(32):
                                    k.mm(ps.t[:, 0:511], w1sb[c].t[:, l, hh * 128:(hh + 1) * 128],
                                         kvTs.t[:, g * 2 + c, l:l + 16 * 510 + 1:16], l == 0, l == 31, [w1sb[c], kvTs], [ps])
                                k.act(hidT.t[:, hh, 0:511], ps.t[:, 0:511], AF.Relu, [ps, bias_sb], [hidT],
                                      bias=bias_sb.t[:, c * 2 + hh:c * 2 + hh + 1])
                            if c == 0:
                                ps = pmm.next()
                                for hh in range(2):
                                    k.mm(ps.t[:, 0:511], w2sb.t[:, 0, hh, :], hidT.t[:, hh, 0:511], hh == 0, hh == 1,
                                         [w2sb, hidT], [ps])
                                k.cp("act", kcTs.t[:, 0:511], ps.t[:, 0:511], [ps], [kcTs])
                            else:
                                ps = pmm.next()
                                for nt in range(4):
                                    for hh in range(2):
                                        k.mm(ps.t[:, nt * 128:(nt + 1) * 128], hidT.t[:, hh, nt * 128:(nt + 1) * 128],
                                             w2sb.t[:, 1, hh, :], hh == 0 and nt == 0, hh == 1, [w2sb, hidT], [ps], sgc=True)
                                k.cp("act", vc1s.t[:, :, 0:128], ps.t[:, 0:512].rearrange("p (a d) -> p a d", a=4), [ps], [vc1s])
                                k.memset("dve", vc1s.t[:, :, 128:129], 1.0, [vc1s])
                        m = s * 2 + g
                        qap = qns.t[:, s, g * 8:(g + 1) * 8]
                        for nt in range(4):
                            ps = pmm.next()
                            k.mm(ps.t[:, 0:8], kcTs.t[:, nt * 128:(nt + 1) * 128], qap, True, True, [kcTs, qns], [ps])
                            eb = es_ring.next()
                            k.act(eb.t[:, 0:8], ps.t[:, 0:8], AF.Exp, [ps], [eb], scale=SCALE)
                            if nt == 3:
                                k.ts("dve", eb.t[:, 0:8], eb.t[:, 0:8], padm.t[:, 0:1], None, ALU.mult, None, [eb, padm], [eb])
                            k.mm(pacc.t[0:8, 0, 0:129], eb.t[:, 0:8], vc1s.t[:, nt, 0:129], nt == 0, nt == 3, [eb, vc1s], [pacc],
                                 sgc=True)
                            k.mm(pacc.t[0:8, 2, 0:129], eb.t[:, 0:8], ovls.t[:, nt, 0:129], nt == 0, nt == 3, [eb, ovls], [pacc],
                                 sgc=True)
                        k.cp("act", ockeep.t[:, m, :], pacc.t[0:8, 0, 0:129], [pacc], [ockeep])
                        rd = rd8_ring.next()
                        k.ts("dve", rd.t[:, 0:1], ockeep.t[:, m, 128:129], 1e-30, None, ALU.max, None, [ockeep], [rd])
                        P.op("dve", lambda e, rd=rd: e.reciprocal(out=rd.t[:, 0:1], in_=rd.t[:, 0:1]), [rd], [rd])
                        k.cp("dve", rdc.t[:, :], rd.t[:, 0:1], [rd], [rdc])
                        k.cp("act", impu_s.t[:, 0:129], pacc.t[0:8, 2, 0:129], [pacc], [impu_s])
                        ps = pmm.next()
                        k.mm(ps.t[0:1, 0:129], rdc.t[:, :], impu_s.t[:, 0:129], True, True, [rdc, impu_s], [ps])
                        k.ts("dve", sws[0].t[:, 0:129], iota_j2.t[:, 0:129], 0.0, None, ALU.is_equal, None, [iota_j2], [sws[0]])
                        k.ts("dve", sws[1].t[:, 0:129], iota_j2.t[:, 0:129], 127.0, None, ALU.is_ge, None, [iota_j2], [sws[1]])
                        k.tt("dve", sws[0].t[:, 0:129], sws[0].t[:, 0:129], sws[1].t[:, 0:129], ALU.add, [sws[0], sws[1]], [sws[0]])
                        k.stt("dve", sws[1].t[:, 0:129], sws[0].t[:, 0:129], 1e4, ps.t[0:1, 0:129], ALU.mult, ALU.add,
                              [sws[0], ps], [sws[1]])
                        ma8 = P.sb("ma8", [1, 8], F32); mb8 = P.sb("mb8", [1, 8], F32)
                        P.op("dve", lambda e, ma8=ma8: e.max(out=ma8.t[:, :], in_=sws[1].t[:, 0:129]), [sws[1]], [ma8])
                        P.op("dve", lambda e, ma8=ma8: e.match_replace(out=sws[2].t[:, 0:129], in_to_replace=ma8.t[:, :],
                                                                      in_values=sws[1].t[:, 0:129], imm_value=-3e38),
                             [sws[1], ma8], [sws[2]])
                        P.op("dve", lambda e, mb8=mb8: e.max(out=mb8.t[:, :], in_=sws[2].t[:, 0:129]), [sws[2]], [mb8])
                        k.ts("dve", selrow.t[:, 0:128], sws[1].t[:, 0:128], mb8.t[:, 7:8], None, ALU.is_ge, None,
                             [sws[1], mb8], [selrow])
                        ps2 = pmm.next()
                        k.mm(ps2.t[:, 0:1], selrow.t[0:1, 0:128], one11.t[:, :], True, True, [selrow, one11], [ps2])
                        k.cp("act", selbT.t[:, m:m + 1], ps2.t[:, 0:1], [ps2], [selbT])

            with P.scope():
                ExS = P.sb("ExS", [128, NPG, 128], BF16)
                with P.scope():
                    ExV = P.sb("ExV", [128, NPG, 128], F32)
                    P.op("pool", lambda e: e.iota(ExV.t[:].rearrange("p j (b i) -> p j b i", b=2),
                                                  pattern=[[-2, NPG], [-1, 2], [0, 64]], base=0, channel_multiplier=1,
                                                  allow_small_or_imprecise_dtypes=True), (), [ExV])
                    k.ts("dve", ExS.t[:], ExV.t[:], 0.0, None, ALU.is_equal, None, [ExV], [ExS])
                psm = pmm.next()
                for j in range(NPG):
                    k.mm(psm.t[:, j * 8:(j + 1) * 8], ExS.t[:, j, :], selbT.t[:, :], True, True, [ExS, selbT], [psm], sgc=True)
                k.cp("act", maskS.t[:, :, :], psm.t[:, 0:512].rearrange("p (j m) -> p j m", m=8), [psm], [maskS])

            def finish_branch(nrow, ngrp, dst_of_g, coef_of_g, first):
                rd = rd8_ring.next()
                k.ts("dve", rd.t[0:nrow, 0:ngrp], pacc.t[0:nrow, 0:ngrp, 128], 1e-30, None, ALU.max, None, [pacc], [rd])
                P.op("dve", lambda e: e.reciprocal(out=rd.t[0:nrow, 0:ngrp], in_=rd.t[0:nrow, 0:ngrp]), [rd], [rd])
                for g in range(ngrp):
                    dap, dbuf = dst_of_g(g)
                    cf = coef_of_g(g)
                    if cf is not None:
                        cap, cbuf = cf
                        k.tt("dve", rd.t[0:nrow, g:g + 1], rd.t[0:nrow, g:g + 1], cap, ALU.mult, [rd, cbuf], [rd])
                    if first:
                        k.ts("dve", dap, pacc.t[0:nrow, g, 0:128], rd.t[0:nrow, g:g + 1], None, ALU.mult, None, [pacc, rd], [dbuf])
                    else:
                        k.stt("dve", dap, pacc.t[0:nrow, g, 0:128], rd.t[0:nrow, g:g + 1], dap, ALU.mult, ALU.add,
                              [pacc, rd, dbuf], [dbuf])

            for s in range(4):
                for g in range(2):
                    m = s * 2 + g
                    rd = rd8_ring.next()
                    k.ts("dve", rd.t[:, 0:1], ockeep.t[:, m, 128:129], 1e-30, None, ALU.max, None, [ockeep], [rd])
                    P.op("dve", lambda e, rd=rd: e.reciprocal(out=rd.t[:, 0:1], in_=rd.t[:, 0:1]), [rd], [rd])
                    k.tt("dve", rd.t[:, 0:1], rd.t[:, 0:1], gateT.t[:, s, g, 0:1], ALU.mult, [rd, gateT], [rd])
                    k.ts("dve", on_all.t[:, s, g, :], ockeep.t[:, m, 0:128], rd.t[:, 0:1], None, ALU.mult, None,
                         [ockeep, rd], [on_all])
                def run_pages(pool_buf, newsrc, ncols, **kw):
                    n = NPG + 1
                    st_ = {}

                    def get_page(idx):
                        if idx < NPG:
                            return gather(pool_buf, s, idx, ncols)
                        return newrow_page(newsrc, s, ncols)
                    st_[0] = attend_stage1(get_page(0), 0, **kw)
                    for idx in range(n):
                        if idx + 1 < n:
                            st_[idx + 1] = attend_stage1(get_page(idx + 1), idx + 1, **kw)
                        attend_stage2(st_.pop(idx), idx == 0, idx == n - 1, kw["nrow"], kw["ngrp"])

                def attend_stage1(pg, pj, nrow, ngrp, q_of_g, kcol_of_g, vcol_of_g, mask_of_page):
                    pt = ptr.next()
                    for g in range(ngrp):
                        c0 = kcol_of_g(g)
                        k.tr(pt.t[:, g * 128:(g + 1) * 128], pg.t[:, c0:c0 + 128], ident.t[:], [pg, ident], [pt])
                    ktp = ktp_ring.next()
                    k.cp("act", ktp.t[:, 0:ngrp * 128], pt.t[:, 0:ngrp * 128], [pt], [ktp])
                    ps = pmm.next()
                    for g in range(ngrp):
                        qap, qb = q_of_g(g)
                        k.mm(ps.t[:, g * nrow:(g + 1) * nrow], ktp.t[:, g * 128:(g + 1) * 128], qap, True, True,
                             [ktp, qb], [ps], sgc=True)
                    eb = es_ring.next()
                    k.act(eb.t[:, 0:ngrp * nrow], ps.t[:, 0:ngrp * nrow], AF.Exp, [ps], [eb], scale=SCALE)
                    mk = mask_of_page(pj)
                    if mk is not None:
                        map_, mbuf = mk
                        ev = eb.t[:, 0:ngrp * nrow].rearrange("p (g r) -> p g r", g=ngrp)
                        k.tt("dve", ev, ev, map_, ALU.mult, [eb, mbuf], [eb])
                    v1 = v1n_ring.next()
                    for g in range(ngrp):
                        c0 = vcol_of_g(g)
                        k.cp("dve" if g % 2 == 0 else "act", v1.t[:, g, 0:128], pg.t[:, c0:c0 + 128], [pg], [v1])
                    return eb, v1

                def attend_stage2(stv, first, last, nrow, ngrp):
                    eb, v1 = stv
                    for g in range(ngrp):
                        k.mm(pacc.t[0:nrow, g, 0:129], eb.t[:, g * nrow:(g + 1) * nrow], v1.t[:, g, 0:129],
                             first and g % 2 == 0, last, [eb, v1], [pacc], sgc=True)

                def mask_sel(pj, s=s):
                    if pj < NPG:
                        return (maskS.t[:, pj, s * 2:s * 2 + 2].unsqueeze(2).broadcast_to([128, 2, 8]), maskS)
                    return (e1col.t[:, 0:1].unsqueeze(1).broadcast_to([128, 2, 8]), e1col)
                run_pages(c_sel, snew_sel, 512, nrow=8, ngrp=2,
                          q_of_g=lambda g, s=s: (qns.t[:, s, g * 8:(g + 1) * 8], qns),
                          kcol_of_g=lambda g: g * 256, vcol_of_g=lambda g: g * 256 + 128, mask_of_page=mask_sel)
                finish_branch(8, 2, lambda g, s=s: (on_all.t[:, s, g, :], on_all),
                              lambda g, s=s: (gateT.t[:, s, g, 1:2], gateT), False)
                for wi_ in range(4):
                    pg = pg_ring.next()
                    k.dma(pg.t[:, 0:512], o_win_s.t[s, wi_ * 128:(wi_ + 1) * 128, :], [o_win_s], [pg], eng="pool")
                    stv = attend_stage1(pg, wi_, nrow=8, ngrp=2,
                                        q_of_g=lambda g, s=s: (qns.t[:, s, g * 8:(g + 1) * 8], qns),
                                        kcol_of_g=lambda g: g * 256, vcol_of_g=lambda g: g * 256 + 128,
                                        mask_of_page=lambda pj: None)
                    attend_stage2(stv, wi_ == 0, wi_ == 3, 8, 2)
                finish_branch(8, 2, lambda g, s=s: (on_all.t[:, s, g, :], on_all),
                              lambda g, s=s: (gateT.t[:, s, g, 2:3], gateT), False)

                def mask_dsa(pj, s=s):
                    if pj < NPG:
                        return (maskD.t[:, pj, s:s + 1].unsqueeze(1).broadcast_to([128, 4, 4]), maskD)
                    return (e1col.t[:, 0:1].unsqueeze(1).broadcast_to([128, 4, 4]), e1col)
                run_pages(c_dkv, snew_dsa, 1024, nrow=4, ngrp=4,
                          q_of_g=lambda g, s=s: (qds.t[:, s, g * 4:(g + 1) * 4], qds),
                          kcol_of_g=lambda g: g * 256, vcol_of_g=lambda g: g * 256 + 128, mask_of_page=mask_dsa)
                finish_branch(4, 4, lambda g, s=s: (od_all.t[:, s, g, :], od_all), lambda g: None, True)

            onb = P.sb("onb", [8, 4, 2, 128], BF16)
            odb = P.sb("odb", [4, 4, 4, 128], BF16)
            k.cp("dve", onb.t[:], on_all.t[:], [on_all], [onb])
            k.cp("dve", odb.t[:], od_all.t[:], [od_all], [odb])
            for s in range(4):
                pt = ptr.next()
                for g in range(2):
                    k.tr(pt.t[:, g * 8:(g + 1) * 8], onb.t[:, s, g, :], ident.t[0:8, 0:8], [onb, ident], [pt])
                for g in range(4):
                    k.tr(pt.t[:, 16 + g * 4:16 + (g + 1) * 4], odb.t[:, s, g, :], ident.t[0:4, 0:4], [odb, ident], [pt])
                k.cp("act", oT8.t[:, :, s], pt.t[:, 0:32], [pt], [oT8])
            k.dma(oT_s.t[8], oT8.t[:, :, :], [oT8], [oT_s])

        NTOK = NSLOT * 128

        def gen_norm(rows_ap, src_buf, slot, hTt, hTd, gainT, xt_ring, xn, ssq):
            xt = xt_ring.next(); sq = ssq.next()
            k.dma(xt.t[:], rows_ap, [src_buf], [xt])
            k.act(xn.t[:], xt.t[:], AF.Square, [xt], [xn, sq], accum_out=sq.t[:])
            k.ts("dve", sq.t[:], sq.t[:], 1.0 / D, 1e-6, ALU.mult, ALU.add, [sq], [sq])
            k.act(sq.t[:], sq.t[:], AF.Sqrt, [sq], [sq])
            P.op("dve", lambda e: e.reciprocal(out=sq.t[:], in_=sq.t[:]), [sq], [sq])
            k.ts("dve", xn.t[:], xt.t[:], sq.t[:, 0:1], None, ALU.mult, None, [xt, sq], [xn])
            for kc4 in range(8):
                pt = ptr.next()
                for u in range(4):
                    kc = kc4 * 4 + u
                    k.tr(pt.t[:, u * 128:(u + 1) * 128], xn.t[:, kc * 128:(kc + 1) * 128], ident.t[:], [xn, ident], [pt])
                for u in range(4):
                    kc = kc4 * 4 + u
                    k.act(hTt.t[:, kc, slot * 128:(slot + 1) * 128], pt.t[:, u * 128:(u + 1) * 128], AF.Copy,
                          [pt, gainT], [hTd[slot]], scale=gainT.t[:, kc:kc + 1])

        def gen_load_w(ring, wbuf, r0, nkc, c0, ncols):
            w = ring.next()
            if not hasattr(w, "hb"):
                w.hb = Buf(w.t)
            src = wbuf.t[r0:r0 + nkc * 128, c0:c0 + ncols].rearrange("(kc p) c -> p kc c", p=128)
            h = max(1, nkc // 2)
            w.half = h
            for a in range(0, nkc, h):
                k.dma(w.t[:, a:a + h, 0:ncols], src[:, a:a + h, :], [wbuf], [w if a == 0 else w.hb], eng="pool")
            return w

        def wd_(w, kc):
            return w if kc < w.half else w.hb

        with P.scope():
            mixT = P.sb("mixT", [128, 32, NTOK], BF16)
            mix_d = [Buf(None) for _ in range(3)]
            with P.scope():
                oT_all = P.sb("oT_all", [128, 32, NTOK], BF16)
                for slot in range(NSLOT):
                    k.dma(oT_all.t[:, :, slot * 128:(slot + 1) * 128], oT_s.t[slot], [oT_s], [oT_all])
                wb_ring = Ring([P.sb("wb", [128, 32, 128], BF16) for i in range(2)])
                gm_ring = Ring([P.sb("gmb", [128, 2, NTOK], BF16) for i in range(2)])
                t_ring = Ring([P.sb("mt", [128, 384], F32) for i in range(4)])

                def load_c1(c):
                    wb = wb_ring.next(); gb = gm_ring.next()
                    k.dma(wb.t[:, 0:16, :], w_bn.t[:, c * 128:(c + 1) * 128].rearrange("(kc p) c -> p kc c", p=128),
                          [w_bn], [wb], eng="pool")
                    k.dma(wb.t[:, 16:32, :], w_bd.t[:, c * 128:(c + 1) * 128].rearrange("(kc p) c -> p kc c", p=128),
                          [w_bd], [wb], eng="pool")
                    k.dma(gb.t[:, 0, :], gmT.t[c], [gmT], [gb])
                    k.dma(gb.t[:, 1, :], gmT.t[32 + c], [gmT], [gb])
                    return wb, gb
                nxt = load_c1(0)
                for c in range(32):
                    wb, gb = nxt
                    if c + 1 < 32:
                        nxt = load_c1(c + 1)
                    for tc in range(3):
                        tsl = slice(tc * 384, (tc + 1) * 384)
                        psn = pmm.next(); psd = pmm.next()
                        for kc in range(16):
                            k.mm(psn.t[:, 0:384], wb.t[:, kc, :], oT_all.t[:, kc, tsl], kc == 0, kc == 15, [wb, oT_all], [psn])
                        for kc in range(16, 32):
                            k.mm(psd.t[:, 0:384], wb.t[:, kc, :], oT_all.t[:, kc, tsl], kc == 16, kc == 31, [wb, oT_all], [psd])
                        t1 = t_ring.next(); t2 = t_ring.next()
                        k.tt("dve", t1.t[:, :], psn.t[:, 0:384], gb.t[:, 0, tsl], ALU.mult, [psn, gb], [t1])
                        k.tt("dve", t2.t[:, :], psd.t[:, 0:384], gb.t[:, 1, tsl], ALU.mult, [psd, gb], [t2])
                        k.tt("pool", mixT.t[:, c, tsl], t1.t[:, :], t2.t[:, :], ALU.add, [t1, t2], [mix_d[tc]])
            with P.scope():
                w_ring = Ring([P.sb("wo", [128, 32, 512], BF16) for i in range(2)])
                xc_ring = Ring([P.sb("xc", [128, 512], F32) for i in range(3)])
                nxt = gen_load_w(w_ring, w_out, 0, 32, 0, 512)
                for n in range(8):
                    w = nxt
                    if n + 1 < 8:
                        nxt = gen_load_w(w_ring, w_out, 0, 32, (n + 1) * 512, 512)
                    for slot in range(NSLOT):
                        ps = pmm.next()
                        for kc in range(32):
                            k.mm(ps.t[:, 0:512], mixT.t[:, kc, slot * 128:(slot + 1) * 128], w.t[:, kc, :], kc == 0, kc == 31,
                                 [mix_d[slot // 3], wd_(w, kc)], [ps])
                        xc = xc_ring.next()
                        k.dma(xc.t[:, :], xq.t[slot * 128:(slot + 1) * 128, n * 512:(n + 1) * 512], [xq], [xc])
                        k.tt("dve", xc.t[:, :], xc.t[:, :], ps.t[:, 0:512], ALU.add, [xc, ps], [xc])
                        k.dma(x1s.t[slot * 128:(slot + 1) * 128, n * 512:(n + 1) * 512], xc.t[:, :], [xc], [x1s])

        with P.scope():
            hT2 = P.sb("hT2", [128, 32, NTOK], BF16)
            hT2_d = [Buf(None) for _ in range(NSLOT)]
            gT2 = P.sb("gT2", [128, 32], F32)
            k.dma(gT2.t[:], norm_mlp.t, [norm_mlp], [gT2])
            with P.scope():
                xt_ring = Ring([P.sb("xt", [128, D], F32) for i in range(2)])
                xn = P.sb("xn", [128, D], BF16)
                ssq = Ring([P.sb("ssq", [128, 1], F32) for i in range(2)])
                for slot in range(NSLOT):
                    gen_norm(x1s.t[slot * 128:(slot + 1) * 128, :], x1s, slot, hT2, hT2_d, gT2, xt_ring, xn, ssq)
            with P.scope():
                w_ring = Ring([P.sb("wu", [128, 32, 512], BF16) for i in range(2)])
                r_ring = Ring([P.sb("rr", [128, 384], F32) for i in range(3)])
                u_ring = Ring([P.sb("ub", [128, 384], BF16) for i in range(3)])
                nxt = gen_load_w(w_ring, w_up, 0, 32, 0, 512)
                ecnt = 0
                for fg in range(32):
                    w = nxt
                    if fg + 1 < 32:
                        nxt = gen_load_w(w_ring, w_up, 0, 32, (fg + 1) * 512, 512)
                    for cc in range(4):
                        f = fg * 4 + cc
                        for tc in range(3):
                            ps = pmm.next()
                            for kc in range(32):
                                k.mm(ps.t[:, 0:384], w.t[:, kc, cc * 128:(cc + 1) * 128], hT2.t[:, kc, tc * 384:(tc + 1) * 384],
                                     kc == 0, kc == 31, [wd_(w, kc)] + hT2_d[tc * 3:tc * 3 + 3], [ps])
                            rr = r_ring.next(); ub = u_ring.next()
                            k.act(rr.t[:, :], ps.t[:, 0:384], AF.Relu, [ps], [rr])
                            k.tt("dve" if ecnt % 2 == 0 else "pool", ub.t[:, :], rr.t[:, :], rr.t[:, :], ALU.mult, [rr], [ub])
                            ecnt += 1
                            k.dma(uT_s.t[f, :, tc * 384:(tc + 1) * 384], ub.t[:, :], [ub], [uT_s])
        with P.scope():
            wd_ring = Ring([P.sb("wd", [128, 16, 256], BF16) for i in range(2)])
            ub_ring = Ring([P.sb("ubk", [128, 16, NTOK], BF16) for i in range(2)])
            xc_ring = Ring([P.sb("xc2", [128, 256], F32) for i in range(3)])

            def acc_region(slot):
                if slot < 8:
                    b_ = pmm.bufs[slot // 2]
                    return b_, b_.t[:, (slot % 2) * 256:(slot % 2) * 256 + 256]
                return pacc, pacc.t[:, 0, :]

            def load_c2(n, kq):
                wd = gen_load_w(wd_ring, w_down, kq * 2048, 16, n * 256, 256)
                ubk = ub_ring.next()
                k.dma(ubk.t[:, :, :], uT_s.t[kq * 16:(kq + 1) * 16].rearrange("f p t -> p f t"), [uT_s], [ubk])
                return wd, ubk
            seq = [(n, kq) for n in range(16) for kq in range(8)]
            nxt = load_c2(*seq[0])
            for si, (n, kq) in enumerate(seq):
                wd, ubk = nxt
                if si + 1 < len(seq):
                    nxt = load_c2(*seq[si + 1])
                for slot in range(NSLOT):
                    rb, rap = acc_region(slot)
                    for fc in range(16):
                        first = (kq == 0 and fc == 0)
                        k.mm(rap, ubk.t[:, fc, slot * 128:(slot + 1) * 128], wd.t[:, fc, :],
                             first and (slot % 2 == 0), kq == 7 and fc == 15, [ubk, wd_(wd, fc)], [rb], sgc=True)
                if kq == 7:
                    for slot in range(NSLOT):
                        rb, rap = acc_region(slot)
                        xc = xc_ring.next()
                        k.dma(xc.t[:, :], x1s.t[slot * 128:(slot + 1) * 128, n * 256:(n + 1) * 256], [x1s], [xc])
                        k.tt("dve", xc.t[:, :], xc.t[:, :], rap, ALU.add, [xc, rb], [xc])
                        k.dma(x2s.t[slot * 128:(slot + 1) * 128, n * 256:(n + 1) * 256], xc.t[:, :], [xc], [x2s])

        with P.scope():
            hT3 = P.sb("hT3", [128, 32, NTOK], BF16)
            hT3_d = [Buf(None) for _ in range(NSLOT)]
            pT = P.sb("pT", [128, 2, NTOK], BF16)
            gT3 = P.sb("gT3", [128, 32], F32)
            k.dma(gT3.t[:], norm_ple.t, [norm_ple], [gT3])
            ssx = P.sb("ssx", [128, NSLOT, 8], F32)
            with P.scope():
                xt_ring = Ring([P.sb("xt", [128, D], F32) for i in range(2)])
                xn = P.sb("xn", [128, D], BF16)
                ssq = Ring([P.sb("ssq", [128, 1], F32) for i in range(2)])
                pin = Ring([P.sb("pin", [128, 256], BF16) for i in range(2)])
                for slot in range(NSLOT):
                    gen_norm(x2s.t[slot * 128:(slot + 1) * 128, :], x2s, slot, hT3, hT3_d, gT3, xt_ring, xn, ssq)
                    pb = pin.next()
                    k.dma(pb.t[:, :], pq.t[slot * 128:(slot + 1) * 128, :], [pq], [pb], eng="pool")
                    pt = ptr.next()
                    for u in range(2):
                        k.tr(pt.t[:, u * 128:(u + 1) * 128], pb.t[:, u * 128:(u + 1) * 128], ident.t[:], [pb, ident], [pt])
                    k.cp("act", pT.t[:, :, slot * 128:(slot + 1) * 128], pt.t[:, 0:256].rearrange("p (a q) -> p a q", a=2),
                         [pt], [pT])
            with P.scope():
                w_ring = Ring([P.sb("wg", [128, 32, 512], BF16) for i in range(2)])
                wp_ring = Ring([P.sb("wp", [128, 2, 512], BF16) for i in range(2)])
                xc_ring = Ring([P.sb("xc3", [128, 512], F32) for i in range(3)])
                g_ring = Ring([P.sb("gt", [128, 512], F32) for i in range(3)])
                junk = P.sb("junk", [128, 512], BF16)

                def load_c3(n):
                    return (gen_load_w(w_ring, w_pg, 0, 32, n * 512, 512), gen_load_w(wp_ring, w_ple, 0, 2, n * 512, 512))
                nxt = load_c3(0)
                for n in range(8):
                    w, wp = nxt
                    if n + 1 < 8:
                        nxt = load_c3(n + 1)
                    for slot in range(NSLOT):
                        psg = pmm.next(); psp = pmm.next()
                        for kc in range(32):
                            k.mm(psg.t[:, 0:512], hT3.t[:, kc, slot * 128:(slot + 1) * 128], w.t[:, kc, :], kc == 0, kc == 31,
                                 [hT3_d[slot], wd_(w, kc)], [psg])
                        for k2 in range(2):
                            k.mm(psp.t[:, 0:512], pT.t[:, k2, slot * 128:(slot + 1) * 128], wp.t[:, k2, :], k2 == 0, k2 == 1,
                                 [pT, wd_(wp, k2)], [psp])
                        gt = g_ring.next(); xc = xc_ring.next()
                        k.act(gt.t[:, :], psg.t[:, 0:512], AF.Sigmoid, [psg], [gt])
                        k.tt("dve", gt.t[:, :], gt.t[:, :], psp.t[:, 0:512], ALU.mult, [gt, psp], [gt])
                        k.dma(xc.t[:, :], x2s.t[slot * 128:(slot + 1) * 128, n * 512:(n + 1) * 512], [x2s], [xc])
                        k.tt("pool", xc.t[:, :], xc.t[:, :], gt.t[:, :], ALU.add, [xc, gt], [xc])
                        k.act(junk.t[:, :], xc.t[:, :], AF.Square, [xc], [junk, ssx], accum_out=ssx.t[:, slot, n:n + 1])
                        k.dma(x3s.t[slot * 128:(slot + 1) * 128, n * 512:(n + 1) * 512], xc.t[:, :], [xc], [x3s])
            with P.scope():
                gfin = P.sb("gfin", [128, D], F32)
                k.dma(gfin.t[:, :], norm_final.t.broadcast_to([128, D]), [norm_final], [gfin])
                xt_ring = Ring([P.sb("xt", [128, D], F32) for i in range(2)])
                rs = P.sb("rs", [128, NSLOT], F32)
                P.op("dve", lambda e: e.tensor_reduce(out=rs.t[:, :], in_=ssx.t[:, :, :], axis=AX.X, op=ALU.add), [ssx], [rs])
                k.ts("dve", rs.t[:, :], rs.t[:, :], 1.0 / D, 1e-6, ALU.mult, ALU.add, [rs], [rs])
                k.act(rs.t[:, :], rs.t[:, :], AF.Sqrt, [rs], [rs])
                P.op("dve", lambda e: e.reciprocal(out=rs.t[:, :], in_=rs.t[:, :]), [rs], [rs])
                for slot in range(NSLOT):
                    xt = xt_ring.next()
                    k.dma(xt.t[:, :], x3s.t[slot * 128:(slot + 1) * 128, :], [x3s], [xt])
                    k.stt("dve", xt.t[:, :], xt.t[:, :], rs.t[:, slot:slot + 1], gfin.t[:, :], ALU.mult, ALU.mult, [xt, rs, gfin], [xt])
                    k.dma(o_y.t[slot * 128:(slot + 1) * 128, :], xt.t[:, :], [xt], [o_y])

        P.finish(outs)
    return nc


_NC_CACHE = {}


def kernel(**inputs):
    x_prompt = np.asarray(inputs["x_prompt"], dtype=np.float32)
    x_sample = np.asarray(inputs["x_sample"], dtype=np.float32)
    w_in = np.ascontiguousarray(np.asarray(inputs["w_in"], dtype=np.float32)[0])
    norm_mix = np.ascontiguousarray(np.asarray(inputs["norm_mix"], dtype=np.float32)[0].reshape(32, 128).T)
    state_win = np.asarray(inputs["state_nsa_win"], dtype=np.float32)
    cmp_pos = np.ascontiguousarray(np.asarray(inputs["cmp_pos"], dtype=np.float32)[0])
    cmp_w1 = np.ascontiguousarray(np.asarray(inputs["cmp_w1"], dtype=np.float32)[0])
    cmp_w2 = np.ascontiguousarray(np.asarray(inputs["cmp_w2"], dtype=np.float32)[0])
    g1 = lambda n: np.ascontiguousarray(np.asarray(inputs[n], dtype=np.float32)[0])
    gTl = lambda n: np.ascontiguousarray(np.asarray(inputs[n], dtype=np.float32)[0].reshape(32, 128).T)
    p_prompt = np.asarray(inputs["p_prompt"], dtype=np.float32)[0]
    p_sample = np.asarray(inputs["p_sample"], dtype=np.float32)[0]
    shared = {"w_bn": g1("w_branch_nsa"), "w_bd": g1("w_branch_dsa"), "w_out": g1("w_out"), "norm_mlp": gTl("norm_mlp"),
              "w_up": g1("w_up"), "w_down": g1("w_down"), "norm_ple": gTl("norm_ple"), "w_pg": g1("w_ple_gate"),
              "w_ple": g1("w_ple"), "norm_final": np.ascontiguousarray(np.asarray(inputs["norm_final"], dtype=np.float32)[None, :])}
    pools = {"c_cmp": np.asarray(inputs["cache_nsa_cmp"], dtype=np.float32)[0].reshape(2560 * 128, 512),
             "c_sel": np.asarray(inputs["cache_nsa_sel"], dtype=np.float32)[0].reshape(2560 * 128, 512),
             "c_dkv": np.asarray(inputs["cache_dsa_kv"], dtype=np.float32)[0].reshape(2560 * 128, 1024),
             "c_didx": np.asarray(inputs["cache_dsa_idx"], dtype=np.float32)[0].reshape(2560 * 128, 64)}
    page_table = np.asarray(inputs["page_table"]).astype(np.int32)
    if "nc" not in _NC_CACHE:
        _NC_CACHE["nc"] = build_program()
    nc = _NC_CACHE["nc"]

    invf = np.concatenate([
        (10000.0 ** (-np.arange(64, dtype=np.float32) / np.float32(64))).astype(np.float32),
        (10000.0 ** (-np.arange(32, dtype=np.float32) / np.float32(32))).astype(np.float32)])[None, :]
    in_maps = []
    for c in range(8):
        b, i = c // 4, c % 4
        xq = np.zeros((NSLOT * 128, D), np.float32)
        pos = np.zeros((128, 64), np.float32)
        cmk = np.zeros((128, 32), np.float32)
        for t in range(32):
            pos[:, t] = 128 * t + np.arange(128)
        for j in range(8):
            t = 4 * j + i
            xq[j * 128:(j + 1) * 128] = x_prompt[b, t * 128:(t + 1) * 128]
            pos[:, 32 + j] = 128 * t + np.arange(128)
            pos[:, 41 + j] = (128 * t + np.arange(128)) // 64
            cmk[:, j] = 128 * t
        xq[1024:1028] = x_sample[4 * c:4 * c + 4, 0]
        pqa = np.zeros((NSLOT * 128, 256), np.float32)
        for j in range(8):
            t = 4 * j + i
            pqa[j * 128:(j + 1) * 128] = p_prompt[b, t * 128:(t + 1) * 128]
        pqa[1024:1028] = p_sample[4 * c:4 * c + 4, 0]
        pos[:, 40] = 8192
        for rel in range(4):
            dl = i - rel
            a, bb = (0.0, 1.0) if dl > 0 else ((1.0, 0.0) if dl == 0 else (0.0, 0.0))
            cmk[:, 8 + 2 * rel] = a
            cmk[:, 9 + 2 * rel] = bb
        for m in range(8):
            dl = i - (m - 4)
            if dl == 0:
                a, bb = 1.0, 0.0
            elif 1 <= dl <= 3:
                a, bb = 0.0, 1.0
            elif dl == 4:
                a, bb = -1.0, 1.0
            else:
                a, bb = 0.0, 0.0
            cmk[:, 16 + 2 * m] = a
            cmk[:, 17 + 2 * m] = bb
        in_maps.append({"xb": np.ascontiguousarray(x_prompt[b]), "xq": xq, "pos": pos, "cmk": cmk,
                        "invf": invf.astype(np.float32), "w_in": w_in, "norm_mix": norm_mix,
                        "cmp_pos": cmp_pos, "cmp_w1": cmp_w1, "cmp_w2": cmp_w2, "pq": pqa,
                        "ptab": np.ascontiguousarray(page_table[4 * c:4 * c + 4].reshape(1, 256)),
                        "state_win": np.ascontiguousarray(state_win[0, 4 * c:4 * c + 4].reshape(4, 512, 512)),
                        **pools, **shared})
    res = run_bass_kernel_spmd(nc, in_maps, core_ids=list(range(8)))
    R = res.results
    _NC_CACHE["R"] = R

    def bcat(name, shape_tail):
        return np.stack([R[0][name], R[4][name]], 0).reshape((1, 2) + shape_tail)

    def scat(name, shape_tail):
        return np.concatenate([R[c][name] for c in range(8)], 0).reshape((1, 32, 1) + shape_tail)

    y_prompt = np.zeros((2, SEQ, D), np.float32)
    y_sample = np.zeros((32, 1, D), np.float32)
    for c in range(8):
        b, i = c // 4, c % 4
        yo = R[c]["y_own"]
        for j in range(8):
            t = 4 * j + i
            y_prompt[b, t * 128:(t + 1) * 128] = yo[j * 128:(j + 1) * 128]
        y_sample[4 * c:4 * c + 4, 0] = yo[1024:1028]
    cmp_p = bcat("cmp_p", (SEQ, 2, 2, 128))
    sel_p = bcat("sel_p", (SEQ, 2, 2, 128))
    dkv_p = bcat("dkv_p", (SEQ, 4, 2, 128))
    didx_p = bcat("didx_p", (SEQ, 64))
    win_p = bcat("win_p", (512, 2, 2, 128))
    cmp_s = scat("cmp_s", (2, 2, 128))
    sel_s = scat("sel_s", (2, 2, 128))
    dkv_s = scat("dkv_s", (4, 2, 128))
    didx_s = scat("didx_s", (64,))
    win_s = np.concatenate([R[c]["win_s"] for c in range(8)], 0).reshape(1, 32, 512, 2, 2, 128)
    return (y_prompt, y_sample, cmp_p, cmp_s, sel_p, sel_s, dkv_p, dkv_s, didx_p, didx_s, win_p, win_s)
```

```python
import math
import numpy as np
from contextlib import ExitStack
import concourse.bass as bass
import concourse.mybir as mybir
from concourse.bass_utils import run_bass_kernel_spmd

F32 = mybir.dt.float32
BF16 = mybir.dt.bfloat16
I32 = mybir.dt.int32
ALU = mybir.AluOpType
AF = mybir.ActivationFunctionType
AX = mybir.AxisListType

ENGS = ("pe", "act", "dve", "pool", "sp")
N_DMA_SEMS = 24
LAZY_INC = True
PI = math.pi

D = 4096
SEQ = 4096
NT = 32
PROJ_W = 16000
C_QN, C_KVC, C_KVS, C_KVW, C_GN, C_QD, C_KVD, C_QI, C_KI, C_WI, C_GM = (
    0, 2048, 2560, 3072, 3584, 3632, 5680, 6704, 7728, 7792, 7808)


class Dep:
    __slots__ = ("w", "r")

    def __init__(self):
        self.w = None
        self.r = {}


class Buf:
    def __init__(self, t, tracked=True):
        self.t = t
        self.d = Dep()
        self.tracked = tracked


class Ring:
    def __init__(self, bufs):
        self.bufs = bufs
        self.i = 0

    def next(self):
        b = self.bufs[self.i]
        self.i = (self.i + 1) % len(self.bufs)
        return b


class Prog:
    def __init__(self, nc, es):
        self.nc = nc
        self.es = es
        self.aes = es
        self.q = {e: [] for e in ENGS}
        self.cnt = {e: 0 for e in ENGS}
        self.seen = {e: {} for e in ENGS}
        self.sems = {}
        self.ekey = {}
        for e in ENGS:
            self.ekey[e] = (e, 0)
            self.sems[(e, 0)] = es.enter_context(nc.semaphore("s_" + e))
        self.dsem_cnt = [0] * N_DMA_SEMS
        for j in range(N_DMA_SEMS):
            self.sems[("d", j)] = es.enter_context(nc.semaphore("sd%d" % j))
        self.dnext = 0

    def sb(self, name, shape, dt):
        self.nalloc = getattr(self, "nalloc", 0) + 1
        return Buf(self.aes.enter_context(self.nc.sbuf_tensor("%s_%d" % (name, self.nalloc), list(shape), dt)))

    def barrier(self):
        snap = [(self.ekey[e], self.cnt[e]) for e in ENGS] + [(("d", j), self.dsem_cnt[j]) for j in range(N_DMA_SEMS)]
        for eng in ENGS:
            need = []
            for kk, v in snap:
                if v > self.seen[eng].get(kk, 0):
                    self.seen[eng][kk] = v
                    need.append((kk, v))
            self.q[eng].append((need, None, None))
        for e in ENGS:
            if self.cnt[e] > 12000:
                ep = self.ekey[e][1] + 1
                self.ekey[e] = (e, ep)
                self.sems[(e, ep)] = self.es.enter_context(self.nc.semaphore("s_%s_%d" % (e, ep)))
                self.cnt[e] = 0

    def scope(self):
        prog = self

        class _S:
            def __enter__(s_):
                s_.old = prog.aes
                s_.st = ExitStack()
                s_.st.__enter__()
                prog.aes = s_.st
                return s_

            def __exit__(s_, *a):
                prog.barrier()
                prog.aes = s_.old
                return s_.st.__exit__(*a)
        return _S()

    def ps(self, name, shape, dt):
        return Buf(self.es.enter_context(self.nc.psum_tensor(name, list(shape), dt)))

    def _deps(self, eng, reads, writes):
        waits = {}

        def add(kv):
            if kv is not None and kv[1] > waits.get(kv[0], 0):
                waits[kv[0]] = kv[1]

        for t in reads:
            if t.tracked:
                add(t.d.w)
        for t in writes:
            if not t.tracked:
                continue
            add(t.d.w)
            for kv in t.d.r.items():
                add(kv)
        need = []
        seen = self.seen[eng]
        for k, v in waits.items():
            if k[0] == "pe" and eng == "pe":
                continue
            if v > seen.get(k, 0):
                seen[k] = v
                need.append((k, v))
        return need

    def op(self, eng, fn, reads=(), writes=()):
        need = self._deps(eng, reads, writes)
        self.cnt[eng] += 1
        my = self.cnt[eng]
        key = self.ekey[eng]
        self.q[eng].append((need, fn, (key, 1, my)))
        for t in reads:
            t.d.r[key] = my
        for t in writes:
            t.d.w = (key, my)
            t.d.r = {}

    def dma(self, fn, reads=(), writes=(), eng="sp"):
        j = self.dnext
        self.dnext = (self.dnext + 1) % N_DMA_SEMS
        key = ("d", j)
        need = self._deps(eng, reads, writes)
        prev = self.dsem_cnt[j]
        if prev > self.seen[eng].get(key, 0):
            self.seen[eng][key] = prev
            need.append((key, prev))
        self.dsem_cnt[j] += 16
        tgt = self.dsem_cnt[j]
        self.q[eng].append((need, fn, (key, 16)))
        for t in reads:
            t.d.r[key] = tgt
        for t in writes:
            t.d.w = (key, tgt)
            t.d.r = {}

    def finish(self, final_bufs):
        self.barrier()
        nc, sems, q = self.nc, self.sems, self.q
        needed = {}
        for e in ENGS:
            for need, fn, inc in q[e]:
                for kk, v in need:
                    if kk[0] != "d":
                        needed.setdefault(kk, set()).add(v)
        valmap = {}
        flagged = {}
        for e in ENGS:
            counts = {}
            for i, (need, fn, inc) in enumerate(q[e]):
                if fn is None or inc[0][0] == "d":
                    continue
                key, seq = inc[0], inc[2]
                if (not LAZY_INC) or seq in needed.get(key, ()):
                    counts[key] = counts.get(key, 0) + 1
                    valmap[(key, seq)] = counts[key]
                    flagged[(e, i)] = True
        with nc.Block() as block:
            def run(engname):
                def body(e):
                    for i, (need, fn, inc) in enumerate(q[engname]):
                        for kk, v in need:
                            e.wait_ge(sems[kk], v if kk[0] == "d" else valmap[(kk, v)])
                        if fn is not None:
                            ins = fn(e)
                            if inc[0][0] == "d":
                                ins.then_inc(sems[inc[0]], 16)
                            elif flagged.get((engname, i)):
                                ins.then_inc(sems[inc[0]], 1)
                return body
            block.tensor(run("pe"))
            block.scalar(run("act"))
            block.vector(run("dve"))
            block.gpsimd(run("pool"))
            block.sync(run("sp"))


class KB:
    def __init__(self, nc, es):
        self.nc = nc
        self.P = Prog(nc, es)

    def dma(self, out, in_, reads, writes, eng="sp"):
        self.P.dma(lambda e: e.dma_start(out=out, in_=in_), reads, writes, eng=eng)

    def mm(self, out, lhsT, rhs, start, stop, reads, writes, sgc=False):
        self.P.op("pe", lambda e: e.matmul(out, lhsT=lhsT, rhs=rhs, start=start, stop=stop,
                                           skip_group_check=sgc), reads, writes)

    def tr(self, out, in_, ident, reads, writes):
        self.P.op("pe", lambda e: e.transpose(out=out, in_=in_, identity=ident), reads, writes)

    def act(self, out, in_, func, reads, writes, **kw):
        self.P.op("act", lambda e: e.activation(out=out, in_=in_, func=func, **kw), reads, writes)

    def ts(self, eng, out, in0, s1, s2, op0, op1, reads, writes, **kw):
        if op1 is None:
            self.P.op(eng, lambda e: e.tensor_single_scalar(out=out, in_=in0, scalar=s1, op=op0), reads, writes)
        else:
            self.P.op(eng, lambda e: e.tensor_scalar(out=out, in0=in0, scalar1=s1, scalar2=s2, op0=op0, op1=op1,
                                                     **kw), reads, writes)

    def tt(self, eng, out, in0, in1, op, reads, writes):
        self.P.op(eng, lambda e: e.tensor_tensor(out=out, in0=in0, in1=in1, op=op), reads, writes)

    def stt(self, eng, out, in0, scalar, in1, op0, op1, reads, writes):
        self.P.op(eng, lambda e: e.scalar_tensor_tensor(out=out, in0=in0, scalar=scalar, in1=in1, op0=op0,
                                                        op1=op1), reads, writes)

    def cp(self, eng, out, in_, reads, writes):
        if eng == "act":
            self.P.op("act", lambda e: e.copy(out=out, in_=in_), reads, writes)
        else:
            self.P.op(eng, lambda e: e.tensor_copy(out=out, in_=in_), reads, writes)

    def memset(self, eng, ap, val, writes):
        self.P.op(eng, lambda e: e.memset(ap, val), (), writes)


SCALE = 128.0 ** -0.5
NSLOT = 9
DEBUG = False


def build_program():
    nc = bass.Bass("TRN2", target_bir_lowering=False)

    def din(name, shape, dt=F32):
        return Buf(nc.dram_tensor(name, list(shape), dt, kind="ExternalInput").ap(), tracked=False)

    def dout(name, shape, dt=F32):
        return Buf(nc.dram_tensor(name, list(shape), dt, kind="ExternalOutput").ap(), tracked=False)

    def dscr(name, shape, dt):
        return Buf(nc.dram_tensor(name, list(shape), dt, kind="Internal").ap(), tracked=False)

    xb = din("xb", [SEQ, D])
    xq = din("xq", [NSLOT * 128, D])
    pos = din("pos", [128, 64])
    cmk = din("cmk", [128, 32])
    invf = din("invf", [1, 96])
    w_in = din("w_in", [D, PROJ_W])
    norm_mix = din("norm_mix", [128, 32])
    cmp_pos = din("cmp_pos", [2, 32, 128])
    cmp_w1 = din("cmp_w1", [2, 32, 128, 256])
    cmp_w2 = din("cmp_w2", [2, 256, 128])
    pq = din("pq", [NSLOT * 128, 256])
    ptab = din("ptab", [1, 256], I32)
    c_cmp = din("c_cmp", [2560 * 128, 512])
    c_sel = din("c_sel", [2560 * 128, 512])
    c_dkv = din("c_dkv", [2560 * 128, 1024])
    c_didx = din("c_didx", [2560 * 128, 64])
    state_win = din("state_win", [4, 512, 512])
    w_bn = din("w_bn", [2048, D])
    w_bd = din("w_bd", [2048, D])
    w_out = din("w_out", [D, D])
    norm_mlp = din("norm_mlp", [128, 32])
    w_up = din("w_up", [D, 4 * D])
    w_down = din("w_down", [4 * D, D])
    norm_ple = din("norm_ple", [128, 32])
    w_pg = din("w_pg", [D, D])
    w_ple = din("w_ple", [256, D])
    norm_final = din("norm_final", [1, D])

    o_cmp_p = dout("cmp_p", [SEQ, 512])
    o_sel_p = dout("sel_p", [SEQ, 512])
    o_dkv_p = dout("dkv_p", [SEQ, 1024])
    o_didx_p = dout("didx_p", [SEQ, 64])
    o_win_p = dout("win_p", [512, 512])
    o_cmp_s = dout("cmp_s", [4, 512])
    o_sel_s = dout("sel_s", [4, 512])
    o_dkv_s = dout("dkv_s", [4, 1024])
    o_didx_s = dout("didx_s", [4, 64])
    o_win_s = dout("win_s", [4, 512, 512])
    o_y = dout("y_own", [NSLOT * 128, D])
    outs = [o_cmp_p, o_sel_p, o_dkv_p, o_didx_p, o_win_p, o_cmp_s, o_sel_s, o_dkv_s, o_didx_s, o_win_s, o_y]

    kT_sel = dscr("kT_sel", [2, 128, SEQ], BF16)
    kT_win = dscr("kT_win", [2, 128, SEQ], BF16)
    kT_dsa = dscr("kT_dsa", [4, 128, SEQ], BF16)
    kT_idx = dscr("kT_idx", [64, SEQ], BF16)
    kvT_cmp = dscr("kvT_cmp", [4, 128, SEQ], BF16)
    v_sel = dscr("v_sel", [SEQ, 2, 128], BF16)
    v_win = dscr("v_win", [SEQ, 2, 128], BF16)
    v_dsa = dscr("v_dsa", [SEQ, 4, 128], BF16)
    qT_n = dscr("qT_n", [NSLOT, 128, 16, 128], BF16)
    qT_d = dscr("qT_d", [NSLOT, 128, 16, 128], BF16)
    qiT = dscr("qiT", [NSLOT, 128, 8, 128], BF16)
    gn_s = dscr("gn_s", [NSLOT, 128, 48], F32)
    wi_s = dscr("wi_s", [NSLOT, 128, 16], F32)
    gmT = dscr("gmT", [64, 128, NSLOT * 128], BF16)
    x1s = dscr("x1s", [NSLOT * 128, D], F32)
    snew_sel = dscr("snew_sel", [4, 512], F32)
    snew_dsa = dscr("snew_dsa", [4, 1024], F32)
    snew_idx = dscr("snew_idx", [4, 64], F32)
    x2s = dscr("x2s", [NSLOT * 128, D], F32)
    x3s = dscr("x3s", [NSLOT * 128, D], F32)
    uT_s = dscr("uT_s", [128, 128, NSLOT * 128], BF16)
    if DEBUG:
        oT_s = dout("oT_s", [NSLOT, 128, 32, 128], BF16)
        outs.append(oT_s)
        dbg_sc = dout("dbg_sc", [128, 512]); outs.append(dbg_sc)
        dbg_sc2 = dout("dbg_sc2", [128, 512]); outs.append(dbg_sc2)
        dbg_selm = dout("dbg_selm", [128, 512], BF16); outs.append(dbg_selm)
        dbg_mT = dout("dbg_mT", [128, 4, 128], BF16); outs.append(dbg_mT)
        dbg_qd = dout("dbg_qd", [128, 16, 128], BF16); outs.append(dbg_qd)
        dbg_obf = dout("dbg_obf", [128, 32, 128], BF16); outs.append(dbg_obf)
        dbg_acc = dout("dbg_acc", [128, 4, 256]); outs.append(dbg_acc)
        dbg_kt = dout("dbg_kt", [128, 512], BF16); outs.append(dbg_kt)
        dbg_v1 = dout("dbg_v1", [128, 4, 136], BF16); outs.append(dbg_v1)
    else:
        oT_s = dscr("oT_s", [NSLOT, 128, 32, 128], BF16)

    with ExitStack() as es:
        k = KB(nc, es)
        P = k.P
        ident = P.sb("ident", [128, 128], BF16)
        io_t = P.sb("io_t", [128, 128], F32)
        P.op("pool", lambda e: e.iota(io_t.t[:], pattern=[[1, 128]], base=0, channel_multiplier=-1,
                                      allow_small_or_imprecise_dtypes=True), (), [io_t])
        k.ts("dve", ident.t[:], io_t.t[:], 0.0, None, ALU.is_equal, None, [io_t], [ident])
        pos_t = P.sb("pos_t", [128, 64], F32)
        k.dma(pos_t.t[:], pos.t, [pos], [pos_t])
        cmk_t = P.sb("cmk_t", [128, 32], F32)
        k.dma(cmk_t.t[:], cmk.t, [cmk], [cmk_t])
        inv_t = P.sb("inv_t", [128, 96], F32)
        k.dma(inv_t.t[:], invf.t.broadcast_to([128, 96]), [invf], [inv_t])
        gT = P.sb("gT", [128, 32], F32)
        k.dma(gT.t[:], norm_mix.t, [norm_mix], [gT])
        for s_ in range(4):
            k.dma(o_win_s.t[s_, 0:511, :], state_win.t[s_, 1:512, :], [state_win], [o_win_s])
        pmm = Ring([P.ps("pmm%d" % i, [128, 512], F32) for i in range(4)])
        ptr = Ring([P.ps("ptr%d" % i, [128, 1024], BF16) for i in range(2)])
        pacc = P.ps("pacc", [128, 4, 256], F32)

        with P.scope():
            hT = P.sb("hT", [128, 32, NSLOT * 128], BF16)
            hT_d = [Buf(None) for _ in range(NSLOT)]
            xt_ring = Ring([P.sb("xt", [128, D], F32) for i in range(2)])
            xn = P.sb("xn", [128, D], BF16)
            ssq = Ring([P.sb("ssq", [128, 1], F32) for i in range(2)])
            wsb_ring = Ring([P.sb("wsb", [128, 32, 512], BF16) for i in range(2)])
            stage_ring = Ring([P.sb("stg", [128, 512], F32) for i in range(3)])
            ob_ring = Ring([P.sb("ob", [128, 512], BF16) for i in range(3)])
            tmp_ring = Ring([P.sb("rtmp", [128, 512], F32) for i in range(4)])
            ktsb_ring = Ring([P.sb("ktsb", [128, 512], BF16) for i in range(2)])
            gsb_ring = Ring([P.sb("gsb", [128, 384], BF16) for i in range(2)])
            trig_all = P.sb("trig_all", [128, NSLOT, 192], F32)
            trig_d = [Buf(None) for _ in range(NSLOT)]
            tg_a = P.sb("tg_a", [128, 192], F32)
            tg_i = P.sb("tg_i", [128, 192], I32)
            tg_f = P.sb("tg_f", [128, 192], F32)

            def make_trig(pos_col, slot):
                k.ts("dve", tg_a.t[:, 0:96], inv_t.t[:], pos_t.t[:, pos_col:pos_col + 1], 1.0 / (2 * PI),
                     ALU.mult, ALU.mult, [inv_t, pos_t], [tg_a])
                k.ts("dve", tg_a.t[:, 96:192], tg_a.t[:, 0:96], 0.25, None, ALU.add, None, [tg_a], [tg_a])
                k.cp("dve", tg_i.t[:], tg_a.t[:], [tg_a], [tg_i])
                k.cp("dve", tg_f.t[:], tg_i.t[:], [tg_i], [tg_f])
                k.tt("dve", tg_a.t[:], tg_a.t[:], tg_f.t[:], ALU.subtract, [tg_a, tg_f], [tg_a])
                k.stt("dve", tg_f.t[:], tg_a.t[:], 0.5, tg_a.t[:], ALU.is_gt, ALU.subtract, [tg_a], [tg_f])
                k.act(trig_all.t[:, slot, :], tg_f.t[:], AF.Sin, [tg_f], [trig_d[slot]], scale=-2 * PI)

            def rope(src_ps, src_ap, dst, dst_ap, nh, half, slot, foff):
                td = trig_d[slot]
                cb = trig_all.t[:, slot, 96 + foff:96 + foff + half].unsqueeze(1).broadcast_to([128, nh, half])
                sb_ = trig_all.t[:, slot, foff:foff + half].unsqueeze(1).broadcast_to([128, nh, half])
                x1 = src_ap[:, :, 0:half]
                x2 = src_ap[:, :, half:2 * half]
                t1 = tmp_ring.next(); t2 = tmp_ring.next()
                a1 = t1.t[:, 0:nh * half].rearrange("p (h d) -> p h d", h=nh)
                a2 = t2.t[:, 0:nh * half].rearrange("p (h d) -> p h d", h=nh)
                k.tt("dve", a1, x1, cb, ALU.mult, [src_ps, td], [t1])
                k.tt("dve", a2, x2, sb_, ALU.mult, [src_ps, td], [t2])
                k.tt("dve", dst_ap[:, :, 0:half], a1, a2, ALU.subtract, [t1, t2], [dst])
                t3 = tmp_ring.next(); t4 = tmp_ring.next()
                a3 = t3.t[:, 0:nh * half].rearrange("p (h d) -> p h d", h=nh)
                a4 = t4.t[:, 0:nh * half].rearrange("p (h d) -> p h d", h=nh)
                k.tt("dve", a3, x1, sb_, ALU.mult, [src_ps, td], [t3])
                k.tt("dve", a4, x2, cb, ALU.mult, [src_ps, td], [t4])
                k.tt("dve", dst_ap[:, :, half:2 * half], a3, a4, ALU.add, [t3, t4], [dst])

            def norm_tile(src_rows_ap, src_buf, slot):
                xt = xt_ring.next(); sq = ssq.next()
                k.dma(xt.t[:], src_rows_ap, [src_buf], [xt])
                k.act(xn.t[:], xt.t[:], AF.Square, [xt], [xn, sq], accum_out=sq.t[:])
                k.ts("dve", sq.t[:], sq.t[:], 1.0 / D, 1e-6, ALU.mult, ALU.add, [sq], [sq])
                k.act(sq.t[:], sq.t[:], AF.Sqrt, [sq], [sq])
                P.op("dve", lambda e: e.reciprocal(out=sq.t[:], in_=sq.t[:]), [sq], [sq])
                k.ts("dve", xn.t[:], xt.t[:], sq.t[:, 0:1], None, ALU.mult, None, [xt, sq], [xn])
                for kc4 in range(8):
                    pt = ptr.next()
                    for u in range(4):
                        kc = kc4 * 4 + u
                        k.tr(pt.t[:, u * 128:(u + 1) * 128], xn.t[:, kc * 128:(kc + 1) * 128], ident.t[:],
                             [xn, ident], [pt])
                    for u in range(4):
                        kc = kc4 * 4 + u
                        k.act(hT.t[:, kc, slot * 128:(slot + 1) * 128], pt.t[:, u * 128:(u + 1) * 128], AF.Copy,
                              [pt, gT], [hT_d[slot]], scale=gT.t[:, kc:kc + 1])

            def load_w(wbuf, c0, ncols):
                w = wsb_ring.next()
                if not hasattr(w, "hb"):
                    w.hb = Buf(w.t)
                src = wbuf.t[:, c0:c0 + ncols].rearrange("(kc p) c -> p kc c", p=128)
                for half in range(2):
                    k.dma(w.t[:, half * 16:(half + 1) * 16, 0:ncols], src[:, half * 16:(half + 1) * 16, :],
                          [wbuf], [w if half == 0 else w.hb], eng="pool")
                return w

            def proj_tile(w, ncols, slot):
                ps = pmm.next()
                for kc in range(32):
                    k.mm(ps.t[:, 0:ncols], hT.t[:, kc, slot * 128:(slot + 1) * 128], w.t[:, kc, 0:ncols],
                         kc == 0, kc == 31, [hT_d[slot], w if kc < 16 else w.hb], [ps])
                return ps

            pend = [None]

            def kv_epilogue(gname, ps, slot, tix):
                st = stage_ring.next()
                ob = ob_ring.next()
                r0 = tix * 128
                if gname == "idx":
                    rope(ps, ps.t[:, 0:64].rearrange("p (h d) -> p h d", h=1), st,
                         st.t[:, 0:64].rearrange("p (h d) -> p h d", h=1), 1, 32, slot, 64)
                    if tix < 0:
                        k.dma(o_didx_s.t[:, :], st.t[0:4, 0:64], [st], [o_didx_s])
                        k.dma(snew_idx.t[:, :], st.t[0:4, 0:64], [st], [snew_idx])
                        return None
                    k.dma(o_didx_p.t[r0:r0 + 128, :], st.t[:, 0:64], [st], [o_didx_p])
                    k.cp("pool", ob.t[:, 0:64], st.t[:, 0:64], [st], [ob])

                    def part2():
                        pt = ptr.next()
                        k.tr(pt.t[0:64, 0:128], ob.t[:, 0:64], ident.t[:], [ob, ident], [pt])
                        kt = ktsb_ring.next()
                        k.cp("act", kt.t[0:64, 0:128], pt.t[0:64, 0:128], [pt], [kt])
                        k.dma(kT_idx.t[:, r0:r0 + 128], kt.t[0:64, 0:128], [kt], [kT_idx])
                    return part2
                psv = ps.t[:, :].rearrange("p (h c d) -> p h c d", h=2, c=2)
                stv = st.t[:, :].rearrange("p (h c d) -> p h c d", h=2, c=2)
                rope(ps, psv[:, :, 0, :], st, stv[:, :, 0, :], 2, 64, slot, 0)
                k.cp("act", stv[:, :, 1, :], psv[:, :, 1, :], [ps], [st])
                if tix < 0:
                    if gname == "win":
                        k.dma(o_win_s.t[:, 511, :], st.t[0:4, :], [st], [o_win_s])
                        return None
                    od = {"cmp": (o_cmp_s, 0), "sel": (o_sel_s, 0),
                          "dsa0": (o_dkv_s, 0), "dsa1": (o_dkv_s, 512)}[gname]
                    k.dma(od[0].t[:, od[1]:od[1] + 512], st.t[0:4, :], [st], [od[0]])
                    if gname == "sel":
                        k.dma(snew_sel.t[:, :], st.t[0:4, :], [st], [snew_sel])
                    elif gname in ("dsa0", "dsa1"):
                        k.dma(snew_dsa.t[:, od[1]:od[1] + 512], st.t[0:4, :], [st], [snew_dsa])
                    return None
                if gname == "cmp":
                    k.dma(o_cmp_p.t[r0:r0 + 128, :], st.t[:, :], [st], [o_cmp_p])
                elif gname == "sel":
                    k.dma(o_sel_p.t[r0:r0 + 128, :], st.t[:, :], [st], [o_sel_p])
                elif gname == "win":
                    if tix >= 28:
                        k.dma(o_win_p.t[(tix - 28) * 128:(tix - 27) * 128, :], st.t[:, :], [st], [o_win_p])
                elif gname == "dsa0":
                    k.dma(o_dkv_p.t[r0:r0 + 128, 0:512], st.t[:, :], [st], [o_dkv_p])
                elif gname == "dsa1":
                    k.dma(o_dkv_p.t[r0:r0 + 128, 512:1024], st.t[:, :], [st], [o_dkv_p])
                k.cp("pool", ob.t[:, :], st.t[:, :], [st], [ob])
                obv = ob.t[:, :].rearrange("p (h c d) -> p h c d", h=2, c=2)

                def part2():
                    pt = ptr.next()
                    kt = ktsb_ring.next()
                    if gname == "cmp":
                        for h in range(2):
                            for c in range(2):
                                q4 = h * 2 + c
                                k.tr(pt.t[:, q4 * 128:(q4 + 1) * 128], obv[:, h, c, :], ident.t[:], [ob, ident], [pt])
                        k.cp("act", kt.t[:, 0:512], pt.t[:, 0:512], [pt], [kt])
                        k.dma(kvT_cmp.t[:, :, r0:r0 + 128].rearrange("f p t -> p f t"),
                              kt.t[:, 0:512].rearrange("p (f t) -> p f t", f=4), [kt], [kvT_cmp])
                    else:
                        for h in range(2):
                            k.tr(pt.t[:, h * 128:(h + 1) * 128], obv[:, h, 0, :], ident.t[:], [ob, ident], [pt])
                        k.cp("act", kt.t[:, 0:256], pt.t[:, 0:256], [pt], [kt])
                        ktd, vd, h0 = {"sel": (kT_sel, v_sel, 0), "win": (kT_win, v_win, 0),
                                       "dsa0": (kT_dsa, v_dsa, 0), "dsa1": (kT_dsa, v_dsa, 2)}[gname]
                        k.dma(ktd.t[h0:h0 + 2, :, r0:r0 + 128].rearrange("f p t -> p f t"),
                              kt.t[:, 0:256].rearrange("p (f t) -> p f t", f=2), [kt], [ktd])
                        k.dma(vd.t[r0:r0 + 128, h0:h0 + 2, :], obv[:, :, 1, :], [ob], [vd])
                return part2

            kv_groups = [
                ("cmp", C_KVC, 512), ("sel", C_KVS, 512), ("win", C_KVW, 512),
                ("dsa0", C_KVD, 512), ("dsa1", C_KVD + 512, 512), ("idx", C_KI, 64),
            ]
            for sbk in range(4):
                tiles = [(sbk * 8 + u, xb.t[(sbk * 8 + u) * 128:(sbk * 8 + u + 1) * 128, :], xb, sbk * 8 + u)
                         for u in range(8)]
                if sbk == 0:
                    tiles.append((-1, xq.t[1024:1152, :], xq, 40))
                for slot, (tix, rows, src, pcol) in enumerate(tiles):
                    norm_tile(rows, src, slot)
                    make_trig(pcol, slot)
                wnext = load_w(w_in, kv_groups[0][1], kv_groups[0][2])
                for gi, (gname, c0, ncols) in enumerate(kv_groups):
                    w = wnext
                    if gi + 1 < len(kv_groups):
                        wnext = load_w(w_in, kv_groups[gi + 1][1], kv_groups[gi + 1][2])
                    for slot, (tix, rows, src, pcol) in enumerate(tiles):
                        ps = proj_tile(w, ncols, slot)
                        if pend[0] is not None:
                            pend[0]()
                            pend[0] = None
                        pend[0] = kv_epilogue(gname, ps, slot, tix)
                if pend[0] is not None:
                    pend[0]()
                    pend[0] = None

            for slot in range(NSLOT):
                norm_tile(xq.t[slot * 128:(slot + 1) * 128, :], xq, slot)
                make_trig(32 + slot, slot)
            own_groups = ([("qn", C_QN + 512 * i, 512, i) for i in range(4)]
                          + [("qd", C_QD + 512 * i, 512, i) for i in range(4)]
                          + [("qi", C_QI + 512 * i, 512, i) for i in range(2)]
                          + [("gn", C_GN, 48, 0), ("wi", C_WI, 16, 0)]
                          + [("gm", C_GM + 512 * i, 512, i) for i in range(16)])
            wnext = load_w(w_in, own_groups[0][1], own_groups[0][2])
            for gi, (gname, c0, ncols, gidx) in enumerate(own_groups):
                w = wnext
                if gi + 1 < len(own_groups):
                    wnext = load_w(w_in, own_groups[gi + 1][1], own_groups[gi + 1][2])
                if gname == "gm":
                    if pend[0] is not None:
                        pend[0]()
                        pend[0] = None
                    for cc in range(4):
                        chunk = gidx * 4 + cc
                        for tc in range(3):
                            ps = pmm.next()
                            for kc in range(32):
                                k.mm(ps.t[:, 0:384], w.t[:, kc, cc * 128:(cc + 1) * 128],
                                     hT.t[:, kc, tc * 384:(tc + 1) * 384], kc == 0, kc == 31,
                                     [w if kc < 16 else w.hb] + hT_d[tc * 3:tc * 3 + 3], [ps])
                            gs = gsb_ring.next()
                            k.act(gs.t[:, :], ps.t[:, 0:384], AF.Sigmoid, [ps], [gs])
                            k.dma(gmT.t[chunk, :, tc * 384:(tc + 1) * 384], gs.t[:, :], [gs], [gmT])
                    continue
                for slot in range(NSLOT):
                    ps = proj_tile(w, ncols, slot)
                    if pend[0] is not None:
                        pend[0]()
                        pend[0] = None
                    if gname in ("qn", "qd", "qi"):
                        ob = ob_ring.next()
                        if gname == "qi":
                            rope(ps, ps.t[:, :].rearrange("p (h d) -> p h d", h=8), ob,
                                 ob.t[:, :].rearrange("p (h d) -> p h d", h=8), 8, 32, slot, 64)
                        else:
                            rope(ps, ps.t[:, :].rearrange("p (h d) -> p h d", h=4), ob,
                                 ob.t[:, :].rearrange("p (h d) -> p h d", h=4), 4, 64, slot, 0)

                        def part2(ob=ob, gname=gname, slot=slot, gidx=gidx):
                            pt = ptr.next()
                            kt = ktsb_ring.next()
                            for u in range(4):
                                k.tr(pt.t[:, u * 128:(u + 1) * 128], ob.t[:, u * 128:(u + 1) * 128], ident.t[:],
                                     [ob, ident], [pt])
                            k.cp("act", kt.t[:, 0:512], pt.t[:, 0:512], [pt], [kt])
                            dst = {"qn": qT_n, "qd": qT_d, "qi": qiT}[gname]
                            k.dma(dst.t[slot, :, gidx * 4:gidx * 4 + 4, :],
                                  kt.t[:, 0:512].rearrange("p (f t) -> p f t", f=4), [kt], [dst])
                        pend[0] = part2
                    elif gname == "gn":
                        st = stage_ring.next()
                        k.act(st.t[:, 0:48], ps.t[:, 0:48], AF.Sigmoid, [ps], [st])
                        k.dma(gn_s.t[slot, :, :], st.t[:, 0:48], [st], [gn_s])
                    elif gname == "wi":
                        st = stage_ring.next()
                        k.ts("dve", st.t[:, 0:16], ps.t[:, 0:16], 1.0 / 32.0, None, ALU.mult, None, [ps], [st])
                        k.dma(wi_s.t[slot, :, :], st.t[:, 0:16], [st], [wi_s])

        with P.scope():
            kcT = P.sb("kcT", [128, 2, 256], BF16)
            vc1 = P.sb("vc1", [128, 2, 2, 200], BF16)
            k.memset("dve", kcT.t[:], 0.0, [kcT])
            k.memset("dve", vc1.t[:], 0.0, [vc1])
            with P.scope():
                w1sb = [P.sb("w1sb", [128, 32, 256], BF16) for c in range(2)]
                w2sb = P.sb("w2sb", [128, 2, 2, 128], BF16)
                posb = P.sb("posb", [32, 2, 128], BF16)
                posT = P.sb("posT", [128, 2, 32], BF16)
                bias_sb = P.sb("bias_sb", [128, 4], F32)
                hidT = P.sb("hidT", [128, 2, 256], BF16)
                kv_ring = Ring([P.sb("kvc", [128, SEQ], BF16) for i in range(2)])
                k.memset("dve", hidT.t[:], 0.0, [hidT])
                for c in range(2):
                    k.dma(w1sb[c].t[:], cmp_w1.t[c].rearrange("l d h -> d l h"), [cmp_w1], [w1sb[c]], eng="pool")
                k.dma(w2sb.t[:], cmp_w2.t.rearrange("c (hh p) d -> p c hh d", p=128), [cmp_w2], [w2sb], eng="pool")
                k.dma(posb.t[:], cmp_pos.t.rearrange("c l d -> l c d"), [cmp_pos], [posb], eng="pool")
                for c in range(2):
                    pt = ptr.next()
                    k.tr(pt.t[:, 0:32], posb.t[:, c, :], ident.t[0:32, 0:32], [posb, ident], [pt])
                    k.cp("act", posT.t[:, c, :], pt.t[:, 0:32], [pt], [posT])
                for c in range(2):
                    for hh in range(2):
                        ps = pmm.next()
                        for l in range(32):
                            k.mm(ps.t[:, 0:1], w1sb[c].t[:, l, hh * 128:(hh + 1) * 128], posT.t[:, c, l:l + 1],
                                 l == 0, l == 31, [w1sb[c], posT], [ps])
                        k.cp("act", bias_sb.t[:, c * 2 + hh:c * 2 + hh + 1], ps.t[:, 0:1], [ps], [bias_sb])
                for g in range(2):
                    for c in range(2):
                        kv = kv_ring.next()
                        k.dma(kv.t[:, :], kvT_cmp.t[g * 2 + c], [kvT_cmp], [kv])
                        for hh in range(2):
                            ps = pmm.next()
                            for l in range(32):
                                k.mm(ps.t[:, 0:255], w1sb[c].t[:, l, hh * 128:(hh + 1) * 128],
                                     kv.t[:, l:l + 16 * 254 + 1:16], l == 0, l == 31, [w1sb[c], kv], [ps])
                            k.act(hidT.t[:, hh, 0:255], ps.t[:, 0:255], AF.Relu, [ps, bias_sb], [hidT],
                                  bias=bias_sb.t[:, c * 2 + hh:c * 2 + hh + 1])
                        if c == 0:
                            ps = pmm.next()
                            for hh in range(2):
                                k.mm(ps.t[:, 0:255], w2sb.t[:, 0, hh, :], hidT.t[:, hh, 0:255], hh == 0, hh == 1,
                                     [w2sb, hidT], [ps])
                            k.cp("act", kcT.t[:, g, 0:255], ps.t[:, 0:255], [ps], [kcT])
                        else:
                            for nt in range(2):
                                ps = pmm.next()
                                for hh in range(2):
                                    k.mm(ps.t[:, 0:128], hidT.t[:, hh, nt * 128:(nt + 1) * 128], w2sb.t[:, 1, hh, :],
                                         hh == 0, hh == 1, [w2sb, hidT], [ps])
                                k.cp("act", vc1.t[:, g, nt, 0:128], ps.t[:, 0:128], [ps], [vc1])
            ovl_v = P.sb("ovl_v", [128, 64], F32)
            ovl_a = P.sb("ovl_a", [128, 64], F32)
            for nt in range(2):
                P.op("pool", lambda e, nt=nt: e.iota(ovl_v.t[:], pattern=[[4, 64]], base=-128 * nt,
                                                     channel_multiplier=-1, allow_small_or_imprecise_dtypes=True),
                     (), [ovl_v])
                k.ts("dve", ovl_a.t[:], ovl_v.t[:], -3.0, None, ALU.is_ge, None, [ovl_v], [ovl_a])
                for g in range(2):
                    k.stt("dve", vc1.t[:, g, nt, 129:193], ovl_v.t[:], 1.0, ovl_a.t[:], ALU.is_le, ALU.mult,
                          [ovl_v, ovl_a], [vc1])
                    k.memset("dve", vc1.t[:, g, nt, 128:129], 1.0, [vc1])

            iota_k = P.sb("iota_k", [128, SEQ], F32)
            P.op("pool", lambda e: e.iota(iota_k.t[:], pattern=[[1, SEQ]], base=0, channel_multiplier=0,
                                          allow_small_or_imprecise_dtypes=True), (), [iota_k])
            iota_j = P.sb("iota_j", [128, 64], F32)
            P.op("pool", lambda e: e.iota(iota_j.t[:], pattern=[[1, 64]], base=0, channel_multiplier=0,
                                          allow_small_or_imprecise_dtypes=True), (), [iota_j])
            io_q = P.sb("io_q", [128, 128], F32)
            P.op("pool", lambda e: e.iota(io_q.t[:], pattern=[[1, 128]], base=0, channel_multiplier=0,
                                          allow_small_or_imprecise_dtypes=True), (), [io_q])
            thr_n = P.sb("thr_n", [128, 2], F32)
            P.op("pool", lambda e: e.iota(thr_n.t[:], pattern=[[2048, 2]], base=31, channel_multiplier=16,
                                          allow_small_or_imprecise_dtypes=True), (), [thr_n])
            e0 = P.sb("e0", [128, 64], F32)
            k.ts("dve", e0.t[:], iota_j.t[:], 0.0, None, ALU.is_equal, None, [iota_j], [e0])
            tri = P.sb("tri", [128, 128], F32)
            k.ts("dve", tri.t[:], io_t.t[:], 0.0, None, ALU.is_ge, None, [io_t], [tri])
            CW = P.sb("CW", [128, 12, 128], BF16)
            for m in range(12):
                k.ts("dve", CW.t[:, m, :], tri.t[:], cmk_t.t[:, 8 + 2 * m:9 + 2 * m], cmk_t.t[:, 9 + 2 * m:10 + 2 * m],
                     ALU.mult, ALU.add, [tri, cmk_t], [CW])
            Ex = P.sb("Ex", [64, 32, 128], BF16)
            with P.scope():
                Ex_v = P.sb("Ex_v", [64, 32, 128], F32)
                Ex_a = P.sb("Ex_a", [64, 32, 128], F32)
                P.op("pool", lambda e: e.iota(Ex_v.t[:], pattern=[[128, 32], [1, 128]], base=0,
                                              channel_multiplier=-64, allow_small_or_imprecise_dtypes=True),
                     (), [Ex_v])
                k.ts("dve", Ex_a.t[:], Ex_v.t[:], 0.0, None, ALU.is_ge, None, [Ex_v], [Ex_a])
                k.stt("dve", Ex.t[:], Ex_v.t[:], 63.0, Ex_a.t[:], ALU.is_le, ALU.mult, [Ex_v, Ex_a], [Ex])
            kidx2 = P.sb("kidx2", [128, SEQ], BF16)
            k.dma(kidx2.t[0:64, :], kT_idx.t, [kT_idx], [kidx2])
            k.dma(kidx2.t[64:128, :], kT_idx.t, [kT_idx], [kidx2])

            qn_ring = Ring([P.sb("qn", [128, 16, 128], BF16) for i in range(2)])
            qd_ring = Ring([P.sb("qd", [128, 16, 128], BF16) for i in range(2)])
            qi_ring = Ring([P.sb("qi", [128, 8, 128], BF16) for i in range(2)])
            gn_ring = Ring([P.sb("gn", [128, 48], F32) for i in range(2)])
            wi_ring = Ring([P.sb("wi", [128, 16], F32) for i in range(2)])
            sc = P.sb("sc", [128, SEQ], F32)
            relu_ring = Ring([P.sb("rl", [128, 512], F32) for i in range(3)])
            m8_ring = Ring([P.sb("m8", [128, 8], F32) for i in range(4)])
            selm = P.sb("selm", [128, SEQ], BF16)
            mT = P.sb("mT", [128, 32, 128], BF16)
            ms = P.sb("ms", [128, 32, 128], BF16)
            mc = P.sb("mc", [128, 2, 128], BF16)
            kt_ring = Ring([P.sb("ktb", [128, SEQ], BF16) for i in range(3)])
            v1_ring = Ring([P.sb("v1b", [128, 32, 136], BF16) for i in range(3)])
            for vb in v1_ring.bufs:
                k.memset("pool", vb.t[:, :, 128:136], 1.0, [vb])
            e_ring = Ring([P.sb("eb", [128, 512], BF16) for i in range(4)])
            rd_ring = Ring([P.sb("rd", [128, 4], F32) for i in range(4)])
            o32 = P.sb("o32", [128, 16, 128], F32)
            obf = P.sb("obf", [128, 32, 128], BF16)
            oTsb = P.sb("oTsb", [128, 32, 128], BF16)
            impu = P.sb("impu", [128, 8, 64], F32)
            imp = P.sb("imp", [128, 64], F32)
            sw = [P.sb("sw%d" % i, [128, 64], F32) for i in range(4)]
            selb = P.sb("selb", [128, 64], BF16)
            selT = P.sb("selT", [64, 128], BF16)
            curm1 = P.sb("curm1", [128, 1], F32)
            mmul_eng = ["dve", "pool"]
            cnt = [0]

            def attend(kts, K_of, V_of, ncols, q_ap, q_buf, M_of):
                n = len(kts)
                ebs = {}

                def stage1(idx):
                    kt = kts[idx]
                    ps = pmm.next()
                    kap, kbuf = K_of(kt)
                    k.mm(ps.t[:, 0:512], kap, q_ap, True, True, [kbuf, q_buf], [ps])
                    eb = e_ring.next()
                    k.act(eb.t[:, :], ps.t[:, 0:512], AF.Exp, [ps], [eb], scale=SCALE)
                    map_, mbuf = M_of(kt)
                    eng = mmul_eng[cnt[0] % 2]; cnt[0] += 1
                    ev = eb.t[:, :].rearrange("p (r q) -> p r q", r=4)
                    k.tt(eng, ev, ev, map_.unsqueeze(1).broadcast_to([128, 4, 128]), ALU.mult, [eb, mbuf], [eb])
                    ebs[idx] = eb

                def stage2(idx):
                    eb = ebs.pop(idx)
                    vap, vbuf = V_of(kts[idx])
                    for r in range(4):
                        k.mm(pacc.t[:, r, 0:ncols], eb.t[:, r * 128:(r + 1) * 128], vap, idx == 0 and r % 2 == 0,
                             idx == n - 1, [eb, vbuf], [pacc], sgc=True)
                stage1(0)
                for idx in range(n):
                    if idx + 1 < n:
                        stage1(idx + 1)
                    stage2(idx)
                if tick_hook[0] is not None:
                    tick_hook[0]()

            def recip_den():
                rd = rd_ring.next()
                k.ts("dve", rd.t[:, :], pacc.t[:, :, 128], 1e-30, None, ALU.max, None, [pacc], [rd])
                P.op("dve", lambda e: e.reciprocal(out=rd.t[:, :], in_=rd.t[:, :]), [rd], [rd])
                return rd

            def load_kv(ktd, vd, head, lo, hi):
                ktb = kt_ring.next(); v1b = v1_ring.next()
                nk = hi - lo
                k.dma(ktb.t[:, 0:nk * 128], ktd.t[head, :, lo * 128:hi * 128], [ktd], [ktb])
                k.dma(v1b.t[:, 0:nk, 0:128],
                      vd.t[lo * 128:hi * 128, head, :].rearrange("(kt p) d -> p kt d", p=128), [vd], [v1b])
                return ktb, v1b

            tick_hook = [None]

            def tile_load_score(j):
                nkt = 4 * j + 4
                Lk = nkt * 128
                qpos = pos_t.t[:, 32 + j:33 + j]
                qn = qn_ring.next(); qd = qd_ring.next(); qi = qi_ring.next(); gn = gn_ring.next(); wi = wi_ring.next()
                k.dma(qn.t[:], qT_n.t[j], [qT_n], [qn])
                k.dma(qd.t[:], qT_d.t[j], [qT_d], [qd])
                k.dma(qi.t[:], qiT.t[j], [qiT], [qi])
                k.dma(gn.t[:], gn_s.t[j], [gn_s], [gn])
                k.dma(wi.t[:], wi_s.t[j], [wi_s], [wi])
                k.ts("dve", sc.t[:, 0:Lk], iota_k.t[:, 0:Lk], qpos, -1e30, ALU.is_gt, ALU.mult, [iota_k, pos_t], [sc])
                for c in range(nkt // 4):
                    for h in range(16):
                        hp = (h % 2) * 64
                        ps = pmm.next()
                        k.mm(ps.t[:, 0:512], qi.t[hp:hp + 64, h // 2, :], kidx2.t[hp:hp + 64, c * 512:(c + 1) * 512],
                             True, True, [qi, kidx2], [ps])
                        rl = relu_ring.next()
                        k.act(rl.t[:, :], ps.t[:, 0:512], AF.Relu, [ps], [rl])
                        k.stt("dve", sc.t[:, c * 512:(c + 1) * 512], rl.t[:, :], wi.t[:, h:h + 1],
                              sc.t[:, c * 512:(c + 1) * 512], ALU.mult, ALU.add, [rl, wi, sc], [sc])
                return dict(qn=qn, qd=qd, gn=gn)

            def topk_closures(j):
                Lk = (4 * j + 4) * 128

                def one():
                    m8 = m8_ring.next()
                    P.op("dve", lambda e, m8=m8, Lk=Lk: e.max(out=m8.t[:, :], in_=sc.t[:, 0:Lk]), [sc], [m8])
                    P.op("dve", lambda e, m8=m8, Lk=Lk: e.match_replace(out=sc.t[:, 0:Lk], in_to_replace=m8.t[:, :],
                                                                      in_values=sc.t[:, 0:Lk], imm_value=-3e38),
                         [sc, m8], [sc])
                return [one for _ in range(32)]

            def build_mask(j):
                nkt = 4 * j + 4
                Lk = nkt * 128
                qpos = pos_t.t[:, 32 + j:33 + j]
                k.ts("dve", selm.t[:, 0:Lk], sc.t[:, 0:Lk], -1e37, None, ALU.is_lt, None, [sc], [selm])
                k.stt("dve", selm.t[:, 0:Lk], iota_k.t[:, 0:Lk], qpos, selm.t[:, 0:Lk], ALU.is_le, ALU.mult,
                      [iota_k, pos_t, selm], [selm])
                for kt8 in range(0, nkt, 8):
                    pt = ptr.next()
                    nn = min(8, nkt - kt8)
                    for u in range(nn):
                        k.tr(pt.t[:, u * 128:(u + 1) * 128], selm.t[:, (kt8 + u) * 128:(kt8 + u + 1) * 128], ident.t[:],
                             [selm, ident], [pt])
                    k.cp("act", mT.t[:, kt8:kt8 + nn, :], pt.t[:, 0:nn * 128].rearrange("p (a q) -> p a q", a=nn),
                         [pt], [mT])

            def attention_part(j, ctx):
                nkt = 4 * j + 4
                Lk = nkt * 128
                qpos = pos_t.t[:, 32 + j:33 + j]
                cur = pos_t.t[:, 41 + j:42 + j]
                qn = ctx["qn"]; qd = ctx["qd"]; gn = ctx["gn"]
                gn3 = gn.t[:, :].rearrange("p (h b) -> p h b", b=3)
                for g in range(4):
                    ktb, v1b = load_kv(kT_dsa, v_dsa, g, 0, nkt)
                    attend(list(range(nkt)),
                           lambda kt, ktb=ktb: (ktb.t[:, kt * 128:(kt + 1) * 128], ktb),
                           lambda kt, v1b=v1b: (v1b.t[:, kt, 0:129], v1b), 129,
                           qd.t[:, g * 4:(g + 1) * 4, :].rearrange("p r q -> p (r q)"), qd,
                           lambda kt: (mT.t[:, kt, :], mT))
                    if DEBUG and j == 0 and g == 0:
                        dacc = P.sb("dacc", [128, 4, 256], F32)
                        k.cp("dve", dacc.t[:, 0:2, :], pacc.t[:, 0:2, :], [pacc], [dacc])
                        k.cp("dve", dacc.t[:, 2:4, :], pacc.t[:, 2:4, :], [pacc], [dacc])
                        k.dma(dbg_acc.t, dacc.t[:, :, :], [dacc], [dbg_acc])
                        k.dma(dbg_mT.t, mT.t[:, 0:4, :], [mT], [dbg_mT])
                        k.dma(dbg_kt.t, ktb.t[:, 0:512], [ktb], [dbg_kt])
                        k.dma(dbg_v1.t, v1b.t[:, 0:4, :], [v1b], [dbg_v1])
                    rd = recip_den()
                    k.tt("dve", obf.t[:, 16 + g * 4:16 + (g + 1) * 4, :], pacc.t[:, :, 0:128],
                         rd.t[:, :].unsqueeze(2).broadcast_to([128, 4, 128]), ALU.mult, [pacc, rd], [obf])

                for nt in range(2):
                    k.ts("dve", mc.t[:, nt, :], io_q.t[:, :], cmk_t.t[:, j:j + 1], thr_n.t[:, nt:nt + 1],
                         ALU.add, ALU.is_ge, [io_q, cmk_t, thr_n], [mc])
                k.ts("dve", curm1.t[:, :], cur, -1.0, None, ALU.add, None, [pos_t], [curm1])
                for g in range(2):
                    for sub in range(2):
                        h0 = g * 8 + sub * 4
                        attend([0, 1],
                               lambda nt, g=g: (kcT.t[:, g, nt * 128:(nt + 1) * 128], kcT),
                               lambda nt, g=g: (vc1.t[:, g, nt, 0:193], vc1), 193,
                               qn.t[:, h0:h0 + 4, :].rearrange("p r q -> p (r q)"), qn,
                               lambda nt: (mc.t[:, nt, :], mc))
                        rd = recip_den()
                        cf = rd_ring.next()
                        k.tt("dve", cf.t[:, :], rd.t[:, :], gn3[:, h0:h0 + 4, 0], ALU.mult, [rd, gn], [cf])
                        k.tt("dve", o32.t[:, h0:h0 + 4, :], pacc.t[:, :, 0:128],
                             cf.t[:, :].unsqueeze(2).broadcast_to([128, 4, 128]), ALU.mult, [pacc, cf], [o32])
                        k.tt("dve", impu.t[:, sub * 4:(sub + 1) * 4, :], pacc.t[:, :, 129:193],
                             rd.t[:, :].unsqueeze(2).broadcast_to([128, 4, 64]), ALU.mult, [pacc, rd], [impu])
                    P.op("dve", lambda e: e.tensor_reduce(out=imp.t[:, :], in_=impu.t[:, :, :].rearrange("p r j -> p j r"),
                                                          axis=AX.X, op=ALU.add), [impu], [imp])
                    k.ts("dve", sw[0].t[:, :], iota_j.t[:, :], cur, None, ALU.is_equal, None, [iota_j, pos_t], [sw[0]])
                    k.ts("dve", sw[1].t[:, :], iota_j.t[:, :], curm1.t[:, 0:1], None, ALU.is_equal, None,
                         [iota_j, curm1], [sw[1]])
                    k.tt("dve", sw[0].t[:, :], sw[0].t[:, :], sw[1].t[:, :], ALU.add, [sw[0], sw[1]], [sw[0]])
                    k.tt("dve", sw[0].t[:, :], sw[0].t[:, :], e0.t[:, :], ALU.add, [sw[0], e0], [sw[0]])
                    k.stt("dve", sw[1].t[:, :], sw[0].t[:, :], 1e4, imp.t[:, :], ALU.mult, ALU.add, [sw[0], imp], [sw[1]])
                    k.ts("dve", sw[2].t[:, :], iota_j.t[:, :], cur, None, ALU.is_le, None, [iota_j, pos_t], [sw[2]])
                    k.tt("dve", sw[1].t[:, :], sw[1].t[:, :], sw[2].t[:, :], ALU.mult, [sw[1], sw[2]], [sw[1]])
                    k.ts("dve", sw[2].t[:, :], sw[2].t[:, :], -1.0, 1e30, ALU.add, ALU.mult, [sw[2]], [sw[2]])
                    k.tt("dve", sw[1].t[:, :], sw[1].t[:, :], sw[2].t[:, :], ALU.add, [sw[1], sw[2]], [sw[1]])
                    ma = m8_ring.next(); mb = m8_ring.next()
                    P.op("dve", lambda e, ma=ma: e.max(out=ma.t[:, :], in_=sw[1].t[:, :]), [sw[1]], [ma])
                    P.op("dve", lambda e, ma=ma: e.match_replace(out=sw[3].t[:, :], in_to_replace=ma.t[:, :],
                                                                in_values=sw[1].t[:, :], imm_value=-3e38),
                         [sw[1], ma], [sw[3]])
                    P.op("dve", lambda e, mb=mb: e.max(out=mb.t[:, :], in_=sw[3].t[:, :]), [sw[3]], [mb])
                    k.ts("dve", selb.t[:, :], sw[1].t[:, :], mb.t[:, 7:8], None, ALU.is_ge, None, [sw[1], mb], [selb])
                    pt = ptr.next()
                    k.tr(pt.t[0:64, 0:128], selb.t[:, :], ident.t[:], [selb, ident], [pt])
                    k.cp("act", selT.t[:, :], pt.t[0:64, 0:128], [pt], [selT])
                    for kt in range(nkt):
                        ps = pmm.next()
                        k.mm(ps.t[:, 0:128], Ex.t[:, kt, :], selT.t[:, :], True, True, [Ex, selT], [ps])
                        if kt < 4 * j:
                            k.cp("act", ms.t[:, kt, :], ps.t[:, 0:128], [ps], [ms])
                        else:
                            k.tt("dve", ms.t[:, kt, :], ps.t[:, 0:128], CW.t[:, kt - 4 * j, :], ALU.mult, [ps, CW], [ms])
                    ktb_s, v1b_s = load_kv(kT_sel, v_sel, g, 0, nkt)
                    wlo = max(0, 4 * j - 4)
                    ktb_w, v1b_w = load_kv(kT_win, v_win, g, wlo, nkt)
                    for sub in range(2):
                        h0 = g * 8 + sub * 4
                        q_ap = qn.t[:, h0:h0 + 4, :].rearrange("p r q -> p (r q)")
                        attend(list(range(nkt)),
                               lambda kt, b=ktb_s: (b.t[:, kt * 128:(kt + 1) * 128], b),
                               lambda kt, b=v1b_s: (b.t[:, kt, 0:129], b), 129, q_ap, qn,
                               lambda kt: (ms.t[:, kt, :], ms))
                        rd = recip_den()
                        cf = rd_ring.next()
                        k.tt("dve", cf.t[:, :], rd.t[:, :], gn3[:, h0:h0 + 4, 1], ALU.mult, [rd, gn], [cf])
                        for r in range(4):
                            k.stt("dve", o32.t[:, h0 + r, :], pacc.t[:, r, 0:128], cf.t[:, r:r + 1], o32.t[:, h0 + r, :],
                                  ALU.mult, ALU.add, [pacc, cf, o32], [o32])
                        attend(list(range(wlo, nkt)),
                               lambda kt, b=ktb_w, wlo=wlo: (b.t[:, (kt - wlo) * 128:(kt - wlo + 1) * 128], b),
                               lambda kt, b=v1b_w, wlo=wlo: (b.t[:, kt - wlo, 0:129], b), 129, q_ap, qn,
                               lambda kt, j=j: (CW.t[:, 4 + (kt - 4 * j + 4), :], CW))
                        rd = recip_den()
                        cf = rd_ring.next()
                        k.tt("dve", cf.t[:, :], rd.t[:, :], gn3[:, h0:h0 + 4, 2], ALU.mult, [rd, gn], [cf])
                        for r in range(4):
                            k.stt("dve", o32.t[:, h0 + r, :], pacc.t[:, r, 0:128], cf.t[:, r:r + 1], o32.t[:, h0 + r, :],
                                  ALU.mult, ALU.add, [pacc, cf, o32], [o32])
                k.cp("pool", obf.t[:, 0:16, :], o32.t[:, :, :], [o32], [obf])
                if DEBUG and j == 0:
                    k.dma(dbg_obf.t, obf.t[:, :, :], [obf], [dbg_obf])
                for h8 in range(0, 32, 8):
                    pt = ptr.next()
                    for u in range(8):
                        k.tr(pt.t[:, u * 128:(u + 1) * 128], obf.t[:, h8 + u, :], ident.t[:], [obf, ident], [pt])
                    k.cp("act", oTsb.t[:, h8:h8 + 8, :], pt.t[:, :].rearrange("p (a q) -> p a q", a=8), [pt], [oTsb])
                k.dma(oT_s.t[j], oTsb.t[:, :, :], [oTsb], [oT_s])


            ctx_cur = tile_load_score(0)
            for f_ in topk_closures(0):
                f_()
            build_mask(0)
            for j in range(8):
                pending_tk = []
                if j + 1 < 8:
                    ctx_next = tile_load_score(j + 1)
                    pending_tk = topk_closures(j + 1)

                def tick(n=2):
                    for _ in range(n):
                        if pending_tk:
                            pending_tk.pop(0)()
                tick_hook[0] = tick
                attention_part(j, ctx_cur)
                tick_hook[0] = None
                while pending_tk:
                    pending_tk.pop(0)()
                if j + 1 < 8:
                    build_mask(j + 1)
                    ctx_cur = ctx_next

        NPG = 64
        with P.scope():
            ptab_i = P.sb("ptab_i", [128, 4 * NPG], I32)
            k.dma(ptab_i.t[:, :], ptab.t.broadcast_to([128, 4 * NPG]), [ptab], [ptab_i])
            ptab_f = P.sb("ptab_f", [128, 4 * NPG], F32)
            k.cp("dve", ptab_f.t[:, :], ptab_i.t[:, :], [ptab_i], [ptab_f])
            pcol = P.sb("pcol", [128, 1], F32)
            P.op("pool", lambda e: e.iota(pcol.t[:, :], pattern=[[0, 1]], base=0, channel_multiplier=1,
                                          allow_small_or_imprecise_dtypes=True), (), [pcol])
            offs = P.sb("offs", [128, 4 * NPG], I32)
            k.ts("dve", offs.t[:, :], ptab_f.t[:, :], 128.0, pcol.t[:, 0:1], ALU.mult, ALU.add, [ptab_f, pcol], [offs])
            qn8 = P.sb("qn8", [128, 16, 128], BF16)
            qd8 = P.sb("qd8", [128, 16, 128], BF16)
            qi8 = P.sb("qi8", [128, 8, 128], BF16)
            k.dma(qn8.t[:], qT_n.t[8], [qT_n], [qn8])
            k.dma(qd8.t[:], qT_d.t[8], [qT_d], [qd8])
            k.dma(qi8.t[:], qiT.t[8], [qiT], [qi8])
            qns = P.sb("qns", [128, 4, 16], BF16)
            qds = P.sb("qds", [128, 4, 16], BF16)
            k.cp("dve", qns.t[:, :, :], qn8.t[:, :, 0:4].rearrange("p h s -> p s h"), [qn8], [qns])
            k.cp("dve", qds.t[:, :, :], qd8.t[:, :, 0:4].rearrange("p h s -> p s h"), [qd8], [qds])
            Qblk = P.sb("Qblk", [128, 4, 8, 2], BF16)
            k.memset("dve", Qblk.t[:], 0.0, [Qblk])
            for hf in range(2):
                k.cp("dve", Qblk.t[hf * 64:(hf + 1) * 64, :, :, hf],
                     qi8.t[hf * 64:(hf + 1) * 64, :, 0:4].rearrange("p a s -> p s a"), [qi8, Qblk], [Qblk])
            gateT = P.sb("gateT", [8, 4, 2, 3], F32)
            P.dma(lambda e: e.dma_start(out=gateT.t[:], in_=gn_s.t[8, 0:4, :].rearrange("s (g r b) -> r s g b", g=2, r=8),
                                        allow_slow_non_contiguous=True), [gn_s], [gateT])
            wiT = P.sb("wiT", [16, 4], F32)
            P.dma(lambda e: e.dma_start(out=wiT.t[:], in_=wi_s.t[8, 0:4, :].rearrange("s h -> h s"),
                                        allow_slow_non_contiguous=True), [wi_s], [wiT])
            W4 = P.sb("W4", [16, 4, 4], BF16)
            k.memset("dve", W4.t[:], 0.0, [W4])
            for s in range(4):
                k.cp("dve", W4.t[:, s, s:s + 1], wiT.t[:, s:s + 1], [wiT, W4], [W4])
            ones14 = P.sb("ones14", [4, 4], BF16)
            k.cp("dve", ones14.t[:, :], ident.t[0:4, 0:4], [ident], [ones14])
            one11 = P.sb("one11", [1, 1], BF16)
            k.memset("dve", one11.t[:], 1.0, [one11])
            e1col = P.sb("e1col", [128, 1], BF16)
            k.ts("dve", e1col.t[:, :], pcol.t[:, :], 0.0, None, ALU.is_equal, None, [pcol], [e1col])
            maskD = P.sb("maskD", [128, NPG, 4], BF16)
            maskS = P.sb("maskS", [128, NPG, 8], BF16)
            ockeep = P.sb("ockeep", [8, 8, 129], F32)
            selbT = P.sb("selbT", [128, 8], BF16)
            on_all = P.sb("on_all", [8, 4, 2, 128], F32)
            od_all = P.sb("od_all", [4, 4, 4, 128], F32)
            oT8 = P.sb("oT8", [128, 32, 128], BF16)
            k.memset("pool", oT8.t[:], 0.0, [oT8])
            pg_ring = Ring([P.sb("pg", [128, 1024], BF16) for i in range(8)])
            ktp_ring = Ring([P.sb("ktp", [128, 512], BF16) for i in range(4)])
            es_ring = Ring([P.sb("es", [128, 16], BF16) for i in range(4)])
            rd8_ring = Ring([P.sb("rd8", [8, 4], F32) for i in range(4)])
            v1n_ring = Ring([P.sb("v1n", [128, 4, 136], BF16) for i in range(3)])
            for v1_ in v1n_ring.bufs:
                k.memset("dve", v1_.t[:, :, 128:136], 1.0, [v1_])

            def gather(pool_buf, s, j, ncols):
                pg = pg_ring.next()
                P.dma(lambda e: e.indirect_dma_start(
                    out=pg.t[:, 0:ncols], out_offset=None, in_=pool_buf.t,
                    in_offset=bass.IndirectOffsetOnAxis(ap=offs.t[:, s * NPG + j:s * NPG + j + 1], axis=0)),
                    [pool_buf, offs], [pg], eng="pool")
                return pg

            def newrow_page(src_buf, s, ncols):
                pg = pg_ring.next()
                k.memset("dve", pg.t[:, 0:ncols], 0.0, [pg])
                k.dma(pg.t[0:1, 0:ncols], src_buf.t[s:s + 1, 0:ncols], [src_buf], [pg], eng="pool")
                return pg

            with P.scope():
                isc = P.sb("isc", [4, 8704], F32)
                k.memset("dve", isc.t[:, :], -1e30, [isc])
                selm4 = P.sb("selm4", [4, 8704], BF16)
                rls_ring = Ring([P.sb("rls", [16, 512], BF16) for i in range(3)])
                psi_ap = pacc.t[0:4, 0:2, :].rearrange("p a b -> p (a b)")
                for c in range(17):
                    ncol = 512 if c < 16 else 1
                    for s in range(4):
                        pt = ptr.next()
                        if c < 16:
                            for u in range(4):
                                pg = gather(c_didx, s, c * 4 + u, 64)
                                k.cp("dve", pg.t[:, 64:128], pg.t[:, 0:64], [pg], [pg])
                                k.tr(pt.t[:, u * 128:(u + 1) * 128], pg.t[:, 0:128], ident.t[:], [pg, ident], [pt])
                        else:
                            pg = newrow_page(snew_idx, s, 64)
                            k.cp("dve", pg.t[:, 64:128], pg.t[:, 0:64], [pg], [pg])
                            k.tr(pt.t[:, 0:128], pg.t[:, 0:128], ident.t[:], [pg, ident], [pt])
                        ktp = ktp_ring.next()
                        k.cp("act", ktp.t[:, 0:ncol], pt.t[:, 0:ncol], [pt], [ktp])
                        ps = pmm.next()
                        k.mm(ps.t[0:16, 0:ncol], Qblk.t[:, s, :, :].rearrange("p a b -> p (a b)"), ktp.t[:, 0:ncol],
                             True, True, [Qblk, ktp], [ps])
                        rl = rls_ring.next()
                        k.act(rl.t[:, 0:ncol], ps.t[0:16, 0:ncol], AF.Relu, [ps], [rl])
                        k.mm(psi_ap[:, 0:ncol], W4.t[:, s, :], rl.t[:, 0:ncol], s == 0, s == 3, [W4, rl], [pacc])
                    k.cp("act", isc.t[:, c * 512:c * 512 + ncol], psi_ap[:, 0:ncol], [pacc], [isc])
                m8s = Ring([P.sb("m8s", [4, 8], F32) for i in range(4)])
                for it in range(32):
                    m8 = m8s.next()
                    P.op("dve", lambda e, m8=m8: e.max(out=m8.t[:, :], in_=isc.t[:, :]), [isc], [m8])
                    P.op("dve", lambda e, m8=m8: e.match_replace(out=isc.t[:, :], in_to_replace=m8.t[:, :],
                                                                in_values=isc.t[:, :], imm_value=-3e38), [isc, m8], [isc])
                k.ts("dve", selm4.t[:, :], isc.t[:, :], -1e37, None, ALU.is_lt, None, [isc], [selm4])
                psm = pmm.next()
                for j in range(NPG):
                    k.mm(psm.t[:, j * 4:(j + 1) * 4], selm4.t[:, j * 128:(j + 1) * 128], ones14.t[:, :], True, True,
                         [selm4, ones14], [psm], sgc=True)
                k.cp("act", maskD.t[:, :, :], psm.t[:, 0:256].rearrange("p (j s) -> p j s", s=4), [psm], [maskD])

            with P.scope():
                w1sb = [P.sb("w1sbs", [128, 32, 256], BF16) for c in range(2)]
                w2sb = P.sb("w2sbs", [128, 2, 2, 128], BF16)
                posb = P.sb("posbs", [32, 2, 128], BF16)
                posT = P.sb("posTs", [128, 2, 32], BF16)
                bias_sb = P.sb("bias_sbs", [128, 4], F32)
                hidT = P.sb("hidTs", [128, 2, 512], BF16)
                kvTs = P.sb("kvTs", [128, 4, 8192], BF16)
                kcTs = P.sb("kcTs", [128, 512], BF16)
                vc1s = P.sb("vc1s", [128, 4, 136], BF16)
                ovls = P.sb("ovls", [128, 4, 132], BF16)
                ovl_vs = P.sb("ovl_vs", [128, 132], F32)
                ovl_as = P.sb("ovl_as", [128, 132], F32)
                iota_j2 = P.sb("iota_j2", [1, 132], F32)
                sws = [P.sb("sws%d" % i, [1, 132], F32) for i in range(4)]
                selrow = P.sb("selrow", [1, 132], BF16)
                impu_s = P.sb("impu_s", [8, 132], BF16)
                rdc = P.sb("rdc", [8, 1], BF16)
                k.memset("dve", hidT.t[:], 0.0, [hidT])
                k.memset("dve", kcTs.t[:], 0.0, [kcTs])
                k.memset("dve", vc1s.t[:], 0.0, [vc1s])
                P.op("pool", lambda e: e.iota(iota_j2.t[:, :], pattern=[[1, 132]], base=0, channel_multiplier=0,
                                              allow_small_or_imprecise_dtypes=True), (), [iota_j2])
                for c in range(2):
                    k.dma(w1sb[c].t[:], cmp_w1.t[c].rearrange("l d h -> d l h"), [cmp_w1], [w1sb[c]], eng="pool")
                k.dma(w2sb.t[:], cmp_w2.t.rearrange("c (hh p) d -> p c hh d", p=128), [cmp_w2], [w2sb], eng="pool")
                k.dma(posb.t[:], cmp_pos.t.rearrange("c l d -> l c d"), [cmp_pos], [posb], eng="pool")
                for c in range(2):
                    pt = ptr.next()
                    k.tr(pt.t[:, 0:32], posb.t[:, c, :], ident.t[0:32, 0:32], [posb, ident], [pt])
                    k.cp("act", posT.t[:, c, :], pt.t[:, 0:32], [pt], [posT])
                for c in range(2):
                    for hh in range(2):
                        ps = pmm.next()
                        for l in range(32):
                            k.mm(ps.t[:, 0:1], w1sb[c].t[:, l, hh * 128:(hh + 1) * 128], posT.t[:, c, l:l + 1],
                                 l == 0, l == 31, [w1sb[c], posT], [ps])
                        k.cp("act", bias_sb.t[:, c * 2 + hh:c * 2 + hh + 1], ps.t[:, 0:1], [ps], [bias_sb])
                for nt in range(4):
                    P.op("pool", lambda e, nt=nt: e.iota(ovl_vs.t[:], pattern=[[4, 132]], base=-128 * nt,
                                                         channel_multiplier=-1, allow_small_or_imprecise_dtypes=True),
                         (), [ovl_vs])
                    k.ts("dve", ovl_as.t[:], ovl_vs.t[:], -3.0, None, ALU.is_ge, None, [ovl_vs], [ovl_as])
                    k.stt("dve", ovls.t[:, nt, :], ovl_vs.t[:], 1.0, ovl_as.t[:], ALU.is_le, ALU.mult,
                          [ovl_vs, ovl_as], [ovls])
                padm = P.sb("padm", [128, 1], BF16)
                k.ts("dve", padm.t[:, :], pcol.t[:, :], 127.0, None, ALU.is_lt, None, [pcol], [padm])
                for s in range(4):
                    for j in range(NPG):
                        pg = gather(c_cmp, s, j, 512)
                        pt = ptr.next()
                        for f in range(4):
                            k.tr(pt.t[:, f * 128:(f + 1) * 128], pg.t[:, f * 128:(f + 1) * 128], ident.t[:], [pg, ident], [pt])
                        k.cp("act", kvTs.t[:, :, j * 128:(j + 1) * 128], pt.t[:, 0:512].rearrange("p (f t) -> p f t", f=4),
                             [pt], [kvTs])
                    for g in range(2):
                        for c in range(2):
                            for hh in range(2):
                                ps = pmm.next()
                                for l in range(32):
                                    k.mm(ps.t[:, 0:511], w1sb[c].t[:, l, hh * 128:(hh + 1) * 128],
                                         kvTs.t[:, g * 2 + c, l:l + 16 * 510 + 1:16], l == 0, l == 31, [w1sb[c], kvTs], [ps])
                                k.act(hidT.t[:, hh, 0:511], ps.t[:, 0:511], AF.Relu, [ps, bias_sb], [hidT],
                                      bias=bias_sb.t[:, c * 2 + hh:c * 2 + hh + 1])
                            if c == 0:
                                ps = pmm.next()
                                for hh in range(2):
                                    k.mm(ps.t[:, 0:511], w2sb.t[:, 0, hh, :], hidT.t[:, hh, 0:511], hh == 0, hh == 1,
                                         [w2sb, hidT], [ps])
                                k.cp("act", kcTs.t[:, 0:511], ps.t[:, 0:511], [ps], [kcTs])
                            else:
                                ps = pmm.next()
                                for nt in range(4):
                                    for hh in range(2):
                                        k.mm(ps.t[:, nt * 128:(nt + 1) * 128], hidT.t[:, hh, nt * 128:(nt + 1) * 128],
                                             w2sb.t[:, 1, hh, :], hh == 0 and nt == 0, hh == 1, [w2sb, hidT], [ps], sgc=True)
                                k.cp("act", vc1s.t[:, :, 0:128], ps.t[:, 0:512].rearrange("p (a d) -> p a d", a=4), [ps], [vc1s])
                                k.memset("dve", vc1s.t[:, :, 128:129], 1.0, [vc1s])
                        m = s * 2 + g
                        qap = qns.t[:, s, g * 8:(g + 1) * 8]
                        for nt in range(4):
                            ps = pmm.next()
                            k.mm(ps.t[:, 0:8], kcTs.t[:, nt * 128:(nt + 1) * 128], qap, True, True, [kcTs, qns], [ps])
                            eb = es_ring.next()
                            k.act(eb.t[:, 0:8], ps.t[:, 0:8], AF.Exp, [ps], [eb], scale=SCALE)
                            if nt == 3:
                                k.ts("dve", eb.t[:, 0:8], eb.t[:, 0:8], padm.t[:, 0:1], None, ALU.mult, None, [eb, padm], [eb])
                            k.mm(pacc.t[0:8, 0, 0:129], eb.t[:, 0:8], vc1s.t[:, nt, 0:129], nt == 0, nt == 3, [eb, vc1s], [pacc],
                                 sgc=True)
                            k.mm(pacc.t[0:8, 2, 0:129], eb.t[:, 0:8], ovls.t[:, nt, 0:129], nt == 0, nt == 3, [eb, ovls], [pacc],
                                 sgc=True)
                        k.cp("act", ockeep.t[:, m, :], pacc.t[0:8, 0, 0:129], [pacc], [ockeep])
                        rd = rd8_ring.next()
                        k.ts("dve", rd.t[:, 0:1], ockeep.t[:, m, 128:129], 1e-30, None, ALU.max, None, [ockeep], [rd])
                        P.op("dve", lambda e, rd=rd: e.reciprocal(out=rd.t[:, 0:1], in_=rd.t[:, 0:1]), [rd], [rd])
                        k.cp("dve", rdc.t[:, :], rd.t[:, 0:1], [rd], [rdc])
                        k.cp("act", impu_s.t[:, 0:129], pacc.t[0:8, 2, 0:129], [pacc], [impu_s])
                        ps = pmm.next()
                        k.mm(ps.t[0:1, 0:129], rdc.t[:, :], impu_s.t[:, 0:129], True, True, [rdc, impu_s], [ps])
                        k.ts("dve", sws[0].t[:, 0:129], iota_j2.t[:, 0:129], 0.0, None, ALU.is_equal, None, [iota_j2], [sws[0]])
                        k.ts("dve", sws[1].t[:, 0:129], iota_j2.t[:, 0:129], 127.0, None, ALU.is_ge, None, [iota_j2], [sws[1]])
                        k.tt("dve", sws[0].t[:, 0:129], sws[0].t[:, 0:129], sws[1].t[:, 0:129], ALU.add, [sws[0], sws[1]], [sws[0]])
                        k.stt("dve", sws[1].t[:, 0:129], sws[0].t[:, 0:129], 1e4, ps.t[0:1, 0:129], ALU.mult, ALU.add,
                              [sws[0], ps], [sws[1]])
                        ma8 = P.sb("ma8", [1, 8], F32); mb8 = P.sb("mb8", [1, 8], F32)
                        P.op("dve", lambda e, ma8=ma8: e.max(out=ma8.t[:, :], in_=sws[1].t[:, 0:129]), [sws[1]], [ma8])
                        P.op("dve", lambda e, ma8=ma8: e.match_replace(out=sws[2].t[:, 0:129], in_to_replace=ma8.t[:, :],
                                                                      in_values=sws[1].t[:, 0:129], imm_value=-3e38),
                             [sws[1], ma8], [sws[2]])
                        P.op("dve", lambda e, mb8=mb8: e.max(out=mb8.t[:, :], in_=sws[2].t[:, 0:129]), [sws[2]], [mb8])
                        k.ts("dve", selrow.t[:, 0:128], sws[1].t[:, 0:128], mb8.t[:, 7:8], None, ALU.is_ge, None,
                             [sws[1], mb8], [selrow])
                        ps2 = pmm.next()
                        k.mm(ps2.t[:, 0:1], selrow.t[0:1, 0:128], one11.t[:, :], True, True, [selrow, one11], [ps2])
                        k.cp("act", selbT.t[:, m:m + 1], ps2.t[:, 0:1], [ps2], [selbT])

            with P.scope():
                ExS = P.sb("ExS", [128, NPG, 128], BF16)
                with P.scope():
                    ExV = P.sb("ExV", [128, NPG, 128], F32)
                    P.op("pool", lambda e: e.iota(ExV.t[:].rearrange("p j (b i) -> p j b i", b=2),
                                                  pattern=[[-2, NPG], [-1, 2], [0, 64]], base=0, channel_multiplier=1,
                                                  allow_small_or_imprecise_dtypes=True), (), [ExV])
                    k.ts("dve", ExS.t[:], ExV.t[:], 0.0, None, ALU.is_equal, None, [ExV], [ExS])
                psm = pmm.next()
                for j in range(NPG):
                    k.mm(psm.t[:, j * 8:(j + 1) * 8], ExS.t[:, j, :], selbT.t[:, :], True, True, [ExS, selbT], [psm], sgc=True)
                k.cp("act", maskS.t[:, :, :], psm.t[:, 0:512].rearrange("p (j m) -> p j m", m=8), [psm], [maskS])

            def finish_branch(nrow, ngrp, dst_of_g, coef_of_g, first):
                rd = rd8_ring.next()
                k.ts("dve", rd.t[0:nrow, 0:ngrp], pacc.t[0:nrow, 0:ngrp, 128], 1e-30, None, ALU.max, None, [pacc], [rd])
                P.op("dve", lambda e: e.reciprocal(out=rd.t[0:nrow, 0:ngrp], in_=rd.t[0:nrow, 0:ngrp]), [rd], [rd])
                for g in range(ngrp):
                    dap, dbuf = dst_of_g(g)
                    cf = coef_of_g(g)
                    if cf is not None:
                        cap, cbuf = cf
                        k.tt("dve", rd.t[0:nrow, g:g + 1], rd.t[0:nrow, g:g + 1], cap, ALU.mult, [rd, cbuf], [rd])
                    if first:
                        k.ts("dve", dap, pacc.t[0:nrow, g, 0:128], rd.t[0:nrow, g:g + 1], None, ALU.mult, None, [pacc, rd], [dbuf])
                    else:
                        k.stt("dve", dap, pacc.t[0:nrow, g, 0:128], rd.t[0:nrow, g:g + 1], dap, ALU.mult, ALU.add,
                              [pacc, rd, dbuf], [dbuf])

            for s in range(4):
                for g in range(2):
                    m = s * 2 + g
                    rd = rd8_ring.next()
                    k.ts("dve", rd.t[:, 0:1], ockeep.t[:, m, 128:129], 1e-30, None, ALU.max, None, [ockeep], [rd])
                    P.op("dve", lambda e, rd=rd: e.reciprocal(out=rd.t[:, 0:1], in_=rd.t[:, 0:1]), [rd], [rd])
                    k.tt("dve", rd.t[:, 0:1], rd.t[:, 0:1], gateT.t[:, s, g, 0:1], ALU.mult, [rd, gateT], [rd])
                    k.ts("dve", on_all.t[:, s, g, :], ockeep.t[:, m, 0:128], rd.t[:, 0:1], None, ALU.mult, None,
                         [ockeep, rd], [on_all])
                def run_pages(pool_buf, newsrc, ncols, **kw):
                    n = NPG + 1
                    st_ = {}

                    def get_page(idx):
                        if idx < NPG:
                            return gather(pool_buf, s, idx, ncols)
                        return newrow_page(newsrc, s, ncols)
                    st_[0] = attend_stage1(get_page(0), 0, **kw)
                    for idx in range(n):
                        if idx + 1 < n:
                            st_[idx + 1] = attend_stage1(get_page(idx + 1), idx + 1, **kw)
                        attend_stage2(st_.pop(idx), idx == 0, idx == n - 1, kw["nrow"], kw["ngrp"])

                def attend_stage1(pg, pj, nrow, ngrp, q_of_g, kcol_of_g, vcol_of_g, mask_of_page):
                    pt = ptr.next()
                    for g in range(ngrp):
                        c0 = kcol_of_g(g)
                        k.tr(pt.t[:, g * 128:(g + 1) * 128], pg.t[:, c0:c0 + 128], ident.t[:], [pg, ident], [pt])
                    ktp = ktp_ring.next()
                    k.cp("act", ktp.t[:, 0:ngrp * 128], pt.t[:, 0:ngrp * 128], [pt], [ktp])
                    ps = pmm.next()
                    for g in range(ngrp):
                        qap, qb = q_of_g(g)
                        k.mm(ps.t[:, g * nrow:(g + 1) * nrow], ktp.t[:, g * 128:(g + 1) * 128], qap, True, True,
                             [ktp, qb], [ps], sgc=True)
                    eb = es_ring.next()
                    k.act(eb.t[:, 0:ngrp * nrow], ps.t[:, 0:ngrp * nrow], AF.Exp, [ps], [eb], scale=SCALE)
                    mk = mask_of_page(pj)
                    if mk is not None:
                        map_, mbuf = mk
                        ev = eb.t[:, 0:ngrp * nrow].rearrange("p (g r) -> p g r", g=ngrp)
                        k.tt("dve", ev, ev, map_, ALU.mult, [eb, mbuf], [eb])
                    v1 = v1n_ring.next()
                    for g in range(ngrp):
                        c0 = vcol_of_g(g)
                        k.cp("dve" if g % 2 == 0 else "act", v1.t[:, g, 0:128], pg.t[:, c0:c0 + 128], [pg], [v1])
                    return eb, v1

                def attend_stage2(stv, first, last, nrow, ngrp):
                    eb, v1 = stv
                    for g in range(ngrp):
                        k.mm(pacc.t[0:nrow, g, 0:129], eb.t[:, g * nrow:(g + 1) * nrow], v1.t[:, g, 0:129],
                             first and g % 2 == 0, last, [eb, v1], [pacc], sgc=True)

                def mask_sel(pj, s=s):
                    if pj < NPG:
                        return (maskS.t[:, pj, s * 2:s * 2 + 2].unsqueeze(2).broadcast_to([128, 2, 8]), maskS)
                    return (e1col.t[:, 0:1].unsqueeze(1).broadcast_to([128, 2, 8]), e1col)
                run_pages(c_sel, snew_sel, 512, nrow=8, ngrp=2,
                          q_of_g=lambda g, s=s: (qns.t[:, s, g * 8:(g + 1) * 8], qns),
                          kcol_of_g=lambda g: g * 256, vcol_of_g=lambda g: g * 256 + 128, mask_of_page=mask_sel)
                finish_branch(8, 2, lambda g, s=s: (on_all.t[:, s, g, :], on_all),
                              lambda g, s=s: (gateT.t[:, s, g, 1:2], gateT), False)
                for wi_ in range(4):
                    pg = pg_ring.next()
                    k.dma(pg.t[:, 0:512], o_win_s.t[s, wi_ * 128:(wi_ + 1) * 128, :], [o_win_s], [pg], eng="pool")
                    stv = attend_stage1(pg, wi_, nrow=8, ngrp=2,
                                        q_of_g=lambda g, s=s: (qns.t[:, s, g * 8:(g + 1) * 8], qns),
                                        kcol_of_g=lambda g: g * 256, vcol_of_g=lambda g: g * 256 + 128,
                                        mask_of_page=lambda pj: None)
                    attend_stage2(stv, wi_ == 0, wi_ == 3, 8, 2)
                finish_branch(8, 2, lambda g, s=s: (on_all.t[:, s, g, :], on_all),
                              lambda g, s=s: (gateT.t[:, s, g, 2:3], gateT), False)

                def mask_dsa(pj, s=s):
                    if pj < NPG:
                        return (maskD.t[:, pj, s:s + 1].unsqueeze(1).broadcast_to([128, 4, 4]), maskD)
                    return (e1col.t[:, 0:1].unsqueeze(1).broadcast_to([128, 4, 4]), e1col)
                run_pages(c_dkv, snew_dsa, 1024, nrow=4, ngrp=4,
                          q_of_g=lambda g, s=s: (qds.t[:, s, g * 4:(g + 1) * 4], qds),
                          kcol_of_g=lambda g: g * 256, vcol_of_g=lambda g: g * 256 + 128, mask_of_page=mask_dsa)
                finish_branch(4, 4, lambda g, s=s: (od_all.t[:, s, g, :], od_all), lambda g: None, True)

            onb = P.sb("onb", [8, 4, 2, 128], BF16)
            odb = P.sb("odb", [4, 4, 4, 128], BF16)
            k.cp("dve", onb.t[:], on_all.t[:], [on_all], [onb])
            k.cp("dve", odb.t[:], od_all.t[:], [od_all], [odb])
            for s in range(4):
                pt = ptr.next()
                for g in range(2):
                    k.tr(pt.t[:, g * 8:(g + 1) * 8], onb.t[:, s, g, :], ident.t[0:8, 0:8], [onb, ident], [pt])
                for g in range(4):
                    k.tr(pt.t[:, 16 + g * 4:16 + (g + 1) * 4], odb.t[:, s, g, :], ident.t[0:4, 0:4], [odb, ident], [pt])
                k.cp("act", oT8.t[:, :, s], pt.t[:, 0:32], [pt], [oT8])
            k.dma(oT_s.t[8], oT8.t[:, :, :], [oT8], [oT_s])

        NTOK = NSLOT * 128

        def gen_norm(rows_ap, src_buf, slot, hTt, hTd, gainT, xt_ring, xn, ssq):
            xt = xt_ring.next(); sq = ssq.next()
            k.dma(xt.t[:], rows_ap, [src_buf], [xt])
            k.act(xn.t[:], xt.t[:], AF.Square, [xt], [xn, sq], accum_out=sq.t[:])
            k.ts("dve", sq.t[:], sq.t[:], 1.0 / D, 1e-6, ALU.mult, ALU.add, [sq], [sq])
            k.act(sq.t[:], sq.t[:], AF.Sqrt, [sq], [sq])
            P.op("dve", lambda e: e.reciprocal(out=sq.t[:], in_=sq.t[:]), [sq], [sq])
            k.ts("dve", xn.t[:], xt.t[:], sq.t[:, 0:1], None, ALU.mult, None, [xt, sq], [xn])
            for kc4 in range(8):
                pt = ptr.next()
                for u in range(4):
                    kc = kc4 * 4 + u
                    k.tr(pt.t[:, u * 128:(u + 1) * 128], xn.t[:, kc * 128:(kc + 1) * 128], ident.t[:], [xn, ident], [pt])
                for u in range(4):
                    kc = kc4 * 4 + u
                    k.act(hTt.t[:, kc, slot * 128:(slot + 1) * 128], pt.t[:, u * 128:(u + 1) * 128], AF.Copy,
                          [pt, gainT], [hTd[slot]], scale=gainT.t[:, kc:kc + 1])

        def gen_load_w(ring, wbuf, r0, nkc, c0, ncols):
            w = ring.next()
            if not hasattr(w, "hb"):
                w.hb = Buf(w.t)
            src = wbuf.t[r0:r0 + nkc * 128, c0:c0 + ncols].rearrange("(kc p) c -> p kc c", p=128)
            h = max(1, nkc // 2)
            w.half = h
            for a in range(0, nkc, h):
                k.dma(w.t[:, a:a + h, 0:ncols], src[:, a:a + h, :], [wbuf], [w if a == 0 else w.hb], eng="pool")
            return w

        def wd_(w, kc):
            return w if kc < w.half else w.hb

        with P.scope():
            mixT = P.sb("mixT", [128, 32, NTOK], BF16)
            mix_d = [Buf(None) for _ in range(3)]
            with P.scope():
                oT_all = P.sb("oT_all", [128, 32, NTOK], BF16)
                for slot in range(NSLOT):
                    k.dma(oT_all.t[:, :, slot * 128:(slot + 1) * 128], oT_s.t[slot], [oT_s], [oT_all])
                wb_ring = Ring([P.sb("wb", [128, 32, 128], BF16) for i in range(2)])
                gm_ring = Ring([P.sb("gmb", [128, 2, NTOK], BF16) for i in range(2)])
                t_ring = Ring([P.sb("mt", [128, 384], F32) for i in range(4)])

                def load_c1(c):
                    wb = wb_ring.next(); gb = gm_ring.next()
                    k.dma(wb.t[:, 0:16, :], w_bn.t[:, c * 128:(c + 1) * 128].rearrange("(kc p) c -> p kc c", p=128),
                          [w_bn], [wb], eng="pool")
                    k.dma(wb.t[:, 16:32, :], w_bd.t[:, c * 128:(c + 1) * 128].rearrange("(kc p) c -> p kc c", p=128),
                          [w_bd], [wb], eng="pool")
                    k.dma(gb.t[:, 0, :], gmT.t[c], [gmT], [gb])
                    k.dma(gb.t[:, 1, :], gmT.t[32 + c], [gmT], [gb])
                    return wb, gb
                nxt = load_c1(0)
                for c in range(32):
                    wb, gb = nxt
                    if c + 1 < 32:
                        nxt = load_c1(c + 1)
                    for tc in range(3):
                        tsl = slice(tc * 384, (tc + 1) * 384)
                        psn = pmm.next(); psd = pmm.next()
                        for kc in range(16):
                            k.mm(psn.t[:, 0:384], wb.t[:, kc, :], oT_all.t[:, kc, tsl], kc == 0, kc == 15, [wb, oT_all], [psn])
                        for kc in range(16, 32):
                            k.mm(psd.t[:, 0:384], wb.t[:, kc, :], oT_all.t[:, kc, tsl], kc == 16, kc == 31, [wb, oT_all], [psd])
                        t1 = t_ring.next(); t2 = t_ring.next()
                        k.tt("dve", t1.t[:, :], psn.t[:, 0:384], gb.t[:, 0, tsl], ALU.mult, [psn, gb], [t1])
                        k.tt("dve", t2.t[:, :], psd.t[:, 0:384], gb.t[:, 1, tsl], ALU.mult, [psd, gb], [t2])
                        k.tt("pool", mixT.t[:, c, tsl], t1.t[:, :], t2.t[:, :], ALU.add, [t1, t2], [mix_d[tc]])
            with P.scope():
                w_ring = Ring([P.sb("wo", [128, 32, 512], BF16) for i in range(2)])
                xc_ring = Ring([P.sb("xc", [128, 512], F32) for i in range(3)])
                nxt = gen_load_w(w_ring, w_out, 0, 32, 0, 512)
                for n in range(8):
                    w = nxt
                    if n + 1 < 8:
                        nxt = gen_load_w(w_ring, w_out, 0, 32, (n + 1) * 512, 512)
                    for slot in range(NSLOT):
                        ps = pmm.next()
                        for kc in range(32):
                            k.mm(ps.t[:, 0:512], mixT.t[:, kc, slot * 128:(slot + 1) * 128], w.t[:, kc, :], kc == 0, kc == 31,
                                 [mix_d[slot // 3], wd_(w, kc)], [ps])
                        xc = xc_ring.next()
                        k.dma(xc.t[:, :], xq.t[slot * 128:(slot + 1) * 128, n * 512:(n + 1) * 512], [xq], [xc])
                        k.tt("dve", xc.t[:, :], xc.t[:, :], ps.t[:, 0:512], ALU.add, [xc, ps], [xc])
                        k.dma(x1s.t[slot * 128:(slot + 1) * 128, n * 512:(n + 1) * 512], xc.t[:, :], [xc], [x1s])

        with P.scope():
            hT2 = P.sb("hT2", [128, 32, NTOK], BF16)
            hT2_d = [Buf(None) for _ in range(NSLOT)]
            gT2 = P.sb("gT2", [128, 32], F32)
            k.dma(gT2.t[:], norm_mlp.t, [norm_mlp], [gT2])
            with P.scope():
                xt_ring = Ring([P.sb("xt", [128, D], F32) for i in range(2)])
                xn = P.sb("xn", [128, D], BF16)
                ssq = Ring([P.sb("ssq", [128, 1], F32) for i in range(2)])
                for slot in range(NSLOT):
                    gen_norm(x1s.t[slot * 128:(slot + 1) * 128, :], x1s, slot, hT2, hT2_d, gT2, xt_ring, xn, ssq)
            with P.scope():
                w_ring = Ring([P.sb("wu", [128, 32, 512], BF16) for i in range(2)])
                r_ring = Ring([P.sb("rr", [128, 384], F32) for i in range(3)])
                u_ring = Ring([P.sb("ub", [128, 384], BF16) for i in range(3)])
                nxt = gen_load_w(w_ring, w_up, 0, 32, 0, 512)
                ecnt = 0
                for fg in range(32):
                    w = nxt
                    if fg + 1 < 32:
                        nxt = gen_load_w(w_ring, w_up, 0, 32, (fg + 1) * 512, 512)
                    for cc in range(4):
                        f = fg * 4 + cc
                        for tc in range(3):
                            ps = pmm.next()
                            for kc in range(32):
                                k.mm(ps.t[:, 0:384], w.t[:, kc, cc * 128:(cc + 1) * 128], hT2.t[:, kc, tc * 384:(tc + 1) * 384],
                                     kc == 0, kc == 31, [wd_(w, kc)] + hT2_d[tc * 3:tc * 3 + 3], [ps])
                            rr = r_ring.next(); ub = u_ring.next()
                            k.act(rr.t[:, :], ps.t[:, 0:384], AF.Relu, [ps], [rr])
                            k.tt("dve" if ecnt % 2 == 0 else "pool", ub.t[:, :], rr.t[:, :], rr.t[:, :], ALU.mult, [rr], [ub])
                            ecnt += 1
                            k.dma(uT_s.t[f, :, tc * 384:(tc + 1) * 384], ub.t[:, :], [ub], [uT_s])
        with P.scope():
            wd_ring = Ring([P.sb("wd", [128, 16, 256], BF16) for i in range(2)])
            ub_ring = Ring([P.sb("ubk", [128, 16, NTOK], BF16) for i in range(2)])
            xc_ring = Ring([P.sb("xc2", [128, 256], F32) for i in range(3)])

            def acc_region(slot):
                if slot < 8:
                    b_ = pmm.bufs[slot // 2]
                    return b_, b_.t[:, (slot % 2) * 256:(slot % 2) * 256 + 256]
                return pacc, pacc.t[:, 0, :]

            def load_c2(n, kq):
                wd = gen_load_w(wd_ring, w_down, kq * 2048, 16, n * 256, 256)
                ubk = ub_ring.next()
                k.dma(ubk.t[:, :, :], uT_s.t[kq * 16:(kq + 1) * 16].rearrange("f p t -> p f t"), [uT_s], [ubk])
                return wd, ubk
            seq = [(n, kq) for n in range(16) for kq in range(8)]
            nxt = load_c2(*seq[0])
            for si, (n, kq) in enumerate(seq):
                wd, ubk = nxt
                if si + 1 < len(seq):
                    nxt = load_c2(*seq[si + 1])
                for slot in range(NSLOT):
                    rb, rap = acc_region(slot)
                    for fc in range(16):
                        first = (kq == 0 and fc == 0)
                        k.mm(rap, ubk.t[:, fc, slot * 128:(slot + 1) * 128], wd.t[:, fc, :],
                             first and (slot % 2 == 0), kq == 7 and fc == 15, [ubk, wd_(wd, fc)], [rb], sgc=True)
                if kq == 7:
                    for slot in range(NSLOT):
                        rb, rap = acc_region(slot)
                        xc = xc_ring.next()
                        k.dma(xc.t[:, :], x1s.t[slot * 128:(slot + 1) * 128, n * 256:(n + 1) * 256], [x1s], [xc])
                        k.tt("dve", xc.t[:, :], xc.t[:, :], rap, ALU.add, [xc, rb], [xc])
                        k.dma(x2s.t[slot * 128:(slot + 1) * 128, n * 256:(n + 1) * 256], xc.t[:, :], [xc], [x2s])

        with P.scope():
            hT3 = P.sb("hT3", [128, 32, NTOK], BF16)
            hT3_d = [Buf(None) for _ in range(NSLOT)]
            pT = P.sb("pT", [128, 2, NTOK], BF16)
            gT3 = P.sb("gT3", [128, 32], F32)
            k.dma(gT3.t[:], norm_ple.t, [norm_ple], [gT3])
            ssx = P.sb("ssx", [128, NSLOT, 8], F32)
            with P.scope():
                xt_ring = Ring([P.sb("xt", [128, D], F32) for i in range(2)])
                xn = P.sb("xn", [128, D], BF16)
                ssq = Ring([P.sb("ssq", [128, 1], F32) for i in range(2)])
                pin = Ring([P.sb("pin", [128, 256], BF16) for i in range(2)])
                for slot in range(NSLOT):
                    gen_norm(x2s.t[slot * 128:(slot + 1) * 128, :], x2s, slot, hT3, hT3_d, gT3, xt_ring, xn, ssq)
                    pb = pin.next()
                    k.dma(pb.t[:, :], pq.t[slot * 128:(slot + 1) * 128, :], [pq], [pb], eng="pool")
                    pt = ptr.next()
                    for u in range(2):
                        k.tr(pt.t[:, u * 128:(u + 1) * 128], pb.t[:, u * 128:(u + 1) * 128], ident.t[:], [pb, ident], [pt])
                    k.cp("act", pT.t[:, :, slot * 128:(slot + 1) * 128], pt.t[:, 0:256].rearrange("p (a q) -> p a q", a=2),
                         [pt], [pT])
            with P.scope():
                w_ring = Ring([P.sb("wg", [128, 32, 512], BF16) for i in range(2)])
                wp_ring = Ring([P.sb("wp", [128, 2, 512], BF16) for i in range(2)])
                xc_ring = Ring([P.sb("xc3", [128, 512], F32) for i in range(3)])
                g_ring = Ring([P.sb("gt", [128, 512], F32) for i in range(3)])
                junk = P.sb("junk", [128, 512], BF16)

                def load_c3(n):
                    return (gen_load_w(w_ring, w_pg, 0, 32, n * 512, 512), gen_load_w(wp_ring, w_ple, 0, 2, n * 512, 512))
                nxt = load_c3(0)
                for n in range(8):
                    w, wp = nxt
                    if n + 1 < 8:
                        nxt = load_c3(n + 1)
                    for slot in range(NSLOT):
                        psg = pmm.next(); psp = pmm.next()
                        for kc in range(32):
                            k.mm(psg.t[:, 0:512], hT3.t[:, kc, slot * 128:(slot + 1) * 128], w.t[:, kc, :], kc == 0, kc == 31,
                                 [hT3_d[slot], wd_(w, kc)], [psg])
                        for k2 in range(2):
                            k.mm(psp.t[:, 0:512], pT.t[:, k2, slot * 128:(slot + 1) * 128], wp.t[:, k2, :], k2 == 0, k2 == 1,
                                 [pT, wd_(wp, k2)], [psp])
                        gt = g_ring.next(); xc = xc_ring.next()
                        k.act(gt.t[:, :], psg.t[:, 0:512], AF.Sigmoid, [psg], [gt])
                        k.tt("dve", gt.t[:, :], gt.t[:, :], psp.t[:, 0:512], ALU.mult, [gt, psp], [gt])
                        k.dma(xc.t[:, :], x2s.t[slot * 128:(slot + 1) * 128, n * 512:(n + 1) * 512], [x2s], [xc])
                        k.tt("pool", xc.t[:, :], xc.t[:, :], gt.t[:, :], ALU.add, [xc, gt], [xc])
                        k.act(junk.t[:, :], xc.t[:, :], AF.Square, [xc], [junk, ssx], accum_out=ssx.t[:, slot, n:n + 1])
                        k.dma(x3s.t[slot * 128:(slot + 1) * 128, n * 512:(n + 1) * 512], xc.t[:, :], [xc], [x3s])
            with P.scope():
                gfin = P.sb("gfin", [128, D], F32)
                k.dma(gfin.t[:, :], norm_final.t.broadcast_to([128, D]), [norm_final], [gfin])
                xt_ring = Ring([P.sb("xt", [128, D], F32) for i in range(2)])
                rs = P.sb("rs", [128, NSLOT], F32)
                P.op("dve", lambda e: e.tensor_reduce(out=rs.t[:, :], in_=ssx.t[:, :, :], axis=AX.X, op=ALU.add), [ssx], [rs])
                k.ts("dve", rs.t[:, :], rs.t[:, :], 1.0 / D, 1e-6, ALU.mult, ALU.add, [rs], [rs])
                k.act(rs.t[:, :], rs.t[:, :], AF.Sqrt, [rs], [rs])
                P.op("dve", lambda e: e.reciprocal(out=rs.t[:, :], in_=rs.t[:, :]), [rs], [rs])
                for slot in range(NSLOT):
                    xt = xt_ring.next()
                    k.dma(xt.t[:, :], x3s.t[slot * 128:(slot + 1) * 128, :], [x3s], [xt])
                    k.stt("dve", xt.t[:, :], xt.t[:, :], rs.t[:, slot:slot + 1], gfin.t[:, :], ALU.mult, ALU.mult, [xt, rs, gfin], [xt])
                    k.dma(o_y.t[slot * 128:(slot + 1) * 128, :], xt.t[:, :], [xt], [o_y])

        P.finish(outs)
    return nc


_NC_CACHE = {}


def kernel(**inputs):
    x_prompt = np.asarray(inputs["x_prompt"], dtype=np.float32)
    x_sample = np.asarray(inputs["x_sample"], dtype=np.float32)
    w_in = np.ascontiguousarray(np.asarray(inputs["w_in"], dtype=np.float32)[0])
    norm_mix = np.ascontiguousarray(np.asarray(inputs["norm_mix"], dtype=np.float32)[0].reshape(32, 128).T)
    state_win = np.asarray(inputs["state_nsa_win"], dtype=np.float32)
    cmp_pos = np.ascontiguousarray(np.asarray(inputs["cmp_pos"], dtype=np.float32)[0])
    cmp_w1 = np.ascontiguousarray(np.asarray(inputs["cmp_w1"], dtype=np.float32)[0])
    cmp_w2 = np.ascontiguousarray(np.asarray(inputs["cmp_w2"], dtype=np.float32)[0])
    g1 = lambda n: np.ascontiguousarray(np.asarray(inputs[n], dtype=np.float32)[0])
    gTl = lambda n: np.ascontiguousarray(np.asarray(inputs[n], dtype=np.float32)[0].reshape(32, 128).T)
    p_prompt = np.asarray(inputs["p_prompt"], dtype=np.float32)[0]
    p_sample = np.asarray(inputs["p_sample"], dtype=np.float32)[0]
    shared = {"w_bn": g1("w_branch_nsa"), "w_bd": g1("w_branch_dsa"), "w_out": g1("w_out"), "norm_mlp": gTl("norm_mlp"),
              "w_up": g1("w_up"), "w_down": g1("w_down"), "norm_ple": gTl("norm_ple"), "w_pg": g1("w_ple_gate"),
              "w_ple": g1("w_ple"), "norm_final": np.ascontiguousarray(np.asarray(inputs["norm_final"], dtype=np.float32)[None, :])}
    pools = {"c_cmp": np.asarray(inputs["cache_nsa_cmp"], dtype=np.float32)[0].reshape(2560 * 128, 512),
             "c_sel": np.asarray(inputs["cache_nsa_sel"], dtype=np.float32)[0].reshape(2560 * 128, 512),
             "c_dkv": np.asarray(inputs["cache_dsa_kv"], dtype=np.float32)[0].reshape(2560 * 128, 1024),
             "c_didx": np.asarray(inputs["cache_dsa_idx"], dtype=np.float32)[0].reshape(2560 * 128, 64)}
    page_table = np.asarray(inputs["page_table"]).astype(np.int32)
    if "nc" not in _NC_CACHE:
        _NC_CACHE["nc"] = build_program()
    nc = _NC_CACHE["nc"]

    invf = np.concatenate([
        (10000.0 ** (-np.arange(64, dtype=np.float32) / np.float32(64))).astype(np.float32),
        (10000.0 ** (-np.arange(32, dtype=np.float32) / np.float32(32))).astype(np.float32)])[None, :]
    in_maps = []
    for c in range(8):
        b, i = c // 4, c % 4
        xq = np.zeros((NSLOT * 128, D), np.float32)
        pos = np.zeros((128, 64), np.float32)
        cmk = np.zeros((128, 32), np.float32)
        for t in range(32):
            pos[:, t] = 128 * t + np.arange(128)
        for j in range(8):
            t = 4 * j + i
            xq[j * 128:(j + 1) * 128] = x_prompt[b, t * 128:(t + 1) * 128]
            pos[:, 32 + j] = 128 * t + np.arange(128)
            pos[:, 41 + j] = (128 * t + np.arange(128)) // 64
            cmk[:, j] = 128 * t
        xq[1024:1028] = x_sample[4 * c:4 * c + 4, 0]
        pqa = np.zeros((NSLOT * 128, 256), np.float32)
        for j in range(8):
            t = 4 * j + i
            pqa[j * 128:(j + 1) * 128] = p_prompt[b, t * 128:(t + 1) * 128]
        pqa[1024:1028] = p_sample[4 * c:4 * c + 4, 0]
        pos[:, 40] = 8192
        for rel in range(4):
            dl = i - rel
            a, bb = (0.0, 1.0) if dl > 0 else ((1.0, 0.0) if dl == 0 else (0.0, 0.0))
            cmk[:, 8 + 2 * rel] = a
            cmk[:, 9 + 2 * rel] = bb
        for m in range(8):
            dl = i - (m - 4)
            if dl == 0:
                a, bb = 1.0, 0.0
            elif 1 <= dl <= 3:
                a, bb = 0.0, 1.0
            elif dl == 4:
                a, bb = -1.0, 1.0
            else:
                a, bb = 0.0, 0.0
            cmk[:, 16 + 2 * m] = a
            cmk[:, 17 + 2 * m] = bb
        in_maps.append({"xb": np.ascontiguousarray(x_prompt[b]), "xq": xq, "pos": pos, "cmk": cmk,
                        "invf": invf.astype(np.float32), "w_in": w_in, "norm_mix": norm_mix,
                        "cmp_pos": cmp_pos, "cmp_w1": cmp_w1, "cmp_w2": cmp_w2, "pq": pqa,
                        "ptab": np.ascontiguousarray(page_table[4 * c:4 * c + 4].reshape(1, 256)),
                        "state_win": np.ascontiguousarray(state_win[0, 4 * c:4 * c + 4].reshape(4, 512, 512)),
                        **pools, **shared})
    res = run_bass_kernel_spmd(nc, in_maps, core_ids=list(range(8)))
    R = res.results
    _NC_CACHE["R"] = R

    def bcat(name, shape_tail):
        return np.stack([R[0][name], R[4][name]], 0).reshape((1, 2) + shape_tail)

    def scat(name, shape_tail):
        return np.concatenate([R[c][name] for c in range(8)], 0).reshape((1, 32, 1) + shape_tail)

    y_prompt = np.zeros((2, SEQ, D), np.float32)
    y_sample = np.zeros((32, 1, D), np.float32)
    for c in range(8):
        b, i = c // 4, c % 4
        yo = R[c]["y_own"]
        for j in range(8):
            t = 4 * j + i
            y_prompt[b, t * 128:(t + 1) * 128] = yo[j * 128:(j + 1) * 128]
        y_sample[4 * c:4 * c + 4, 0] = yo[1024:1028]
    cmp_p = bcat("cmp_p", (SEQ, 2, 2, 128))
    sel_p = bcat("sel_p", (SEQ, 2, 2, 128))
    dkv_p = bcat("dkv_p", (SEQ, 4, 2, 128))
    didx_p = bcat("didx_p", (SEQ, 64))
    win_p = bcat("win_p", (512, 2, 2, 128))
    cmp_s = scat("cmp_s", (2, 2, 128))
    sel_s = scat("sel_s", (2, 2, 128))
    dkv_s = scat("dkv_s", (4, 2, 128))
    didx_s = scat("didx_s", (64,))
    win_s = np.concatenate([R[c]["win_s"] for c in range(8)], 0).reshape(1, 32, 512, 2, 2, 128)
    return (y_prompt, y_sample, cmp_p, cmp_s, sel_p, sel_s, dkv_p, dkv_s, didx_p, didx_s, win_p, win_s)
```
